# Optimizing a Trainium2 kernel written in Bass

```python
import math
import jax, jax.numpy as jnp
from jax import lax
import numpy as np

D_MODEL = 1024
BATCH = 4
SEQ = 4096
DEPTH = 4

D_MIX = D_MODEL
H_A = 4
DK_A = 32
DV_A = 64
W_A = H_A * DV_A
H_B = 6
DH_B = 64
W_B = H_B * DH_B
DIL_PATTERNS = ((128, 1), (512, 4), (2048, 16))
H_C = 4
DK_C = 48
DV_C = 96
W_C = H_C * DV_C
SPLIT_SIZES = (H_A * 2 * DK_A, H_A * 2 * DK_A, W_A, W_A,
               W_B, W_B, W_B, W_B,
               H_C * DK_C, H_C * DK_C, W_C, W_C)
IN_COLS = 3712
ROT_THETA = 500000.0
ROT_A = DK_A // 4
ROT_B = DH_B // 4
RET_THETA = 10000.0
Q_BLOCK = 128
RET_CHUNK = 128
EPS = 1e-6
NEG = -1e30

kernel_name = "hymba_diff_dilated_retention_encoder"


def rms_norm(x, g):
    xf = x.astype(jnp.float32)
    y = xf * lax.rsqrt(jnp.mean(xf * xf, axis=-1, keepdims=True) + EPS)
    return y.astype(x.dtype) * g.astype(x.dtype)


def rope(x, theta, rot_dim):
    S = x.shape[1]
    half = rot_dim // 2
    pos = jnp.arange(S, dtype=jnp.float32)
    inv = theta ** (-jnp.arange(0, rot_dim, 2, dtype=jnp.float32) / rot_dim)
    ang = pos[:, None] * inv[None, :]
    shape = (S,) + (1,) * (x.ndim - 3) + (half,)
    cos = jnp.cos(ang).reshape(shape).astype(x.dtype)
    sin = jnp.sin(ang).reshape(shape).astype(x.dtype)
    x1 = x[..., :half]
    x2 = x[..., half:rot_dim]
    return jnp.concatenate([x1 * cos - x2 * sin, x1 * sin + x2 * cos, x[..., rot_dim:]], axis=-1)


def diff_attention(q, k, v, lam, lam_init, subln_g):
    B, S = q.shape[:2]
    nb = S // Q_BLOCK
    scale = DK_A ** -0.5
    qb = q.reshape(B, nb, Q_BLOCK, H_A, 2, DK_A).transpose(1, 0, 2, 3, 4, 5)

    def block(qi):
        s = jnp.einsum('bqhmd,bkhmd->bhmqk', qi, k).astype(jnp.float32) * scale
        p = jax.nn.softmax(s, axis=-1)
        pd = p[:, :, 0] - lam * p[:, :, 1]
        return jnp.einsum('bhqk,bkhd->bqhd', pd.astype(v.dtype), v)

    o = lax.map(block, qb)
    o = o.transpose(1, 0, 2, 3, 4).reshape(B, S, H_A, DV_A)
    return rms_norm(o, subln_g) * (1.0 - lam_init)


def dilated_branch(qb, starts, k, v, offsets):
    S = k.shape[1]
    scale = DH_B ** -0.5

    def block(args):
        qi, start = args
        idx = start + jnp.arange(Q_BLOCK)[:, None] + offsets[None, :]
        valid = (idx >= 0) & (idx < S)
        idx_c = jnp.clip(idx, 0, S - 1)
        kg = k[:, idx_c]
        vg = v[:, idx_c]
        s = jnp.einsum('bqhd,bqjhd->bhqj', qi, kg).astype(jnp.float32) * scale
        s = jnp.where(valid[None, None], s, NEG)
        lse = jax.nn.logsumexp(s, axis=-1)
        p = jnp.exp(s - lse[..., None])
        o = jnp.einsum('bhqj,bqjhd->bqhd', p.astype(v.dtype), vg)
        return o, lse

    return lax.map(block, (qb, starts))


def dilated_attention(q, k, v):
    B, S = q.shape[:2]
    nb = S // Q_BLOCK
    qb = q.reshape(B, nb, Q_BLOCK, H_B, DH_B).transpose(1, 0, 2, 3, 4)
    starts = jnp.arange(nb, dtype=jnp.int32) * Q_BLOCK
    outs, lses = [], []
    for window, dil in DIL_PATTERNS:
        n_side = (window // 2) // dil
        offsets = dil * jnp.arange(-n_side, n_side + 1, dtype=jnp.int32)
        o, lse = dilated_branch(qb, starts, k, v, offsets)
        outs.append(o)
        lses.append(lse)
    o = jnp.stack(outs)
    w = jax.nn.softmax(jnp.stack(lses), axis=0)
    w = w.transpose(0, 1, 2, 4, 3)[..., None].astype(o.dtype)
    o = jnp.sum(w * o, axis=0)
    return o.transpose(1, 0, 2, 3, 4).reshape(B, S, H_B, DH_B)


def retention_scan(q, k, v, log_g, include_diag):
    B, H, S, dk = q.shape
    dv = v.shape[-1]
    C = RET_CHUNK
    N = S // C
    i = jnp.arange(C, dtype=jnp.float32)
    diff = i[:, None] - i[None, :]
    mask = (diff >= 0) if include_diag else (diff > 0)
    lg = log_g[:, None, None]
    d_intra = jnp.where(mask[None], jnp.exp(lg * diff[None]), 0.0)
    q_dec = jnp.exp(log_g[:, None] * (i[None, :] + 1.0))[..., None]
    k_dec = jnp.exp(log_g[:, None] * (C - 1.0 - i[None, :]))[..., None]
    chunk_dec = jnp.exp(log_g * C)[:, None, None]

    def to_chunks(t):
        return t.reshape(B, H, N, C, t.shape[-1]).transpose(2, 0, 1, 3, 4)

    def step(R, xs):
        qc, kc, vc = xs
        att = jnp.einsum('bhid,bhjd->bhij', qc, kc) * d_intra
        inner = jnp.einsum('bhij,bhje->bhie', att, vc)
        cross = jnp.einsum('bhid,bhde->bhie', qc, R) * q_dec
        R = R * chunk_dec + jnp.einsum('bhjd,bhje->bhde', kc * k_dec, vc)
        return R, inner + cross

    R0 = jnp.zeros((B, H, dk, dv), jnp.float32)
    _, o = lax.scan(step, R0, (to_chunks(q), to_chunks(k), to_chunks(v)))
    return o.transpose(1, 2, 0, 3, 4).reshape(B, H, S, dv)


def retention_bidir(q, k, v, decay_logit, gn):
    dt = v.dtype
    qf = q.astype(jnp.float32).transpose(0, 2, 1, 3)
    kf = k.astype(jnp.float32).transpose(0, 2, 1, 3) * (DK_C ** -0.5)
    vf = v.astype(jnp.float32).transpose(0, 2, 1, 3)
    log_g = jax.nn.log_sigmoid(decay_logit.astype(jnp.float32))
    fwd = retention_scan(qf, kf, vf, log_g[0], True)
    bwd = jnp.flip(retention_scan(jnp.flip(qf, 2), jnp.flip(kf, 2), jnp.flip(vf, 2), log_g[1], False), 2)
    o = (fwd + bwd).transpose(0, 2, 1, 3)
    return rms_norm(o, gn).astype(dt)


def setup_inputs(seed: int = 0) -> dict:
    key = jax.random.key(seed)
    ks = jax.random.split(key, 20)
    L, D = DEPTH, D_MODEL
    f32 = jnp.float32
    gamma = 1.0 - 2.0 ** (-jnp.linspace(5.0, 12.0, H_C))
    base_logit = jnp.log(gamma) - jnp.log1p(-gamma)
    return {
        "x": jax.random.normal(ks[0], (BATCH, SEQ, D), f32),
        "c": jax.random.normal(ks[1], (BATCH, D), f32),
        "norm_g": 1.0 + 0.02 * jax.random.normal(ks[2], (L, D), f32),
        "w_ada": 0.5 * D ** -0.5 * jax.random.normal(ks[3], (L, D, 3 * D), f32),
        "b_ada": 0.02 * jax.random.normal(ks[4], (L, 3 * D), f32),
        "w_in": D ** -0.5 * jax.random.normal(ks[5], (L, D, IN_COLS), f32),
        "w_out": D_MIX ** -0.5 * jax.random.normal(ks[6], (L, D_MIX, D), f32),
        "qn_a": 1.0 + 0.02 * jax.random.normal(ks[7], (L, DK_A), f32),
        "kn_a": 1.0 + 0.02 * jax.random.normal(ks[8], (L, DK_A), f32),
        "lambda_q1": 0.1 * jax.random.normal(ks[9], (L, DK_A), f32),
        "lambda_k1": 0.1 * jax.random.normal(ks[10], (L, DK_A), f32),
        "lambda_q2": 0.1 * jax.random.normal(ks[11], (L, DK_A), f32),
        "lambda_k2": 0.1 * jax.random.normal(ks[12], (L, DK_A), f32),
        "subln_a": 1.0 + 0.02 * jax.random.normal(ks[13], (L, DV_A), f32),
        "qn_b": 1.0 + 0.02 * jax.random.normal(ks[14], (L, DH_B), f32),
        "kn_b": 1.0 + 0.02 * jax.random.normal(ks[15], (L, DH_B), f32),
        "ret_decay": base_logit[None, None, :] + 0.1 * jax.random.normal(ks[16], (L, 2, H_C), f32),
        "gn_c": 1.0 + 0.02 * jax.random.normal(ks[17], (L, DV_C), f32),
    }


def reference(x, c, norm_g, w_ada, b_ada, w_in, w_out, qn_a, kn_a, lambda_q1, lambda_k1,
              lambda_q2, lambda_k2, subln_a, qn_b, kn_b, ret_decay, gn_c):
    B, S, _ = x.shape
    split_idx = [int(v) for v in np.cumsum(SPLIT_SIZES)[:-1]]
    cs = jax.nn.silu(c)
    for l in range(DEPTH):
        mod = cs @ w_ada[l] + b_ada[l]
        shift, scale, gate = jnp.split(mod, 3, axis=-1)
        h = rms_norm(x, norm_g[l]) * (1.0 + scale[:, None]) + shift[:, None]
        proj = h @ w_in[l]
        qa, ka, va, ga, qb, kb, vb, gb, qc, kc, vc, gc = jnp.split(proj, split_idx, axis=-1)

        qa = rope(rms_norm(qa.reshape(B, S, H_A, 2, DK_A), qn_a[l]), ROT_THETA, ROT_A)
        ka = rope(rms_norm(ka.reshape(B, S, H_A, 2, DK_A), kn_a[l]), ROT_THETA, ROT_A)
        lam_init = 0.8 - 0.6 * math.exp(-0.3 * l)
        lam = (jnp.exp(jnp.sum(lambda_q1[l] * lambda_k1[l]).astype(jnp.float32))
               - jnp.exp(jnp.sum(lambda_q2[l] * lambda_k2[l]).astype(jnp.float32)) + lam_init)
        oa = diff_attention(qa, ka, va.reshape(B, S, H_A, DV_A), lam, lam_init, subln_a[l])
        oa = oa.reshape(B, S, W_A) * jax.nn.silu(ga)

        qb = rope(rms_norm(qb.reshape(B, S, H_B, DH_B), qn_b[l]), ROT_THETA, ROT_B)
        kb = rope(rms_norm(kb.reshape(B, S, H_B, DH_B), kn_b[l]), ROT_THETA, ROT_B)
        ob = dilated_attention(qb, kb, vb.reshape(B, S, H_B, DH_B))
        ob = ob.reshape(B, S, W_B) * jax.nn.silu(gb)

        qc = rope(qc.reshape(B, S, H_C, DK_C), RET_THETA, DK_C)
        kc = rope(kc.reshape(B, S, H_C, DK_C), RET_THETA, DK_C)
        oc = retention_bidir(qc, kc, vc.reshape(B, S, H_C, DV_C), ret_decay[l], gn_c[l])
        oc = oc.reshape(B, S, W_C) * jax.nn.silu(gc)

        y = jnp.concatenate([oa, ob, oc], axis=-1) @ w_out[l]
        x = x + gate[:, None] * y
    return x
```

```python
import math
from contextlib import ExitStack
import numpy as np
import ml_dtypes
import concourse.bass as bass
import concourse.mybir as mybir
from concourse.bass_utils import run_bass_kernel_spmd

F32 = mybir.dt.float32
BF16 = mybir.dt.bfloat16
ALU = mybir.AluOpType
AF = mybir.ActivationFunctionType
AX = mybir.AxisListType

D_MODEL = 1024
IN_COLS = 3712
EPS = 1e-6
ROT_THETA = 500000.0
RET_THETA = 10000.0
NSMALL = 488
SP_QNA, SP_KNA, SP_LQ1, SP_LK1, SP_LQ2, SP_LK2 = 0, 32, 64, 96, 128, 160
SP_SUB, SP_QNB, SP_KNB, SP_DEC, SP_GNC = 192, 256, 320, 384, 392
OFF_QA, OFF_KA, OFF_VA, OFF_GA = 0, 256, 512, 768
OFF_QB, OFF_KB, OFF_VB, OFF_GB = 1024, 1408, 1792, 2176
OFF_QC, OFF_KC, OFF_VC, OFF_GC = 2560, 2752, 2944, 3328
OCOL_A, OCOL_B, OCOL_C = 0, 256, 640


class Buf:
    __slots__ = ("name", "w", "r", "excl")

    def __init__(self, name, excl=False):
        self.name = name
        self.w = None
        self.r = {}
        self.excl = excl


class Prog:
    ENG = ("pe", "act", "dve", "pool", "sp")

    def __init__(self, nc, stack):
        self.nc = nc
        self.stack = stack
        self.q = {e: [] for e in self.ENG}
        self.cnt = {e: 0 for e in self.ENG}
        self.seen = {e: {} for e in self.ENG}
        self.sems = {}
        self.dcnt = {}
        for e in self.ENG:
            self.sems[e] = stack.enter_context(nc.semaphore("s_" + e))
        self.ninst = 0
        self.desc = {}
        self.total = 0

    def dsem(self, name):
        if name not in self.sems:
            self.sems[name] = self.stack.enter_context(self.nc.semaphore("d_" + name))
            self.dcnt[name] = 0
        return name

    def _wait(self, eng, k, v):
        if k == eng and eng == "pe":
            return
        if self.seen[eng].get(k, 0) < v:
            self.seen[eng][k] = v
            sem = self.sems[k]
            self.ninst += 1
            self.desc.setdefault(eng, []).append("wait %s>=%d" % (k, v))
            self.q[eng].append(lambda e, sem=sem, v=v: e.wait_ge(sem, v))

    def _deps(self, eng, reads, writes):
        need = {}
        for b in reads:
            if b.w is not None:
                k, v = b.w
                if need.get(k, 0) < v:
                    need[k] = v
            if b.excl:
                for k, v in b.r.items():
                    if k != eng and need.get(k, 0) < v:
                        need[k] = v
        for b in writes:
            if b.w is not None:
                k, v = b.w
                if need.get(k, 0) < v:
                    need[k] = v
            for k, v in b.r.items():
                if need.get(k, 0) < v:
                    need[k] = v
        for k, v in need.items():
            self._wait(eng, k, v)

    def op(self, eng, fn, reads=(), writes=()):
        self.total = getattr(self, "total", 0) + 1
        if self.total > DBG.get("cut", 10 ** 9):
            return
        self._deps(eng, reads, writes)
        if DBG.get("serial") and getattr(self, "last", None):
            self._wait(eng, *self.last)
        self.cnt[eng] += 1
        c = self.cnt[eng]
        self.last = (eng, c)
        sem = self.sems[eng]
        self.ninst += 1
        self.desc.setdefault(eng, []).append("op#%d (%s=%d)" % (self.total, eng, c))
        self.q[eng].append(lambda e, fn=fn, sem=sem: fn(e).then_inc(sem, 1))
        for b in writes:
            b.w = (eng, c)
            b.r = {}
        for b in reads:
            if b.w != (eng, c):
                b.r[eng] = c

    def dma(self, semname, out, in_, reads=(), writes=(), qeng="sp", **kw):
        self.total = getattr(self, "total", 0) + 1
        if self.total > DBG.get("cut", 10 ** 9):
            return
        self.dsem(semname)
        self._deps(qeng, reads, writes)
        if DBG.get("serial") and getattr(self, "last", None):
            self._wait(qeng, *self.last)
        self.dcnt[semname] += 16
        v = self.dcnt[semname]
        self.last = (semname, v)
        sem = self.sems[semname]
        self.ninst += 1
        self.desc.setdefault(qeng, []).append("dma#%d (%s=%d)" % (self.total, semname, v))
        self.q[qeng].append(
            lambda e, out=out, in_=in_, sem=sem, kw=kw: e.dma_start(out=out, in_=in_, **kw).then_inc(sem, 16))
        for b in writes:
            b.w = (semname, v)
            b.r = {}
        for b in reads:
            b.r[semname] = v

    def barrier(self):
        for e in self.ENG:
            for k in list(self.sems.keys()):
                v = self.cnt[k] if k in self.cnt else self.dcnt[k]
                if v > 0 and k != e:
                    self._wait(e, k, v)
            if e != "pe" and self.cnt[e] > 0:
                self._wait(e, e, self.cnt[e])

    def emit(self):
        nc = self.nc
        with nc.Block() as block:
            @block.tensor
            def _(e):
                for t in self.q["pe"]:
                    t(e)

            @block.scalar
            def _(e):
                for t in self.q["act"]:
                    t(e)

            @block.vector
            def _(e):
                for t in self.q["dve"]:
                    t(e)

            @block.gpsimd
            def _(e):
                for t in self.q["pool"]:
                    t(e)

            @block.sync
            def _(e):
                for t in self.q["sp"]:
                    t(e)


DBG = {}


def build_program(S, L, stages=("A", "B", "C", "O"), dbg=None):
    NT = S // 128
    nc = bass.Bass("TRN2", target_bir_lowering=False)
    dr = lambda name, shape, dt, kind="ExternalInput": nc.dram_tensor(name, shape, dt, kind=kind).ap()
    x_in = dr("x", [S, D_MODEL], F32)
    cT_in = dr("cT", [128, 8], F32)
    w_ada = dr("w_ada", [L, D_MODEL, 3 * D_MODEL], F32)
    b_ada = dr("b_ada", [L, 3 * D_MODEL], F32)
    norm_g = dr("norm_g", [L, D_MODEL], F32)
    w_in = dr("w_in", [L, D_MODEL, IN_COLS], F32)
    w_out = dr("w_out", [L, D_MODEL, D_MODEL], F32)
    smallp = dr("smallp", [L, NSMALL], F32)
    ident_b_d = dr("ident_b", [128, 128], BF16)
    ident_f_d = dr("ident_f", [128, 128], F32)
    rope_d = dr("rope", [128, NT, 72], F32)
    maskB_d = dr("maskB", [128, 384], BF16)
    retc_d = dr("retc", [128, 516], F32)
    kmask_d = dr("kmask", [128, 2], F32)
    y_out = dr("y", [S, D_MODEL], F32, kind="ExternalOutput")
    o_scr = dr("o_scr", [S, D_MODEL], BF16, kind="ExternalOutput")
    dbg_hT = dr("dbg_hT", [128, 8 * S], F32, kind="ExternalOutput") if dbg == "hT" else None
    dbg_o = dr("dbg_o", [S, D_MODEL], F32, kind="ExternalOutput") if dbg == "o" else None

    with ExitStack() as st:
        P = Prog(nc, st)

        uid = [0]

        def sbuf(stack, name, shape, dt):
            uid[0] += 1
            t = stack.enter_context(nc.sbuf_tensor("sb%d_%s" % (uid[0], name), shape, dt))
            return t, Buf(name)

        def op(eng, fn, reads=(), writes=()):
            if eng == "pool" and DBG.get("nopool"):
                eng = "dve"
            P.op(eng, fn, reads, writes)

        def cp(eng, out, in_, reads, writes):
            if eng == "act":
                op("act", lambda e: e.copy(out=out, in_=in_), reads, writes)
            else:
                op(eng, lambda e: e.tensor_copy(out=out, in_=in_), reads, writes)

        def tt(eng, out, in0, in1, alu, reads, writes):
            op(eng, lambda e: e.tensor_tensor(out=out, in0=in0, in1=in1, op=alu), reads, writes)

        def ts(eng, out, in0, s1, alu, reads, writes):
            op(eng, lambda e: e.tensor_scalar(out=out, in0=in0, scalar1=s1, scalar2=None, op0=alu), reads, writes)

        hT, b_hT = sbuf(st, "hT", [128, 8, S], BF16)
        rope, b_rope = sbuf(st, "rope", [128, NT, 72], F32)
        ident_b, b_idb = sbuf(st, "ident_b", [128, 128], BF16)
        ident_f, b_idf = sbuf(st, "ident_f", [128, 128], F32)
        maskB, b_maskB = sbuf(st, "maskB", [128, 384], BF16)
        retc, b_retc = sbuf(st, "retc", [128, 516], F32)
        kmask, b_kmask = sbuf(st, "kmask", [128, 2], F32)
        cs, b_cs = sbuf(st, "cs", [128, 8], F32)
        ones_row, b_ones = sbuf(st, "ones_row", [1, 128], F32)
        gate_bc, b_gate = sbuf(st, "gate_bc", [128, 1024], F32)
        spt, b_spt = sbuf(st, "spt", [128, NSMALL], F32)
        gA, b_gA = sbuf(st, "gA", [128, 8, 32], F32)
        gB, b_gB = sbuf(st, "gB", [128, 4, 64], F32)
        gS, b_gS = sbuf(st, "gS", [128, 2, 64], F32)
        lamt, b_lam = sbuf(st, "lamt", [128, 8], F32)
        decs, b_decs = sbuf(st, "decs", [128, 64], F32)
        DT, b_DT = sbuf(st, "DT", [128, 4, 128], F32)
        eps_t, b_eps = sbuf(st, "eps_t", [128, 1], F32)
        wstage = [sbuf(st, f"wstage{i}", [128, 8, 256], F32) for i in range(2)]
        state = {"ws": 0, "mm": 0, "st": 0, "ot": 0, "tb": 0, "scr": 0}

        psf = [(st.enter_context(nc.psum_tensor(f"psf{i}", [128, 512], F32)), Buf(f"psf{i}", True)) for i in range(6)]
        psb = [(st.enter_context(nc.psum_tensor(f"psb{i}", [128, 1024], BF16)), Buf(f"psb{i}", True)) for i in range(2)]

        def ps_mm():
            state["mm"] ^= 1
            return psf[state["mm"]]

        def ps_st():
            state["st"] ^= 1
            return psf[2 + state["st"]]

        def ps_ot():
            state["ot"] ^= 1
            return psf[4 + state["ot"]]

        def ps_tb():
            state["tb"] ^= 1
            return psb[state["tb"]]

        P.dma("c_rope", rope[:], rope_d[:, :, :], writes=[b_rope])
        P.dma("c_idb", ident_b[:], ident_b_d[:, :], writes=[b_idb])
        P.dma("c_idf", ident_f[:], ident_f_d[:, :], writes=[b_idf])
        P.dma("c_mb", maskB[:], maskB_d[:, :], writes=[b_maskB])
        P.dma("c_rc", retc[:], retc_d[:, :], writes=[b_retc])
        P.dma("c_km", kmask[:], kmask_d[:, :], writes=[b_kmask])
        P.dma("c_cs", cs[:], cT_in[:, :], writes=[b_cs])
        op("act", lambda e: e.activation(out=cs[:], in_=cs[:], func=AF.Silu), [b_cs], [b_cs])
        op("pool", lambda e: e.memset(ones_row[:], 1.0), [], [b_ones])
        op("pool", lambda e: e.memset(eps_t[:], EPS), [], [b_eps])

        def load_w(dst, b_dst, src_ap, ncols, col0=0):
            done = 0
            while done < ncols:
                n = min(256, ncols - done)
                i = state["ws"]
                state["ws"] ^= 1
                stg, b_stg = wstage[i]
                P.dma("ws%d" % i, stg[:, :, 0:n], src_ap[:, done:done + n].rearrange("(k p) n -> p k n", p=128), writes=[b_stg])
                cp("pool", dst[:, :, col0 + done:col0 + done + n], stg[:, :, 0:n], [b_stg], [b_dst])
                done += n

        def proj(t_cols, W, b_W, n, ps, b_ps, wcol0=0, pcol0=0):
            for k in range(8):
                op("pe", lambda e, k=k: e.matmul(ps[:, pcol0:pcol0 + n], lhsT=hT[:, k, t_cols], rhs=W[:, k, wcol0:wcol0 + n],
                                                start=(k == 0), stop=(k == 7)), [b_hT, b_W], [b_ps])

        def rstd_inplace(ss, b_ss, d):
            op("act", lambda e: e.activation(out=ss, in_=ss, func=AF.Ln, scale=1.0 / d, bias=eps_t[:, 0:1]), [b_ss, b_eps], [b_ss])
            op("act", lambda e: e.activation(out=ss, in_=ss, func=AF.Exp, scale=-0.5), [b_ss], [b_ss])

        def tcols(t):
            return slice(t * 128, (t + 1) * 128)

        def norm_rope(src3, b_src, G, d, gains, b_g, half, cos, sin, out3, b_out, scr):
            rot = 2 * half
            sq, b_sq = scr["sq"]
            xn, b_xn = scr["xn"]
            ssx, b_ssx = scr["ss"]
            xn3 = xn[:, 0:G * d].rearrange("p (g d) -> p g d", d=d)
            if gains is not None:
                sq3 = sq[:, 0:G * d].rearrange("p (g d) -> p g d", d=d)
                op("act", lambda e: e.activation(out=sq3, in_=src3, func=AF.Square), [b_src], [b_sq])
                op("dve", lambda e: e.reduce_sum(out=ssx[:, 0:G], in_=sq3, axis=AX.X), [b_sq], [b_ssx])
                rstd_inplace(ssx[:, 0:G], b_ssx, float(d))
                tt("dve", xn3, src3, ssx[:, 0:G].unsqueeze(2).to_broadcast([128, G, d]), ALU.mult, [b_src, b_ssx], [b_xn])
                tt("pool", xn3, xn3, gains, ALU.mult, [b_xn, b_g], [b_xn])
            else:
                cp("act", xn3, src3, [b_src], [b_xn])
            x1 = xn3[:, :, 0:half]
            x2 = xn3[:, :, half:rot]
            cb = cos.unsqueeze(1).to_broadcast([128, G, half])
            sb_ = sin.unsqueeze(1).to_broadcast([128, G, half])
            tv = []
            for i in range(4):
                tq, b_tq = scr["t%d" % i]
                tv.append((tq[:, 0:G * half].rearrange("p (g d) -> p g d", d=half), b_tq))
            tt("dve", tv[0][0], x1, cb, ALU.mult, [b_xn, b_rope], [tv[0][1]])
            tt("pool", tv[1][0], x2, sb_, ALU.mult, [b_xn, b_rope], [tv[1][1]])
            tt("dve", out3[:, :, 0:half], tv[0][0], tv[1][0], ALU.subtract, [tv[0][1], tv[1][1]], [b_out])
            tt("pool", tv[2][0], x1, sb_, ALU.mult, [b_xn, b_rope], [tv[2][1]])
            tt("dve", tv[3][0], x2, cb, ALU.mult, [b_xn, b_rope], [tv[3][1]])
            tt("pool", out3[:, :, half:rot], tv[2][0], tv[3][0], ALU.add, [tv[2][1], tv[3][1]], [b_out])
            if rot < d:
                cp("act", out3[:, :, rot:d], xn3[:, :, rot:d], [b_xn], [b_out])

        def make_scr(ph, tag, n):
            out = []
            for i in range(n):
                s = {}
                s["sq"] = sbuf(ph, f"{tag}sq{i}", [128, 256], F32)
                s["xn"] = sbuf(ph, f"{tag}xn{i}", [128, 256], F32)
                s["ss"] = sbuf(ph, f"{tag}ss{i}", [128, 8], F32)
                for j in range(4):
                    s["t%d" % j] = sbuf(ph, f"{tag}t{j}_{i}", [128, 96], F32)
                out.append(s)
            return out

        def gate_and_store(t, on2d, b_on, Wg, b_Wg, n, ocol, gsc, b_gsc, ofin, b_ofin):
            psg, b_psg = ps_mm()
            proj(tcols(t), Wg, b_Wg, n, psg, b_psg)
            op("act", lambda e: e.activation(out=gsc[:, 0:n], in_=psg[:, 0:n], func=AF.Silu), [b_psg], [b_gsc])
            tt("dve", ofin[:, 0:n], on2d, gsc[:, 0:n], ALU.mult, [b_on, b_gsc], [b_ofin])
            P.dma("ofin_" + b_ofin.name, o_scr[t * 128:(t + 1) * 128, ocol:ocol + n], ofin[:, 0:n], reads=[b_ofin], writes=[obufs[t]])

        xbufs = [Buf(f"x{t}") for t in range(NT)]
        obufs = [Buf(f"o{t}") for t in range(NT)]

        for l in range(L):
            lam_init = 0.8 - 0.6 * math.exp(-0.3 * l)
            xsrc = x_in if l == 0 else y_out
            P.dma("spt", spt[:], smallp[l:l + 1, :].partition_broadcast(128), writes=[b_spt])
            cp("dve", gA[:, 0:4, :], spt[:, SP_QNA:SP_QNA + 32].unsqueeze(1).to_broadcast([128, 4, 32]), [b_spt], [b_gA])
            cp("dve", gA[:, 4:8, :], spt[:, SP_KNA:SP_KNA + 32].unsqueeze(1).to_broadcast([128, 4, 32]), [b_spt], [b_gA])
            cp("dve", gB[:, 0:2, :], spt[:, SP_QNB:SP_QNB + 64].unsqueeze(1).to_broadcast([128, 2, 64]), [b_spt], [b_gB])
            cp("dve", gB[:, 2:4, :], spt[:, SP_KNB:SP_KNB + 64].unsqueeze(1).to_broadcast([128, 2, 64]), [b_spt], [b_gB])
            ts("dve", gS[:], spt[:, SP_SUB:SP_SUB + 64].unsqueeze(1).to_broadcast([128, 2, 64]), 1.0 - lam_init, ALU.mult, [b_spt], [b_gS])
            tt("dve", decs[:, 0:32], spt[:, SP_LQ1:SP_LQ1 + 32], spt[:, SP_LK1:SP_LK1 + 32], ALU.mult, [b_spt], [b_decs])
            op("dve", lambda e: e.reduce_sum(out=lamt[:, 0:1], in_=decs[:, 0:32], axis=AX.X), [b_decs], [b_lam])
            tt("dve", decs[:, 32:64], spt[:, SP_LQ2:SP_LQ2 + 32], spt[:, SP_LK2:SP_LK2 + 32], ALU.mult, [b_spt], [b_decs])
            op("dve", lambda e: e.reduce_sum(out=lamt[:, 1:2], in_=decs[:, 32:64], axis=AX.X), [b_decs], [b_lam])
            op("act", lambda e: e.activation(out=lamt[:, 2:4], in_=lamt[:, 0:2], func=AF.Exp), [b_lam], [b_lam])
            tt("dve", lamt[:, 4:5], lamt[:, 3:4], lamt[:, 2:3], ALU.subtract, [b_lam], [b_lam])
            ts("dve", lamt[:, 6:7], lamt[:, 4:5], -lam_init, ALU.add, [b_lam], [b_lam])
            op("act", lambda e: e.activation(out=decs[:, 0:8], in_=spt[:, SP_DEC:SP_DEC + 8], func=AF.Exp, scale=-1.0), [b_spt, b_decs], [b_decs])
            ts("dve", decs[:, 0:8], decs[:, 0:8], 1.0, ALU.add, [b_decs], [b_decs])
            op("act", lambda e: e.activation(out=decs[:, 0:8], in_=decs[:, 0:8], func=AF.Ln), [b_decs], [b_decs])
            ts("dve", decs[:, 0:8], decs[:, 0:8], -1.0, ALU.mult, [b_decs], [b_decs])
            for (o0, i0, rc) in ((8, 0, 513), (12, 4, 515), (16, 0, 512), (20, 4, 514)):
                op("act", lambda e, o0=o0, i0=i0, rc=rc: e.activation(out=decs[:, o0:o0 + 4], in_=decs[:, i0:i0 + 4], func=AF.Exp, scale=retc[:, rc:rc + 1]),
                   [b_decs, b_retc], [b_decs])
            op("act", lambda e: e.activation(out=decs[:, 24:32], in_=decs[:, 0:8], func=AF.Exp, scale=128.0), [b_decs], [b_decs])
            with ExitStack() as ph:
                dtmp, b_dtmp = sbuf(ph, "dtmp", [128, 128], F32)
                for h in range(4):
                    op("act", lambda e, h=h: e.activation(out=DT[:, h, :], in_=retc[:, 0:128], func=AF.Exp, scale=decs[:, h:h + 1]), [b_decs, b_retc], [b_DT])
                    tt("dve", DT[:, h, :], DT[:, h, :], retc[:, 128:256], ALU.mult, [b_DT, b_retc], [b_DT])
                    op("act", lambda e, h=h: e.activation(out=dtmp[:], in_=retc[:, 256:384], func=AF.Exp, scale=decs[:, 4 + h:5 + h]), [b_decs, b_retc], [b_dtmp])
                    tt("dve", dtmp[:], dtmp[:], retc[:, 384:512], ALU.mult, [b_dtmp, b_retc], [b_dtmp])
                    tt("dve", DT[:, h, :], DT[:, h, :], dtmp[:], ALU.add, [b_dtmp, b_DT], [b_DT])
                P.barrier()

            ph_norm = ExitStack()
            gs_bc, b_gs = sbuf(ph_norm, "gs_bc", [128, 1024], F32)
            shift_bc, b_shift = sbuf(ph_norm, "shift_bc", [128, 1024], F32)
            ph_ada = ExitStack()
            modrow, b_modrow = sbuf(ph_ada, "modrow", [1, 3072], F32)
            bada, b_bada = sbuf(ph_ada, "bada", [1, 3072], F32)
            P.dma("bada", bada[:], b_ada[l:l + 1, :], writes=[b_bada])
            for nt in range(12):
                i = state["ws"]
                state["ws"] ^= 1
                stg, b_stg = wstage[i]
                P.dma("ws%d" % i, stg[:, :, :], w_ada[l, :, nt * 256:(nt + 1) * 256].rearrange("(k p) n -> p k n", p=128), writes=[b_stg])
                ps, b_ps = ps_mm()
                for k in range(8):
                    op("pe", lambda e, k=k, stg=stg, ps=ps: e.matmul(ps[0:1, 0:256], lhsT=cs[:, k:k + 1], rhs=stg[:, k, :], start=(k == 0), stop=(k == 7)),
                       [b_cs, b_stg], [b_ps])
                tt("dve", modrow[0:1, nt * 256:(nt + 1) * 256], ps[0:1, 0:256], bada[0:1, nt * 256:(nt + 1) * 256], ALU.add, [b_ps, b_bada], [b_modrow])
            with ExitStack() as ph:
                g_bc, b_gbc = sbuf(ph, "g_bc", [128, 1024], F32)
                P.dma("gbc", g_bc[:], norm_g[l:l + 1, :].partition_broadcast(128), writes=[b_gbc])
                for j in range(6):
                    ps, b_ps = ps_mm()
                    op("pe", lambda e, ps=ps, j=j: e.matmul(ps[:, :], lhsT=ones_row[0:1, :], rhs=modrow[0:1, j * 512:(j + 1) * 512], start=True, stop=True),
                       [b_ones, b_modrow], [b_ps])
                    sl = slice((j % 2) * 512, (j % 2 + 1) * 512)
                    if j < 2:
                        cp("act", shift_bc[:, sl], ps[:, :], [b_ps], [b_shift])
                    elif j < 4:
                        op("dve", lambda e, ps=ps, sl=sl: e.scalar_tensor_tensor(out=gs_bc[:, sl], in0=ps[:, :], scalar=1.0, in1=g_bc[:, sl], op0=ALU.add, op1=ALU.mult),
                           [b_ps, b_gbc], [b_gs])
                    else:
                        cp("act", gate_bc[:, sl], ps[:, :], [b_ps], [b_gate])
                P.barrier()
            ph_ada.close()

            with ExitStack() as ph:
                xt = [sbuf(ph, f"xt{i}", [128, 1024], F32) for i in range(2)]
                junk, b_junk = sbuf(ph, "junk", [128, 1024], BF16)
                h1 = [sbuf(ph, f"h1_{i}", [128, 1024], F32) for i in range(2)]
                hb = [sbuf(ph, f"hb{i}", [128, 1024], BF16) for i in range(2)]
                ssn = [sbuf(ph, f"ssn{i}", [128, 1], F32) for i in range(2)]
                for t in range(NT):
                    x_t, b_x = xt[t % 2]
                    h_t, b_h = h1[t % 2]
                    hb_t, b_hb = hb[t % 2]
                    ss, b_ss = ssn[t % 2]
                    P.dma("xt%d" % (t % 2), x_t[:], xsrc[t * 128:(t + 1) * 128, :], reads=[xbufs[t]], writes=[b_x])
                    op("act", lambda e, x_t=x_t, ss=ss: e.activation(out=junk[:], in_=x_t[:], func=AF.Square, accum_out=ss[:]), [b_x], [b_junk, b_ss])
                    rstd_inplace(ss[:], b_ss, 1024.0)
                    op("dve", lambda e, x_t=x_t, ss=ss, h_t=h_t: e.scalar_tensor_tensor(out=h_t[:], in0=x_t[:], scalar=ss[:, 0:1], in1=gs_bc[:], op0=ALU.mult, op1=ALU.mult),
                       [b_x, b_ss, b_gs], [b_h])
                    tt("pool", hb_t[:], h_t[:], shift_bc[:], ALU.add, [b_h, b_shift], [b_hb])
                    pb, b_pb = ps_tb()
                    for k in list(range(8)) * (2 if DBG.get("dup") else 1):
                        op("pe", lambda e, k=k, pb=pb, hb_t=hb_t: e.transpose(pb[:, k * 128:(k + 1) * 128], hb_t[:, k * 128:(k + 1) * 128], ident_b[:]),
                           [b_hb, b_idb], [b_pb])
                    cp("act" if t % 2 else "dve", hT[:, :, tcols(t)], pb[:, :].rearrange("p (k n) -> p k n", n=128), [b_pb], [b_hT])
                P.barrier()
            ph_norm.close()
            if dbg == "hT":
                with ExitStack() as ph:
                    d32, b_d32 = sbuf(ph, "d32", [128, 8 * S], F32)
                    cp("dve", d32[:], hT[:].rearrange("p k s -> p (k s)"), [b_hT], [b_d32])
                    P.dma("dbg", dbg_hT[:, :], d32[:], reads=[b_d32], writes=[xbufs[0]])
                break

            for c in range(2 if "A" in stages else 0):
                with ExitStack() as ph:
                    Wqk, b_Wqk = sbuf(ph, "A_Wqk", [128, 8, 256], BF16)
                    Wv, b_Wv = sbuf(ph, "A_Wv", [128, 8, 128], BF16)
                    Wg, b_Wg = sbuf(ph, "A_Wg", [128, 8, 128], BF16)
                    wl = w_in[l]
                    load_w(Wqk, b_Wqk, wl[:, OFF_QA + 128 * c:OFF_QA + 128 * c + 128], 128, 0)
                    load_w(Wqk, b_Wqk, wl[:, OFF_KA + 128 * c:OFF_KA + 128 * c + 128], 128, 128)
                    load_w(Wv, b_Wv, wl[:, OFF_VA + 128 * c:OFF_VA + 128 * c + 128], 128)
                    load_w(Wg, b_Wg, wl[:, OFF_GA + 128 * c:OFF_GA + 128 * c + 128], 128)
                    qT, b_qT = sbuf(ph, "A_qT", [128, S], BF16)
                    k1p, b_k1p = sbuf(ph, "A_k1p", [128, S], BF16)
                    k2p, b_k2p = sbuf(ph, "A_k2p", [128, S], BF16)
                    vaug, b_vaug = sbuf(ph, "A_vaug", [128, NT, 2, 66], BF16)
                    if not DBG.get("A_nomemset"):
                        op("pool", lambda e: e.memset(vaug[:, :, :, 64:65], 1.0), [], [b_vaug])
                    ph2 = ExitStack()
                    scr = make_scr(ph2, "A", 2)
                    qkb = [sbuf(ph2, f"A_qkb{i}", [128, 256], BF16) for i in range(2)]
                    for t in range(NT):
                        ps, b_ps = ps_mm()
                        proj(tcols(t), Wqk, b_Wqk, 256, ps, b_ps)
                        qk_t, b_qk = qkb[t % 2]
                        if DBG.get("A_nonr"):
                            cp("act", qk_t[:], ps[:, 0:256], [b_ps], [b_qk])
                        else:
                            norm_rope(ps[:, 0:256].rearrange("p (g d) -> p g d", d=32), b_ps, 8, 32, gA[:], b_gA, 4,
                                      rope[:, t, 0:4], rope[:, t, 4:8], qk_t[:].rearrange("p (g d) -> p g d", d=32), b_qk, scr[t % 2])
                        pb, b_pb = ps_tb()
                        for j in ((1, 0) if DBG.get("A_swap") else (0, 1)):
                            op("pe", lambda e, j=j, pb=pb, qk_t=qk_t: e.transpose(pb[:, j * 128:(j + 1) * 128], qk_t[:, j * 128:(j + 1) * 128], ident_b[:]),
                               [b_qk, b_idb], [b_pb])
                        cp("act", qT[:, tcols(t)], pb[:, 0:128], [b_pb], [b_qT])
                        if DBG.get("A_nokm"):
                            cp("dve", k1p[:, tcols(t)], pb[:, 128:256], [b_pb], [b_k1p])
                        else:
                            ts("dve", k1p[:, tcols(t)], pb[:, 128:256], kmask[:, 0:1], ALU.mult, [b_pb, b_kmask], [b_k1p])
                            ts("dve", k2p[:, tcols(t)], pb[:, 128:256], kmask[:, 1:2], ALU.mult, [b_pb, b_kmask], [b_k2p])
                        psv, b_psv = ps_mm()
                        proj(tcols(t), Wv, b_Wv, 128, psv, b_psv)
                        cp("act", vaug[:, t, :, 0:64], psv[:, 0:128].rearrange("p (h d) -> p h d", d=64), [b_psv], [b_vaug])
                    P.barrier()
                    ph2.close()
                    Et = [sbuf(ph, f"A_E{i}", [128, 512], BF16) for i in range(4)]
                    OTs = [sbuf(ph, f"A_OTs{i}", [65, 2, 512], F32) for i in range(2)]
                    opre, b_opre = sbuf(ph, "A_opre", [128, 4, 2, 64], F32)
                    tA = [sbuf(ph, f"A_tA{i}", [128, 64], F32) for i in range(2)]
                    tB = [sbuf(ph, f"A_tB{i}", [128, 64], F32) for i in range(2)]
                    rden = [sbuf(ph, f"A_rden{i}", [128, 2], F32) for i in range(2)]
                    osq = [sbuf(ph, f"A_osq{i}", [128, 128], F32) for i in range(2)]
                    oss = [sbuf(ph, f"A_oss{i}", [128, 2], F32) for i in range(2)]
                    onn = [sbuf(ph, f"A_on{i}", [128, 128], F32) for i in range(2)]
                    gsc = [sbuf(ph, f"A_gsc{i}", [128, 128], F32) for i in range(2)]
                    ofin = [sbuf(ph, f"A_ofin{i}", [128, 128], BF16) for i in range(2)]
                    ei = 0
                    oi = 0
                    for g in range(S // 512):
                        qcols = slice(g * 512, (g + 1) * 512)
                        for hl in range(2):
                            base = 64 * hl
                            ots, b_ots = OTs[oi % 2]
                            oi += 1
                            for m in range(2):
                                kp, b_kp = (k1p, b_k1p) if m == 0 else (k2p, b_k2p)
                                ot, b_ot = ps_ot()
                                for kt in range(NT):
                                    stp, b_st = ps_st()
                                    op("pe", lambda e, stp=stp, kp=kp, kt=kt, base=base, qcols=qcols: e.matmul(
                                        stp[:, :], lhsT=kp[base:base + 64, tcols(kt)], rhs=qT[base:base + 64, qcols], start=True, stop=True),
                                       [b_kp, b_qT], [b_st])
                                    E, b_E = Et[ei % 4]
                                    ei += 1
                                    op("act", lambda e, E=E, stp=stp: e.activation(out=E[:], in_=stp[:], func=AF.Exp, scale=32 ** -0.5), [b_st], [b_E])
                                    op("pe", lambda e, ot=ot, kt=kt, hl=hl, E=E: e.matmul(ot[0:65, :], lhsT=vaug[:, kt, hl, 0:65], rhs=E[:], start=(kt == 0), stop=(kt == NT - 1)),
                                       [b_vaug, b_E], [b_ot])
                                cp("dve", ots[0:65, m, :], ot[0:65, :], [b_ot], [b_ots])
                            for t4 in range(4):
                                psT, b_psT = ps_mm()
                                for m in range(2):
                                    op("pe", lambda e, psT=psT, m=m, ots=ots, t4=t4: e.transpose(psT[:, m * 65:(m + 1) * 65], ots[0:65, m, t4 * 128:(t4 + 1) * 128], ident_f[0:65, 0:65]),
                                       [b_ots, b_idf], [b_psT])
                                rd, b_rd = rden[t4 % 2]
                                ta, b_ta = tA[t4 % 2]
                                tb_, b_tb = tB[t4 % 2]
                                op("dve", lambda e, rd=rd, psT=psT: e.reciprocal(out=rd[:], in_=psT[:, 0:130].rearrange("p (m e) -> p m e", e=65)[:, :, 64]), [b_psT], [b_rd])
                                ts("dve", ta[:], psT[:, 0:64], rd[:, 0:1], ALU.mult, [b_psT, b_rd], [b_ta])
                                op("act", lambda e, tb_=tb_, psT=psT, rd=rd: e.activation(out=tb_[:], in_=psT[:, 65:129], func=AF.Copy, scale=rd[:, 1:2]), [b_psT, b_rd], [b_tb])
                                op("dve", lambda e, t4=t4, hl=hl, tb_=tb_, ta=ta: e.scalar_tensor_tensor(out=opre[:, t4, hl, :], in0=tb_[:], scalar=lamt[:, 6:7], in1=ta[:],
                                                                                                        op0=ALU.mult, op1=ALU.add), [b_tb, b_ta, b_lam], [b_opre])
                        for t4 in range(4):
                            t = g * 4 + t4
                            sq_, b_sq = osq[t4 % 2]
                            ss_, b_ss = oss[t4 % 2]
                            on_, b_on = onn[t4 % 2]
                            o2 = opre[:, t4, :, :]
                            tt("pool", sq_[:].rearrange("p (h d) -> p h d", d=64), o2, o2, ALU.mult, [b_opre], [b_sq])
                            op("dve", lambda e, ss_=ss_, sq_=sq_: e.reduce_sum(out=ss_[:], in_=sq_[:].rearrange("p (h d) -> p h d", d=64), axis=AX.X), [b_sq], [b_ss])
                            rstd_inplace(ss_[:], b_ss, 64.0)
                            on3 = on_[:].rearrange("p (h d) -> p h d", d=64)
                            tt("dve", on3, o2, ss_[:].unsqueeze(2).to_broadcast([128, 2, 64]), ALU.mult, [b_opre, b_ss], [b_on])
                            tt("pool", on3, on3, gS[:], ALU.mult, [b_on, b_gS], [b_on])
                            gate_and_store(t, on_[:], b_on, Wg, b_Wg, 128, OCOL_A + 128 * c, gsc[t4 % 2][0], gsc[t4 % 2][1], ofin[t4 % 2][0], ofin[t4 % 2][1])
                    P.barrier()

            for c in range(3 if "B" in stages else 0):
                with ExitStack() as ph:
                    Wqk, b_Wqk = sbuf(ph, "B_Wqk", [128, 8, 256], BF16)
                    Wv, b_Wv = sbuf(ph, "B_Wv", [128, 8, 128], BF16)
                    Wg, b_Wg = sbuf(ph, "B_Wg", [128, 8, 128], BF16)
                    wl = w_in[l]
                    load_w(Wqk, b_Wqk, wl[:, OFF_QB + 128 * c:OFF_QB + 128 * c + 128], 128, 0)
                    load_w(Wqk, b_Wqk, wl[:, OFF_KB + 128 * c:OFF_KB + 128 * c + 128], 128, 128)
                    load_w(Wv, b_Wv, wl[:, OFF_VB + 128 * c:OFF_VB + 128 * c + 128], 128)
                    load_w(Wg, b_Wg, wl[:, OFF_GB + 128 * c:OFF_GB + 128 * c + 128], 128)
                    qT, b_qT = sbuf(ph, "B_qT", [128, S], BF16)
                    kT, b_kT = sbuf(ph, "B_kT", [128, S], BF16)
                    vB, b_vB = sbuf(ph, "B_vB", [128, 3, NT, 2, 66], BF16)
                    op("pool", lambda e: e.memset(vB[:, :, :, :, 64:65].rearrange("p a b c d -> p (a b c d)"), 1.0), [], [b_vB])
                    ph2 = ExitStack()
                    scr = make_scr(ph2, "B", 2)
                    qkb = [sbuf(ph2, f"B_qkb{i}", [128, 256], BF16) for i in range(2)]
                    for t in range(NT):
                        ps, b_ps = ps_mm()
                        proj(tcols(t), Wqk, b_Wqk, 256, ps, b_ps)
                        qk_t, b_qk = qkb[t % 2]
                        norm_rope(ps[:, 0:256].rearrange("p (g d) -> p g d", d=64), b_ps, 4, 64, gB[:], b_gB, 8,
                                  rope[:, t, 8:16], rope[:, t, 16:24], qk_t[:].rearrange("p (g d) -> p g d", d=64), b_qk, scr[t % 2])
                        pb, b_pb = ps_tb()
                        for j in range(2):
                            op("pe", lambda e, j=j, pb=pb, qk_t=qk_t: e.transpose(pb[:, j * 128:(j + 1) * 128], qk_t[:, j * 128:(j + 1) * 128], ident_b[:]),
                               [b_qk, b_idb], [b_pb])
                        cp("act", qT[:, tcols(t)], pb[:, 0:128], [b_pb], [b_qT])
                        cp("dve", kT[:, tcols(t)], pb[:, 128:256], [b_pb], [b_kT])
                    for gi, D in enumerate((1, 4, 16)):
                        ntl = S // D // 128
                        for r in range(D):
                            for j in range(ntl):
                                psv, b_psv = ps_mm()
                                c0 = r + D * 128 * j
                                proj(slice(c0, c0 + D * 127 + 1, D), Wv, b_Wv, 128, psv, b_psv)
                                cp("act" if (r + j) % 2 else "dve", vB[:, gi, r * ntl + j, :, 0:64], psv[:, 0:128].rearrange("p (h d) -> p h d", d=64), [b_psv], [b_vB])
                    P.barrier()
                    ph2.close()
                    Et = [sbuf(ph, f"B_E{i}", [128, 384], BF16) for i in range(4)]
                    acc, b_acc = sbuf(ph, "B_acc", [65, 2048], F32)
                    ob, b_ob = sbuf(ph, "B_ob", [128, 16, 2, 64], F32)
                    rden = [sbuf(ph, f"B_rden{i}", [128, 1], F32) for i in range(2)]
                    gsc = [sbuf(ph, f"B_gsc{i}", [128, 128], F32) for i in range(2)]
                    ofin = [sbuf(ph, f"B_ofin{i}", [128, 128], BF16) for i in range(2)]
                    ei = 0
                    for u in range(S // 2048):
                        for hl in range(2):
                            base = 64 * hl
                            for gi, D in enumerate((1, 4, 16)):
                                ntl = S // D // 128
                                per = 16 // D
                                for r in range(D):
                                    for i in range(u * per, (u + 1) * per):
                                        qc0 = r + D * 128 * i
                                        qcols = slice(qc0, qc0 + D * 127 + 1, D)
                                        js = [j for j in (i - 1, i, i + 1) if 0 <= j < ntl]
                                        stp, b_st = ps_st()
                                        for j in js:
                                            jj = j - (i - 1)
                                            kc0 = r + D * 128 * j
                                            op("pe", lambda e, stp=stp, jj=jj, kc0=kc0, D=D, base=base, qcols=qcols: e.matmul(
                                                stp[:, jj * 128:(jj + 1) * 128], lhsT=kT[base:base + 64, slice(kc0, kc0 + D * 127 + 1, D)], rhs=qT[base:base + 64, qcols],
                                                start=True, stop=True), [b_kT, b_qT], [b_st])
                                        lo = (js[0] - (i - 1)) * 128
                                        hi = (js[-1] - (i - 1) + 1) * 128
                                        E, b_E = Et[ei % 4]
                                        ei += 1
                                        op("act", lambda e, E=E, stp=stp, lo=lo, hi=hi: e.activation(out=E[:, lo:hi], in_=stp[:, lo:hi], func=AF.Exp, scale=0.125), [b_st], [b_E])
                                        tt("pool", E[:, lo:hi], E[:, lo:hi], maskB[:, lo:hi], ALU.mult, [b_E, b_maskB], [b_E])
                                        ot, b_ot = ps_ot()
                                        for j in js:
                                            jj = j - (i - 1)
                                            op("pe", lambda e, ot=ot, gi=gi, tile=r * ntl + j, hl=hl, E=E, jj=jj, first=(j == js[0]), last=(j == js[-1]): e.matmul(
                                                ot[0:65, 0:128], lhsT=vB[:, gi, tile, hl, 0:65], rhs=E[:, jj * 128:(jj + 1) * 128], start=first, stop=last),
                                               [b_vB, b_E], [b_ot])
                                        a0 = qc0 - 2048 * u
                                        acc_ap = acc[0:65, a0:a0 + D * 127 + 1:D]
                                        if gi == 0:
                                            cp("dve", acc_ap, ot[0:65, 0:128], [b_ot], [b_acc])
                                        else:
                                            tt("dve", acc_ap, ot[0:65, 0:128], acc_ap, ALU.add, [b_ot, b_acc], [b_acc])
                            for t16 in range(16):
                                psT, b_psT = ps_mm()
                                op("pe", lambda e, psT=psT, t16=t16: e.transpose(psT[:, 0:65], acc[0:65, t16 * 128:(t16 + 1) * 128], ident_f[0:65, 0:65]), [b_acc, b_idf], [b_psT])
                                rd, b_rd = rden[t16 % 2]
                                op("dve", lambda e, rd=rd, psT=psT: e.reciprocal(out=rd[:], in_=psT[:, 64:65]), [b_psT], [b_rd])
                                ts("dve", ob[:, t16, hl, :], psT[:, 0:64], rd[:, 0:1], ALU.mult, [b_psT, b_rd], [b_ob])
                        for t16 in range(16):
                            t = u * 16 + t16
                            gate_and_store(t, ob[:, t16, :, :].rearrange("p h d -> p (h d)"), b_ob, Wg, b_Wg, 128, OCOL_B + 128 * c,
                                           gsc[t16 % 2][0], gsc[t16 % 2][1], ofin[t16 % 2][0], ofin[t16 % 2][1])
                    P.barrier()

            for c in range(2 if "C" in stages else 0):
                with ExitStack() as ph:
                    Wqk, b_Wqk = sbuf(ph, "C_Wqk", [128, 8, 192], BF16)
                    Wv, b_Wv = sbuf(ph, "C_Wv", [128, 8, 192], BF16)
                    Wg, b_Wg = sbuf(ph, "C_Wg", [128, 8, 192], BF16)
                    wl = w_in[l]
                    load_w(Wqk, b_Wqk, wl[:, OFF_QC + 96 * c:OFF_QC + 96 * c + 96], 96, 0)
                    load_w(Wqk, b_Wqk, wl[:, OFF_KC + 96 * c:OFF_KC + 96 * c + 96], 96, 96)
                    load_w(Wv, b_Wv, wl[:, OFF_VC + 192 * c:OFF_VC + 192 * c + 192], 192)
                    load_w(Wg, b_Wg, wl[:, OFF_GC + 192 * c:OFF_GC + 192 * c + 192], 192)
                    qkT, b_qkT = sbuf(ph, "C_qkT", [64, 4, S], BF16)
                    vC, b_vC = sbuf(ph, "C_vC", [128, NT, 2, 96], BF16)
                    Rst, b_Rst = sbuf(ph, "C_Rst", [64, NT, 2, 2, 96], BF16)
                    ph_mid = ExitStack()
                    kfb, b_kfb = sbuf(ph_mid, "C_kfb", [128, NT, 2, 2, 48], BF16)
                    Rs = [sbuf(ph_mid, f"C_R{i}", [64, 2, 96], F32) for i in range(2)]
                    cdt = [sbuf(ph_mid, f"C_cd{i}", [64, 2, 96], F32) for i in range(2)]
                    ph2 = ExitStack()
                    scr = make_scr(ph2, "C", 2)
                    qkr = [sbuf(ph2, f"C_qkr{i}", [128, 4, 48], F32) for i in range(2)]
                    qkp = [sbuf(ph2, f"C_qkp{i}", [128, 4, 64], BF16) for i in range(2)]
                    for i in range(2):
                        op("pool", lambda e, i=i: e.memset(qkp[i][0][:].rearrange("p g d -> p (g d)"), 0.0), [], [qkp[i][1]])
                    for t in range(NT):
                        ps, b_ps = ps_mm()
                        proj(tcols(t), Wqk, b_Wqk, 192, ps, b_ps)
                        qr, b_qr = qkr[t % 2]
                        qp, b_qp = qkp[t % 2]
                        norm_rope(ps[:, 0:192].rearrange("p (g d) -> p g d", d=48), b_ps, 4, 48, None, None, 24,
                                  rope[:, t, 24:48], rope[:, t, 48:72], qr[:], b_qr, scr[t % 2])
                        ts("pool", qr[:, 2:4, :], qr[:, 2:4, :], 48 ** -0.5, ALU.mult, [b_qr], [b_qr])
                        cp("act", qp[:, :, 0:48], qr[:], [b_qr], [b_qp])
                        pb, b_pb = ps_tb()
                        for s_ in range(4):
                            op("pe", lambda e, s_=s_, pb=pb, qp=qp: e.transpose(pb[0:64, s_ * 128:(s_ + 1) * 128], qp[:, s_, :], ident_b[:]), [b_qp, b_idb], [b_pb])
                        cp("dve", qkT[:, :, tcols(t)], pb[0:64, 0:512].rearrange("p (g n) -> p g n", n=128), [b_pb], [b_qkT])
                        tt("dve", kfb[:, t, 0, :, :], qr[:, 2:4, :], decs[:, 8 + 2 * c:10 + 2 * c].unsqueeze(2).to_broadcast([128, 2, 48]), ALU.mult, [b_qr, b_decs], [b_kfb])
                        tt("pool", kfb[:, t, 1, :, :], qr[:, 2:4, :], decs[:, 12 + 2 * c:14 + 2 * c].unsqueeze(2).to_broadcast([128, 2, 48]), ALU.mult, [b_qr, b_decs], [b_kfb])
                        psv, b_psv = ps_mm()
                        proj(tcols(t), Wv, b_Wv, 192, psv, b_psv)
                        cp("act", vC[:, t, :, :], psv[:, 0:192].rearrange("p (h d) -> p h d", d=96), [b_psv], [b_vC])
                    P.barrier()
                    ph2.close()
                    for d_ in range(2):
                        R_, b_R = Rs[d_]
                        cd_, b_cd = cdt[d_]
                        op("pool", lambda e, R_=R_: e.memset(R_[:].rearrange("p h d -> p (h d)"), 0.0), [], [b_R])
                        cp("dve", cd_[0:48, :, :], decs[0:48, 24 + 4 * d_ + 2 * c:26 + 4 * d_ + 2 * c].unsqueeze(2).to_broadcast([48, 2, 96]), [b_decs], [b_cd])
                        order = range(NT) if d_ == 0 else range(NT - 1, -1, -1)
                        for n in order:
                            psU, b_psU = ps_mm()
                            for hl in range(2):
                                op("pe", lambda e, psU=psU, n=n, d_=d_, hl=hl: e.matmul(psU[0:48, hl * 96:(hl + 1) * 96], lhsT=kfb[:, n, d_, hl, :], rhs=vC[:, n, hl, :], start=True, stop=True),
                                   [b_kfb, b_vC], [b_psU])
                            cp("act", Rst[0:48, n, d_, :, :], R_[0:48, :, :], [b_R], [b_Rst])
                            tt("dve", R_[0:48, :, :], R_[0:48, :, :], cd_[0:48, :, :], ALU.mult, [b_R, b_cd], [b_R])
                            tt("dve", R_[0:48, :, :], psU[0:48, 0:192].rearrange("p (h d) -> p h d", d=96), R_[0:48, :, :], ALU.add, [b_psU, b_R], [b_R])
                    P.barrier()
                    ph_mid.close()
                    att = [sbuf(ph, f"C_att{i}", [128, 128], BF16) for i in range(2)]
                    oc = [sbuf(ph, f"C_oc{i}", [128, 2, 96], F32) for i in range(2)]
                    osq = [sbuf(ph, f"C_osq{i}", [128, 2, 96], F32) for i in range(2)]
                    oss = [sbuf(ph, f"C_oss{i}", [128, 2], F32) for i in range(2)]
                    gsc = [sbuf(ph, f"C_gsc{i}", [128, 192], F32) for i in range(2)]
                    ofin = [sbuf(ph, f"C_ofin{i}", [128, 192], BF16) for i in range(2)]
                    ai = 0
                    for n in range(NT):
                        oc_, b_oc = oc[n % 2]
                        for hl in range(2):
                            head = 2 * c + hl
                            stp, b_st = ps_st()
                            op("pe", lambda e, stp=stp, hl=hl, n=n: e.matmul(stp[:, 0:128], lhsT=qkT[0:48, 2 + hl, tcols(n)], rhs=qkT[0:48, hl, tcols(n)], start=True, stop=True),
                               [b_qkT], [b_st])
                            at_, b_at = att[ai % 2]
                            ai += 1
                            tt("dve", at_[:], stp[:, 0:128], DT[:, head, :], ALU.mult, [b_st, b_DT], [b_at])
                            psO, b_psO = ps_ot()
                            op("pe", lambda e, psO=psO, at_=at_, n=n, hl=hl: e.matmul(psO[:, 0:96], lhsT=at_[:], rhs=vC[:, n, hl, :], start=True, stop=True), [b_at, b_vC], [b_psO])
                            for d_ in range(2):
                                op("pe", lambda e, psO=psO, n=n, hl=hl, d_=d_: e.matmul(psO[:, 96 * (d_ + 1):96 * (d_ + 2)], lhsT=qkT[0:48, hl, tcols(n)], rhs=Rst[0:48, n, d_, hl, :],
                                                                                       start=True, stop=True), [b_qkT, b_Rst], [b_psO])
                            cp("act", oc_[:, hl, :], psO[:, 0:96], [b_psO], [b_oc])
                            for d_ in range(2):
                                op("dve", lambda e, oc_=oc_, hl=hl, psO=psO, d_=d_, head=head: e.scalar_tensor_tensor(
                                    out=oc_[:, hl, :], in0=psO[:, 96 * (d_ + 1):96 * (d_ + 2)], scalar=decs[:, 16 + 4 * d_ + head:17 + 4 * d_ + head], in1=oc_[:, hl, :],
                                    op0=ALU.mult, op1=ALU.add), [b_psO, b_decs, b_oc], [b_oc])
                        sq_, b_sq = osq[n % 2]
                        ss_, b_ss = oss[n % 2]
                        tt("pool", sq_[:], oc_[:], oc_[:], ALU.mult, [b_oc], [b_sq])
                        op("dve", lambda e, ss_=ss_, sq_=sq_: e.reduce_sum(out=ss_[:], in_=sq_[:], axis=AX.X), [b_sq], [b_ss])
                        rstd_inplace(ss_[:], b_ss, 96.0)
                        tt("dve", sq_[:], oc_[:], ss_[:].unsqueeze(2).to_broadcast([128, 2, 96]), ALU.mult, [b_oc, b_ss], [b_sq])
                        tt("pool", sq_[:], sq_[:], spt[:, SP_GNC:SP_GNC + 96].unsqueeze(1).to_broadcast([128, 2, 96]), ALU.mult, [b_sq, b_spt], [b_sq])
                        gate_and_store(n, sq_[:].rearrange("p h d -> p (h d)"), b_sq, Wg, b_Wg, 192, OCOL_C + 192 * c,
                                       gsc[n % 2][0], gsc[n % 2][1], ofin[n % 2][0], ofin[n % 2][1])
                    P.barrier()

            if dbg == "o":
                with ExitStack() as ph:
                    ob16, b_ob16 = sbuf(ph, "dbg_ob", [128, 1024], BF16)
                    o32, b_o32 = sbuf(ph, "dbg_o32", [128, 1024], F32)
                    for t in range(NT):
                        P.dma("dbg_l", ob16[:], o_scr[t * 128:(t + 1) * 128, :], reads=[obufs[t]], writes=[b_ob16])
                        cp("dve", o32[:], ob16[:], [b_ob16], [b_o32])
                        P.dma("dbg_s", dbg_o[t * 128:(t + 1) * 128, :], o32[:], reads=[b_o32], writes=[xbufs[t]])
                break

            if "O" in stages:
                with ExitStack() as ph:
                    Wout, b_Wout = sbuf(ph, "Wout", [128, 8, 1024], BF16)
                    load_w(Wout, b_Wout, w_out[l], 1024)
                    ott = [sbuf(ph, f"O_ot{i}", [128, 1024], BF16) for i in range(2)]
                    oTt = [sbuf(ph, f"O_oT{i}", [128, 8, 128], BF16) for i in range(2)]
                    xt = [sbuf(ph, f"O_xt{i}", [128, 1024], F32) for i in range(2)]
                    tm = [sbuf(ph, f"O_tm{i}", [128, 1024], F32) for i in range(2)]
                    for t in range(NT):
                        o_t, b_o = ott[t % 2]
                        oT_t, b_oT = oTt[t % 2]
                        x_t, b_x = xt[t % 2]
                        tm_t, b_tm = tm[t % 2]
                        P.dma("O_ot%d" % (t % 2), o_t[:], o_scr[t * 128:(t + 1) * 128, :], reads=[obufs[t]], writes=[b_o])
                        P.dma("O_xt%d" % (t % 2), x_t[:], xsrc[t * 128:(t + 1) * 128, :], reads=[xbufs[t]], writes=[b_x])
                        pb, b_pb = ps_tb()
                        for k in range(8):
                            op("pe", lambda e, k=k, pb=pb, o_t=o_t: e.transpose(pb[:, k * 128:(k + 1) * 128], o_t[:, k * 128:(k + 1) * 128], ident_b[:]), [b_o, b_idb], [b_pb])
                        cp("act", oT_t[:], pb[:, :].rearrange("p (k n) -> p k n", n=128), [b_pb], [b_oT])
                        for nh in range(2):
                            ps, b_ps = ps_mm()
                            for k in range(8):
                                op("pe", lambda e, k=k, ps=ps, oT_t=oT_t, nh=nh: e.matmul(ps[:, :], lhsT=oT_t[:, k, :], rhs=Wout[:, k, nh * 512:(nh + 1) * 512], start=(k == 0), stop=(k == 7)),
                                   [b_oT, b_Wout], [b_ps])
                            sl = slice(nh * 512, (nh + 1) * 512)
                            tt("dve", tm_t[:, sl], ps[:, :], gate_bc[:, sl], ALU.mult, [b_ps, b_gate], [b_tm])
                        tt("pool", tm_t[:], tm_t[:], x_t[:], ALU.add, [b_tm, b_x], [b_tm])
                        P.dma("O_st%d" % (t % 2), y_out[t * 128:(t + 1) * 128, :], tm_t[:], reads=[b_tm], writes=[xbufs[t]])
                    P.barrier()

        P.barrier()
        P.emit()
    return nc, P


def host_constants(S):
    NT = S // 128
    pos = np.arange(S, dtype=np.float32)

    def tab(theta, rot):
        inv = (np.float32(theta) ** (-np.arange(0, rot, 2, dtype=np.float32) / np.float32(rot))).astype(np.float32)
        ang = (pos[:, None] * inv[None, :]).astype(np.float32)
        return np.cos(ang).astype(np.float32), np.sin(ang).astype(np.float32)

    cA, sA = tab(ROT_THETA, 8)
    cB, sB = tab(ROT_THETA, 16)
    cC, sC = tab(RET_THETA, 48)
    rope = np.concatenate([cA, sA, cB, sB, cC, sC], axis=1)
    rope = np.ascontiguousarray(rope.reshape(NT, 128, 72).transpose(1, 0, 2)).astype(np.float32)
    a = np.arange(128)[:, None]
    b = np.arange(128)[None, :]
    maskB = np.concatenate([(a - b >= 64), (np.abs(a - b) <= 64), (b - a >= 64)], axis=1).astype(np.float32).astype(ml_dtypes.bfloat16)
    j = a
    i = b
    retc = np.zeros((128, 516), np.float32)
    retc[:, 0:128] = np.maximum(i - j, 0)
    retc[:, 128:256] = (i >= j)
    retc[:, 256:384] = np.maximum(j - i, 0)
    retc[:, 384:512] = (j > i)
    p = np.arange(128, dtype=np.float32)
    retc[:, 512] = p + 1
    retc[:, 513] = 127 - p
    retc[:, 514] = 128 - p
    retc[:, 515] = p
    kmask = np.zeros((128, 2), np.float32)
    kmask[:, 0] = ((np.arange(128) % 64) < 32)
    kmask[:, 1] = ((np.arange(128) % 64) >= 32)
    return {
        "rope": rope, "maskB": maskB, "retc": retc, "kmask": kmask,
        "ident_b": np.eye(128, dtype=np.float32).astype(ml_dtypes.bfloat16),
        "ident_f": np.eye(128, dtype=np.float32),
    }


def pack_small(inp, L):
    parts = [inp["qn_a"], inp["kn_a"], inp["lambda_q1"], inp["lambda_k1"], inp["lambda_q2"], inp["lambda_k2"],
             inp["subln_a"], inp["qn_b"], inp["kn_b"], np.asarray(inp["ret_decay"]).reshape(L, 8), inp["gn_c"]]
    return np.ascontiguousarray(np.concatenate([np.asarray(p_, np.float32).reshape(L, -1) for p_ in parts], axis=1))


def make_in_maps(inp, S, L, B):
    consts = host_constants(S)
    f = lambda k: np.ascontiguousarray(np.asarray(inp[k], np.float32))
    shared = {
        "w_ada": f("w_ada")[:L], "b_ada": f("b_ada")[:L], "norm_g": f("norm_g")[:L], "w_in": f("w_in")[:L], "w_out": f("w_out")[:L],
        "smallp": pack_small({k: np.asarray(v)[:L] for k, v in inp.items() if k not in ("x", "c")}, L),
    }
    shared.update(consts)
    maps = []
    x = f("x")
    c = f("c")
    for b in range(B):
        m = dict(shared)
        m["x"] = np.ascontiguousarray(x[b])
        m["cT"] = np.ascontiguousarray(c[b].reshape(8, 128).T)
        maps.append(m)
    return maps


_CACHE = {}


def kernel(**inputs):
    x = np.asarray(inputs["x"])
    B, S, _ = x.shape
    L = np.asarray(inputs["w_in"]).shape[0]
    key = (S, L)
    if key not in _CACHE:
        _CACHE[key] = build_program(S, L)[0]
    nc = _CACHE[key]
    maps = make_in_maps(inputs, S, L, B)
    in_maps = [maps[i % B] for i in range(8)]
    res = run_bass_kernel_spmd(nc, in_maps, core_ids=list(range(8)))
    out = np.stack([np.asarray(res.results[b]["y"], np.float32) for b in range(B)], axis=0)
    return out
```

```python
import math
from contextlib import ExitStack
import numpy as np
import ml_dtypes
import concourse.bass as bass
import concourse.mybir as mybir
from concourse.bass_utils import run_bass_kernel_spmd

F32 = mybir.dt.float32
BF16 = mybir.dt.bfloat16
ALU = mybir.AluOpType
AF = mybir.ActivationFunctionType
AX = mybir.AxisListType

D_MODEL = 1024
IN_COLS = 3712
EPS = 1e-6
ROT_THETA = 500000.0
RET_THETA = 10000.0
NSMALL = 488
SP_QNA, SP_KNA, SP_LQ1, SP_LK1, SP_LQ2, SP_LK2 = 0, 32, 64, 96, 128, 160
SP_SUB, SP_QNB, SP_KNB, SP_DEC, SP_GNC = 192, 256, 320, 384, 392
OFF_QA, OFF_KA, OFF_VA, OFF_GA = 0, 256, 512, 768
OFF_QB, OFF_KB, OFF_VB, OFF_GB = 1024, 1408, 1792, 2176
OFF_QC, OFF_KC, OFF_VC, OFF_GC = 2560, 2752, 2944, 3328
OCOL_A, OCOL_B, OCOL_C = 0, 256, 640


class Buf:
    __slots__ = ("name", "w", "r", "excl")

    def __init__(self, name, excl=False):
        self.name = name
        self.w = None
        self.r = {}
        self.excl = excl


class Prog:
    ENG = ("pe", "act", "dve", "pool", "sp")

    def __init__(self, nc, stack):
        self.nc = nc
        self.stack = stack
        self.q = {e: [] for e in self.ENG}
        self.cnt = {e: 0 for e in self.ENG}
        self.seen = {e: {} for e in self.ENG}
        self.sems = {}
        self.dcnt = {}
        for e in self.ENG:
            self.sems[e] = stack.enter_context(nc.semaphore("s_" + e))
        self.ninst = 0
        self.desc = {}
        self.total = 0

    def dsem(self, name):
        if name not in self.sems:
            self.sems[name] = self.stack.enter_context(self.nc.semaphore("d_" + name))
            self.dcnt[name] = 0
        return name

    def _wait(self, eng, k, v):
        if k == eng and eng == "pe":
            return
        if self.seen[eng].get(k, 0) < v:
            self.seen[eng][k] = v
            sem = self.sems[k]
            self.ninst += 1
            self.desc.setdefault(eng, []).append("wait %s>=%d" % (k, v))
            self.q[eng].append(lambda e, sem=sem, v=v: e.wait_ge(sem, v))

    def _deps(self, eng, reads, writes):
        need = {}
        for b in reads:
            if b.w is not None:
                k, v = b.w
                if need.get(k, 0) < v:
                    need[k] = v
            if b.excl:
                for k, v in b.r.items():
                    if k != eng and need.get(k, 0) < v:
                        need[k] = v
        for b in writes:
            if b.w is not None:
                k, v = b.w
                if need.get(k, 0) < v:
                    need[k] = v
            for k, v in b.r.items():
                if need.get(k, 0) < v:
                    need[k] = v
        for k, v in need.items():
            self._wait(eng, k, v)

    def op(self, eng, fn, reads=(), writes=()):
        self.total = getattr(self, "total", 0) + 1
        if self.total > DBG.get("cut", 10 ** 9):
            return
        self._deps(eng, reads, writes)
        if DBG.get("serial") and getattr(self, "last", None):
            self._wait(eng, *self.last)
        self.cnt[eng] += 1
        c = self.cnt[eng]
        self.last = (eng, c)
        sem = self.sems[eng]
        self.ninst += 1
        self.desc.setdefault(eng, []).append("op#%d (%s=%d)" % (self.total, eng, c))
        self.q[eng].append(lambda e, fn=fn, sem=sem: fn(e).then_inc(sem, 1))
        for b in writes:
            b.w = (eng, c)
            b.r = {}
        for b in reads:
            if b.w != (eng, c):
                b.r[eng] = c

    def dma(self, semname, out, in_, reads=(), writes=(), qeng="sp", **kw):
        self.total = getattr(self, "total", 0) + 1
        if self.total > DBG.get("cut", 10 ** 9):
            return
        self.dsem(semname)
        self._deps(qeng, reads, writes)
        if DBG.get("serial") and getattr(self, "last", None):
            self._wait(qeng, *self.last)
        self.dcnt[semname] += 16
        v = self.dcnt[semname]
        self.last = (semname, v)
        sem = self.sems[semname]
        self.ninst += 1
        self.desc.setdefault(qeng, []).append("dma#%d (%s=%d)" % (self.total, semname, v))
        self.q[qeng].append(
            lambda e, out=out, in_=in_, sem=sem, kw=kw: e.dma_start(out=out, in_=in_, **kw).then_inc(sem, 16))
        for b in writes:
            b.w = (semname, v)
            b.r = {}
        for b in reads:
            b.r[semname] = v

    def barrier(self):
        for e in self.ENG:
            for k in list(self.sems.keys()):
                v = self.cnt[k] if k in self.cnt else self.dcnt[k]
                if v > 0 and k != e:
                    self._wait(e, k, v)
            if e != "pe" and self.cnt[e] > 0:
                self._wait(e, e, self.cnt[e])

    def emit(self):
        nc = self.nc
        with nc.Block() as block:
            @block.tensor
            def _(e):
                for t in self.q["pe"]:
                    t(e)

            @block.scalar
            def _(e):
                for t in self.q["act"]:
                    t(e)

            @block.vector
            def _(e):
                for t in self.q["dve"]:
                    t(e)

            @block.gpsimd
            def _(e):
                for t in self.q["pool"]:
                    t(e)

            @block.sync
            def _(e):
                for t in self.q["sp"]:
                    t(e)


DBG = {}


def build_program(S, L, stages=("A", "B", "C", "O"), dbg=None):
    NT = S // 128
    nc = bass.Bass("TRN2", target_bir_lowering=False)
    dr = lambda name, shape, dt, kind="ExternalInput": nc.dram_tensor(name, shape, dt, kind=kind).ap()
    x_in = dr("x", [S, D_MODEL], F32)
    cT_in = dr("cT", [128, 8], F32)
    w_ada = dr("w_ada", [L, D_MODEL, 3 * D_MODEL], F32)
    b_ada = dr("b_ada", [L, 3 * D_MODEL], F32)
    norm_g = dr("norm_g", [L, D_MODEL], F32)
    w_in = dr("w_in", [L, D_MODEL, IN_COLS], F32)
    w_out = dr("w_out", [L, D_MODEL, D_MODEL], F32)
    smallp = dr("smallp", [L, NSMALL], F32)
    ident_b_d = dr("ident_b", [128, 128], BF16)
    ident_f_d = dr("ident_f", [128, 128], F32)
    rope_d = dr("rope", [128, NT, 72], F32)
    maskB_d = dr("maskB", [128, 384], BF16)
    retc_d = dr("retc", [128, 516], F32)
    kmask_d = dr("kmask", [128, 2], F32)
    y_out = dr("y", [S, D_MODEL], F32, kind="ExternalOutput")
    o_scr = dr("o_scr", [S, D_MODEL], BF16, kind="ExternalOutput")
    dbg_hT = dr("dbg_hT", [128, 8 * S], F32, kind="ExternalOutput") if dbg == "hT" else None
    dbg_o = dr("dbg_o", [S, D_MODEL], F32, kind="ExternalOutput") if dbg == "o" else None

    with ExitStack() as st:
        P = Prog(nc, st)

        uid = [0]

        def sbuf(stack, name, shape, dt):
            uid[0] += 1
            t = stack.enter_context(nc.sbuf_tensor("sb%d_%s" % (uid[0], name), shape, dt))
            return t, Buf(name)

        def op(eng, fn, reads=(), writes=()):
            if eng == "pool" and DBG.get("nopool"):
                eng = "dve"
            P.op(eng, fn, reads, writes)

        def cp(eng, out, in_, reads, writes):
            if eng == "act":
                op("act", lambda e: e.copy(out=out, in_=in_), reads, writes)
            else:
                op(eng, lambda e: e.tensor_copy(out=out, in_=in_), reads, writes)

        def tt(eng, out, in0, in1, alu, reads, writes):
            op(eng, lambda e: e.tensor_tensor(out=out, in0=in0, in1=in1, op=alu), reads, writes)

        def ts(eng, out, in0, s1, alu, reads, writes):
            op(eng, lambda e: e.tensor_scalar(out=out, in0=in0, scalar1=s1, scalar2=None, op0=alu), reads, writes)

        hT, b_hT = sbuf(st, "hT", [128, 8, S], BF16)
        rope, b_rope = sbuf(st, "rope", [128, NT, 72], F32)
        ident_b, b_idb = sbuf(st, "ident_b", [128, 128], BF16)
        ident_f, b_idf = sbuf(st, "ident_f", [128, 128], F32)
        maskB, b_maskB = sbuf(st, "maskB", [128, 384], BF16)
        retc, b_retc = sbuf(st, "retc", [128, 516], F32)
        kmask, b_kmask = sbuf(st, "kmask", [128, 2], F32)
        cs, b_cs = sbuf(st, "cs", [128, 8], F32)
        ones_row, b_ones = sbuf(st, "ones_row", [1, 128], F32)
        gate_bc, b_gate = sbuf(st, "gate_bc", [128, 1024], F32)
        spt, b_spt = sbuf(st, "spt", [128, NSMALL], F32)
        gA, b_gA = sbuf(st, "gA", [128, 8, 32], F32)
        gB, b_gB = sbuf(st, "gB", [128, 4, 64], F32)
        gS, b_gS = sbuf(st, "gS", [128, 2, 64], F32)
        lamt, b_lam = sbuf(st, "lamt", [128, 8], F32)
        decs, b_decs = sbuf(st, "decs", [128, 64], F32)
        DT, b_DT = sbuf(st, "DT", [128, 4, 128], F32)
        eps_t, b_eps = sbuf(st, "eps_t", [128, 1], F32)
        wstage = [sbuf(st, f"wstage{i}", [128, 8, 256], F32) for i in range(2)]
        state = {"ws": 0, "mm": 0, "st": 0, "ot": 0, "tb": 0, "scr": 0}

        psf = []
        psbv = []
        for i in range(8):
            t_ = st.enter_context(nc.psum_tensor(f"psf{i}", [128, 512], F32))
            b_ = Buf(f"psf{i}", True)
            psf.append((t_, b_))
            psbv.append((t_.bitcast(BF16), b_))
        pools = {"mm": [0, 1, 2, 3, 4, 5], "st": [1, 2, 3], "ot": [4, 5, 6, 7], "tb": [6, 7]}
        ppos = {"mm": 0, "st": 0, "ot": 0, "tb": 0}

        def set_pools(**kw):
            for k_, v_ in kw.items():
                pools[k_] = list(v_)
                ppos[k_] = 0

        def _rot(kind):
            lst = pools[kind]
            i = lst[ppos[kind] % len(lst)]
            ppos[kind] += 1
            return i

        def ps_mm():
            return psf[_rot("mm")]

        def ps_st():
            return psf[_rot("st")]

        def ps_ot():
            return psf[_rot("ot")]

        def ps_tb():
            return psbv[_rot("tb")]

        def interleave(gens, K):
            active = []
            it = iter(gens)
            more = True
            while True:
                while more and len(active) < K:
                    try:
                        active.append(next(it))
                    except StopIteration:
                        more = False
                if not active:
                    break
                for g_ in list(active):
                    try:
                        next(g_)
                    except StopIteration:
                        active.remove(g_)

        P.dma("c_rope", rope[:], rope_d[:, :, :], writes=[b_rope])
        P.dma("c_idb", ident_b[:], ident_b_d[:, :], writes=[b_idb])
        P.dma("c_idf", ident_f[:], ident_f_d[:, :], writes=[b_idf])
        P.dma("c_mb", maskB[:], maskB_d[:, :], writes=[b_maskB])
        P.dma("c_rc", retc[:], retc_d[:, :], writes=[b_retc])
        P.dma("c_km", kmask[:], kmask_d[:, :], writes=[b_kmask])
        P.dma("c_cs", cs[:], cT_in[:, :], writes=[b_cs])
        op("act", lambda e: e.activation(out=cs[:], in_=cs[:], func=AF.Silu), [b_cs], [b_cs])
        op("pool", lambda e: e.memset(ones_row[:], 1.0), [], [b_ones])
        op("pool", lambda e: e.memset(eps_t[:], EPS), [], [b_eps])

        def load_w(dst, b_dst, src_ap, ncols, col0=0):
            done = 0
            while done < ncols:
                n = min(256, ncols - done)
                i = state["ws"]
                state["ws"] ^= 1
                stg, b_stg = wstage[i]
                P.dma("ws%d" % i, stg[:, :, 0:n], src_ap[:, done:done + n].rearrange("(k p) n -> p k n", p=128), writes=[b_stg])
                cp("pool", dst[:, :, col0 + done:col0 + done + n], stg[:, :, 0:n], [b_stg], [b_dst])
                done += n

        def proj(t_cols, W, b_W, n, ps, b_ps, wcol0=0, pcol0=0):
            for k in range(8):
                op("pe", lambda e, k=k: e.matmul(ps[:, pcol0:pcol0 + n], lhsT=hT[:, k, t_cols], rhs=W[:, k, wcol0:wcol0 + n],
                                                start=(k == 0), stop=(k == 7)), [b_hT, b_W], [b_ps])

        def rstd_inplace(ss, b_ss, d):
            op("act", lambda e: e.activation(out=ss, in_=ss, func=AF.Ln, scale=1.0 / d, bias=eps_t[:, 0:1]), [b_ss, b_eps], [b_ss])
            op("act", lambda e: e.activation(out=ss, in_=ss, func=AF.Exp, scale=-0.5), [b_ss], [b_ss])

        def tcols(t):
            return slice(t * 128, (t + 1) * 128)

        def norm_rope(src3, b_src, G, d, gains, b_g, half, cos, sin, out3, b_out, scr):
            rot = 2 * half
            sq, b_sq = scr["sq"]
            xn, b_xn = scr["xn"]
            ssx, b_ssx = scr["ss"]
            xn3 = xn[:, 0:G * d].rearrange("p (g d) -> p g d", d=d)
            if gains is not None:
                sq3 = sq[:, 0:G * d].rearrange("p (g d) -> p g d", d=d)
                op("act", lambda e: e.activation(out=sq3, in_=src3, func=AF.Square), [b_src], [b_sq])
                yield
                op("dve", lambda e: e.reduce_sum(out=ssx[:, 0:G], in_=sq3, axis=AX.X), [b_sq], [b_ssx])
                yield
                rstd_inplace(ssx[:, 0:G], b_ssx, float(d))
                yield
                tt("dve", xn3, src3, ssx[:, 0:G].unsqueeze(2).to_broadcast([128, G, d]), ALU.mult, [b_src, b_ssx], [b_xn])
                yield
                tt("pool", xn3, xn3, gains, ALU.mult, [b_xn, b_g], [b_xn])
                yield
            else:
                cp("act", xn3, src3, [b_src], [b_xn])
                yield
            x1 = xn3[:, :, 0:half]
            x2 = xn3[:, :, half:rot]
            cb = cos.unsqueeze(1).to_broadcast([128, G, half])
            sb_ = sin.unsqueeze(1).to_broadcast([128, G, half])
            tv = []
            for i in range(4):
                tq, b_tq = scr["t%d" % i]
                tv.append((tq[:, 0:G * half].rearrange("p (g d) -> p g d", d=half), b_tq))
            tt("dve", tv[0][0], x1, cb, ALU.mult, [b_xn, b_rope], [tv[0][1]])
            tt("pool", tv[1][0], x2, sb_, ALU.mult, [b_xn, b_rope], [tv[1][1]])
            tt("pool", tv[2][0], x1, sb_, ALU.mult, [b_xn, b_rope], [tv[2][1]])
            tt("dve", tv[3][0], x2, cb, ALU.mult, [b_xn, b_rope], [tv[3][1]])
            if rot < d:
                cp("act", out3[:, :, rot:d], xn3[:, :, rot:d], [b_xn], [b_out])
            yield
            tt("dve", out3[:, :, 0:half], tv[0][0], tv[1][0], ALU.subtract, [tv[0][1], tv[1][1]], [b_out])
            tt("pool", out3[:, :, half:rot], tv[2][0], tv[3][0], ALU.add, [tv[2][1], tv[3][1]], [b_out])
            yield

        def make_scr(ph, tag, n):
            out = []
            for i in range(n):
                s = {}
                s["sq"] = sbuf(ph, f"{tag}sq{i}", [128, 256], F32)
                s["xn"] = sbuf(ph, f"{tag}xn{i}", [128, 256], F32)
                s["ss"] = sbuf(ph, f"{tag}ss{i}", [128, 8], F32)
                for j in range(4):
                    s["t%d" % j] = sbuf(ph, f"{tag}t{j}_{i}", [128, 96], F32)
                out.append(s)
            return out

        def gate_and_store(t, on2d, b_on, Wg, b_Wg, n, ocol, gsc, b_gsc, ofin, b_ofin):
            psg, b_psg = ps_mm()
            proj(tcols(t), Wg, b_Wg, n, psg, b_psg)
            yield
            op("act", lambda e: e.activation(out=gsc[:, 0:n], in_=psg[:, 0:n], func=AF.Silu), [b_psg], [b_gsc])
            yield
            tt("dve", ofin[:, 0:n], on2d, gsc[:, 0:n], ALU.mult, [b_on, b_gsc], [b_ofin])
            yield
            P.dma("ofin_" + b_ofin.name, o_scr[t * 128:(t + 1) * 128, ocol:ocol + n], ofin[:, 0:n], reads=[b_ofin], writes=[obufs[t]])
            yield

        xbufs = [Buf(f"x{t}") for t in range(NT)]
        obufs = [Buf(f"o{t}") for t in range(NT)]

        for l in range(L):
            lam_init = 0.8 - 0.6 * math.exp(-0.3 * l)
            xsrc = x_in if l == 0 else y_out
            P.dma("spt", spt[:], smallp[l:l + 1, :].partition_broadcast(128), writes=[b_spt])
            cp("dve", gA[:, 0:4, :], spt[:, SP_QNA:SP_QNA + 32].unsqueeze(1).to_broadcast([128, 4, 32]), [b_spt], [b_gA])
            cp("dve", gA[:, 4:8, :], spt[:, SP_KNA:SP_KNA + 32].unsqueeze(1).to_broadcast([128, 4, 32]), [b_spt], [b_gA])
            cp("dve", gB[:, 0:2, :], spt[:, SP_QNB:SP_QNB + 64].unsqueeze(1).to_broadcast([128, 2, 64]), [b_spt], [b_gB])
            cp("dve", gB[:, 2:4, :], spt[:, SP_KNB:SP_KNB + 64].unsqueeze(1).to_broadcast([128, 2, 64]), [b_spt], [b_gB])
            ts("dve", gS[:], spt[:, SP_SUB:SP_SUB + 64].unsqueeze(1).to_broadcast([128, 2, 64]), 1.0 - lam_init, ALU.mult, [b_spt], [b_gS])
            tt("dve", decs[:, 0:32], spt[:, SP_LQ1:SP_LQ1 + 32], spt[:, SP_LK1:SP_LK1 + 32], ALU.mult, [b_spt], [b_decs])
            op("dve", lambda e: e.reduce_sum(out=lamt[:, 0:1], in_=decs[:, 0:32], axis=AX.X), [b_decs], [b_lam])
            tt("dve", decs[:, 32:64], spt[:, SP_LQ2:SP_LQ2 + 32], spt[:, SP_LK2:SP_LK2 + 32], ALU.mult, [b_spt], [b_decs])
            op("dve", lambda e: e.reduce_sum(out=lamt[:, 1:2], in_=decs[:, 32:64], axis=AX.X), [b_decs], [b_lam])
            op("act", lambda e: e.activation(out=lamt[:, 2:4], in_=lamt[:, 0:2], func=AF.Exp), [b_lam], [b_lam])
            tt("dve", lamt[:, 4:5], lamt[:, 3:4], lamt[:, 2:3], ALU.subtract, [b_lam], [b_lam])
            ts("dve", lamt[:, 6:7], lamt[:, 4:5], -lam_init, ALU.add, [b_lam], [b_lam])
            op("act", lambda e: e.activation(out=decs[:, 0:8], in_=spt[:, SP_DEC:SP_DEC + 8], func=AF.Exp, scale=-1.0), [b_spt, b_decs], [b_decs])
            ts("dve", decs[:, 0:8], decs[:, 0:8], 1.0, ALU.add, [b_decs], [b_decs])
            op("act", lambda e: e.activation(out=decs[:, 0:8], in_=decs[:, 0:8], func=AF.Ln), [b_decs], [b_decs])
            ts("dve", decs[:, 0:8], decs[:, 0:8], -1.0, ALU.mult, [b_decs], [b_decs])
            for (o0, i0, rc) in ((8, 0, 513), (12, 4, 515), (16, 0, 512), (20, 4, 514)):
                op("act", lambda e, o0=o0, i0=i0, rc=rc: e.activation(out=decs[:, o0:o0 + 4], in_=decs[:, i0:i0 + 4], func=AF.Exp, scale=retc[:, rc:rc + 1]),
                   [b_decs, b_retc], [b_decs])
            op("act", lambda e: e.activation(out=decs[:, 24:32], in_=decs[:, 0:8], func=AF.Exp, scale=128.0), [b_decs], [b_decs])
            with ExitStack() as ph:
                dtmp, b_dtmp = sbuf(ph, "dtmp", [128, 128], F32)
                for h in range(4):
                    op("act", lambda e, h=h: e.activation(out=DT[:, h, :], in_=retc[:, 0:128], func=AF.Exp, scale=decs[:, h:h + 1]), [b_decs, b_retc], [b_DT])
                    tt("dve", DT[:, h, :], DT[:, h, :], retc[:, 128:256], ALU.mult, [b_DT, b_retc], [b_DT])
                    op("act", lambda e, h=h: e.activation(out=dtmp[:], in_=retc[:, 256:384], func=AF.Exp, scale=decs[:, 4 + h:5 + h]), [b_decs, b_retc], [b_dtmp])
                    tt("dve", dtmp[:], dtmp[:], retc[:, 384:512], ALU.mult, [b_dtmp, b_retc], [b_dtmp])
                    tt("dve", DT[:, h, :], DT[:, h, :], dtmp[:], ALU.add, [b_dtmp, b_DT], [b_DT])
                P.barrier()

            ph_norm = ExitStack()
            gs_bc, b_gs = sbuf(ph_norm, "gs_bc", [128, 1024], F32)
            shift_bc, b_shift = sbuf(ph_norm, "shift_bc", [128, 1024], F32)
            ph_ada = ExitStack()
            modrow, b_modrow = sbuf(ph_ada, "modrow", [1, 3072], F32)
            bada, b_bada = sbuf(ph_ada, "bada", [1, 3072], F32)
            P.dma("bada", bada[:], b_ada[l:l + 1, :], writes=[b_bada])
            for nt in range(12):
                i = state["ws"]
                state["ws"] ^= 1
                stg, b_stg = wstage[i]
                P.dma("ws%d" % i, stg[:, :, :], w_ada[l, :, nt * 256:(nt + 1) * 256].rearrange("(k p) n -> p k n", p=128), writes=[b_stg])
                ps, b_ps = ps_mm()
                for k in range(8):
                    op("pe", lambda e, k=k, stg=stg, ps=ps: e.matmul(ps[0:1, 0:256], lhsT=cs[:, k:k + 1], rhs=stg[:, k, :], start=(k == 0), stop=(k == 7)),
                       [b_cs, b_stg], [b_ps])
                tt("dve", modrow[0:1, nt * 256:(nt + 1) * 256], ps[0:1, 0:256], bada[0:1, nt * 256:(nt + 1) * 256], ALU.add, [b_ps, b_bada], [b_modrow])
            with ExitStack() as ph:
                g_bc, b_gbc = sbuf(ph, "g_bc", [128, 1024], F32)
                P.dma("gbc", g_bc[:], norm_g[l:l + 1, :].partition_broadcast(128), writes=[b_gbc])
                for j in range(6):
                    ps, b_ps = ps_mm()
                    op("pe", lambda e, ps=ps, j=j: e.matmul(ps[:, :], lhsT=ones_row[0:1, :], rhs=modrow[0:1, j * 512:(j + 1) * 512], start=True, stop=True),
                       [b_ones, b_modrow], [b_ps])
                    sl = slice((j % 2) * 512, (j % 2 + 1) * 512)
                    if j < 2:
                        cp("act", shift_bc[:, sl], ps[:, :], [b_ps], [b_shift])
                    elif j < 4:
                        op("dve", lambda e, ps=ps, sl=sl: e.scalar_tensor_tensor(out=gs_bc[:, sl], in0=ps[:, :], scalar=1.0, in1=g_bc[:, sl], op0=ALU.add, op1=ALU.mult),
                           [b_ps, b_gbc], [b_gs])
                    else:
                        cp("act", gate_bc[:, sl], ps[:, :], [b_ps], [b_gate])
                P.barrier()
            ph_ada.close()

            with ExitStack() as ph:
                KN = 3
                set_pools(mm=[0, 1, 2, 3, 4, 5], tb=[5, 6, 7])
                xt = [sbuf(ph, f"xt{i}", [128, 1024], F32) for i in range(KN)]
                junk, b_junk = sbuf(ph, "junk", [128, 1024], BF16)
                h1 = [sbuf(ph, f"h1_{i}", [128, 1024], F32) for i in range(KN)]
                hb = [sbuf(ph, f"hb{i}", [128, 1024], BF16) for i in range(KN)]
                ssn = [sbuf(ph, f"ssn{i}", [128, 1], F32) for i in range(KN)]

                def norm_body(t):
                    x_t, b_x = xt[t % KN]
                    h_t, b_h = h1[t % KN]
                    hb_t, b_hb = hb[t % KN]
                    ss, b_ss = ssn[t % KN]
                    P.dma("xt%d" % (t % KN), x_t[:], xsrc[t * 128:(t + 1) * 128, :], reads=[xbufs[t]], writes=[b_x])
                    yield
                    op("act", lambda e: e.activation(out=junk[:], in_=x_t[:], func=AF.Square, accum_out=ss[:]), [b_x], [b_junk, b_ss])
                    yield
                    rstd_inplace(ss[:], b_ss, 1024.0)
                    yield
                    op("dve", lambda e: e.scalar_tensor_tensor(out=h_t[:], in0=x_t[:], scalar=ss[:, 0:1], in1=gs_bc[:], op0=ALU.mult, op1=ALU.mult),
                       [b_x, b_ss, b_gs], [b_h])
                    yield
                    tt("pool", hb_t[:], h_t[:], shift_bc[:], ALU.add, [b_h, b_shift], [b_hb])
                    yield
                    pb, b_pb = ps_tb()
                    for k in range(8):
                        op("pe", lambda e, k=k: e.transpose(pb[:, k * 128:(k + 1) * 128], hb_t[:, k * 128:(k + 1) * 128], ident_b[:]),
                           [b_hb, b_idb], [b_pb])
                    yield
                    cp("act" if t % 2 else "dve", hT[:, :, tcols(t)], pb[:, :].rearrange("p (k n) -> p k n", n=128), [b_pb], [b_hT])
                    yield

                interleave((norm_body(t) for t in range(NT)), KN)
                P.barrier()
            ph_norm.close()
            if dbg == "hT":
                with ExitStack() as ph:
                    d32, b_d32 = sbuf(ph, "d32", [128, 8 * S], F32)
                    cp("dve", d32[:], hT[:].rearrange("p k s -> p (k s)"), [b_hT], [b_d32])
                    P.dma("dbg", dbg_hT[:, :], d32[:], reads=[b_d32], writes=[xbufs[0]])
                break

            for c in range(2 if "A" in stages else 0):
                with ExitStack() as ph:
                    Wqk, b_Wqk = sbuf(ph, "A_Wqk", [128, 8, 256], BF16)
                    Wv, b_Wv = sbuf(ph, "A_Wv", [128, 8, 128], BF16)
                    Wg, b_Wg = sbuf(ph, "A_Wg", [128, 8, 128], BF16)
                    wl = w_in[l]
                    load_w(Wqk, b_Wqk, wl[:, OFF_QA + 128 * c:OFF_QA + 128 * c + 128], 128, 0)
                    load_w(Wqk, b_Wqk, wl[:, OFF_KA + 128 * c:OFF_KA + 128 * c + 128], 128, 128)
                    load_w(Wv, b_Wv, wl[:, OFF_VA + 128 * c:OFF_VA + 128 * c + 128], 128)
                    load_w(Wg, b_Wg, wl[:, OFF_GA + 128 * c:OFF_GA + 128 * c + 128], 128)
                    qT, b_qT = sbuf(ph, "A_qT", [128, S], BF16)
                    k1p, b_k1p = sbuf(ph, "A_k1p", [128, S], BF16)
                    k2p, b_k2p = sbuf(ph, "A_k2p", [128, S], BF16)
                    vaug, b_vaug = sbuf(ph, "A_vaug", [128, NT, 2, 66], BF16)
                    if not DBG.get("A_nomemset"):
                        op("pool", lambda e: e.memset(vaug[:, :, :, 64:65], 1.0), [], [b_vaug])
                    ph2 = ExitStack()
                    KA = 3
                    set_pools(mm=[0, 1, 2, 3, 4], tb=[5, 6, 7])
                    scr = make_scr(ph2, "A", KA)
                    qkb = [sbuf(ph2, f"A_qkb{i}", [128, 256], BF16) for i in range(KA)]

                    def a_proj_body(t):
                        ps, b_ps = ps_mm()
                        proj(tcols(t), Wqk, b_Wqk, 256, ps, b_ps)
                        yield
                        qk_t, b_qk = qkb[t % KA]
                        yield from norm_rope(ps[:, 0:256].rearrange("p (g d) -> p g d", d=32), b_ps, 8, 32, gA[:], b_gA, 4,
                                             rope[:, t, 0:4], rope[:, t, 4:8], qk_t[:].rearrange("p (g d) -> p g d", d=32), b_qk, scr[t % KA])
                        pb, b_pb = ps_tb()
                        for j in range(2):
                            op("pe", lambda e, j=j: e.transpose(pb[:, j * 128:(j + 1) * 128], qk_t[:, j * 128:(j + 1) * 128], ident_b[:]),
                               [b_qk, b_idb], [b_pb])
                        yield
                        cp("act", qT[:, tcols(t)], pb[:, 0:128], [b_pb], [b_qT])
                        ts("dve", k1p[:, tcols(t)], pb[:, 128:256], kmask[:, 0:1], ALU.mult, [b_pb, b_kmask], [b_k1p])
                        ts("dve", k2p[:, tcols(t)], pb[:, 128:256], kmask[:, 1:2], ALU.mult, [b_pb, b_kmask], [b_k2p])
                        yield
                        psv, b_psv = ps_mm()
                        proj(tcols(t), Wv, b_Wv, 128, psv, b_psv)
                        yield
                        cp("act", vaug[:, t, :, 0:64], psv[:, 0:128].rearrange("p (h d) -> p h d", d=64), [b_psv], [b_vaug])
                        yield

                    interleave((a_proj_body(t) for t in range(NT)), KA)
                    P.barrier()
                    ph2.close()
                    set_pools(mm=[0], st=[1, 2, 3], ot=[4, 5, 6, 7])
                    Et = [sbuf(ph, f"A_E{i}", [128, 512], BF16) for i in range(4)]
                    OTs = [sbuf(ph, f"A_OTs{i}", [65, 2, 512], F32) for i in range(2)]
                    opre = [sbuf(ph, f"A_opre{i}", [128, 4, 2, 64], F32) for i in range(2)]
                    KE = 3
                    tA = [sbuf(ph, f"A_tA{i}", [128, 64], F32) for i in range(KE)]
                    tB = [sbuf(ph, f"A_tB{i}", [128, 64], F32) for i in range(KE)]
                    rden = [sbuf(ph, f"A_rden{i}", [128, 2], F32) for i in range(KE)]
                    osq = [sbuf(ph, f"A_osq{i}", [128, 128], F32) for i in range(KE)]
                    oss = [sbuf(ph, f"A_oss{i}", [128, 2], F32) for i in range(KE)]
                    onn = [sbuf(ph, f"A_on{i}", [128, 128], F32) for i in range(KE)]
                    gsc = [sbuf(ph, f"A_gsc{i}", [128, 128], F32) for i in range(KE)]
                    ofin = [sbuf(ph, f"A_ofin{i}", [128, 128], BF16) for i in range(KE)]
                    side = []
                    cnt_e = [0]

                    def a_epi(g, hl, ots, b_ots):
                        for t4 in range(4):
                            yield from a_epi_tile(g, hl, ots, b_ots, t4)

                    def a_epi_tile(g, hl, ots, b_ots, t4):
                        opre_, b_opre = opre[g % 2]
                        if True:
                            i_ = cnt_e[0] % KE
                            cnt_e[0] += 1
                            psT, b_psT = ps_mm()
                            for m in range(2):
                                op("pe", lambda e, m=m: e.transpose(psT[:, m * 65:(m + 1) * 65], ots[0:65, m, t4 * 128:(t4 + 1) * 128], ident_f[0:65, 0:65]),
                                   [b_ots, b_idf], [b_psT])
                            yield
                            rd, b_rd = rden[i_]
                            ta, b_ta = tA[i_]
                            tb_, b_tb = tB[i_]
                            op("dve", lambda e: e.reciprocal(out=rd[:], in_=psT[:, 0:130].rearrange("p (m e) -> p m e", e=65)[:, :, 64]), [b_psT], [b_rd])
                            yield
                            ts("dve", ta[:], psT[:, 0:64], rd[:, 0:1], ALU.mult, [b_psT, b_rd], [b_ta])
                            yield
                            op("act", lambda e: e.activation(out=tb_[:], in_=psT[:, 65:129], func=AF.Copy, scale=rd[:, 1:2]), [b_psT, b_rd], [b_tb])
                            yield
                            op("dve", lambda e: e.scalar_tensor_tensor(out=opre_[:, t4, hl, :], in0=tb_[:], scalar=lamt[:, 6:7], in1=ta[:],
                                                                       op0=ALU.mult, op1=ALU.add), [b_tb, b_ta, b_lam], [b_opre])
                            yield

                    def a_fin(g):
                        for t4 in range(4):
                            yield from a_fin_tile(g, t4)

                    def a_fin_tile(g, t4):
                        opre_, b_opre = opre[g % 2]
                        if True:
                            t = g * 4 + t4
                            i_ = cnt_e[0] % KE
                            cnt_e[0] += 1
                            sq_, b_sq = osq[i_]
                            ss_, b_ss = oss[i_]
                            on_, b_on = onn[i_]
                            o2 = opre_[:, t4, :, :]
                            tt("pool", sq_[:].rearrange("p (h d) -> p h d", d=64), o2, o2, ALU.mult, [b_opre], [b_sq])
                            yield
                            op("dve", lambda e: e.reduce_sum(out=ss_[:], in_=sq_[:].rearrange("p (h d) -> p h d", d=64), axis=AX.X), [b_sq], [b_ss])
                            yield
                            rstd_inplace(ss_[:], b_ss, 64.0)
                            yield
                            on3 = on_[:].rearrange("p (h d) -> p h d", d=64)
                            tt("dve", on3, o2, ss_[:].unsqueeze(2).to_broadcast([128, 2, 64]), ALU.mult, [b_opre, b_ss], [b_on])
                            yield
                            tt("pool", on3, on3, gS[:], ALU.mult, [b_on, b_gS], [b_on])
                            yield
                            yield from gate_and_store(t, on_[:], b_on, Wg, b_Wg, 128, OCOL_A + 128 * c, gsc[i_][0], gsc[i_][1], ofin[i_][0], ofin[i_][1])

                    def a_main():
                        LA = 2
                        items = [(g, hl, m, kt) for g in range(S // 512) for hl in range(2) for m in range(2) for kt in range(NT)]
                        pend = []
                        otmap = {}
                        for idx in range(len(items) + LA):
                            if idx < len(items):
                                g, hl, m, kt = items[idx]
                                base = 64 * hl
                                kp, b_kp = (k1p, b_k1p) if m == 0 else (k2p, b_k2p)
                                stp, b_st = ps_st()
                                op("pe", lambda e, stp=stp, kp=kp, kt=kt, base=base, g=g: e.matmul(
                                    stp[:, :], lhsT=kp[base:base + 64, tcols(kt)], rhs=qT[base:base + 64, g * 512:(g + 1) * 512], start=True, stop=True),
                                   [b_kp, b_qT], [b_st])
                                E, b_E = Et[idx % 4]
                                op("act", lambda e, E=E, stp=stp: e.activation(out=E[:], in_=stp[:], func=AF.Exp, scale=32 ** -0.5), [b_st], [b_E])
                                pend.append((g, hl, m, kt, E, b_E))
                            if idx >= LA:
                                g, hl, m, kt, E, b_E = pend.pop(0)
                                if kt == 0:
                                    otmap[(g, hl, m)] = ps_ot()
                                ot, b_ot = otmap[(g, hl, m)]
                                op("pe", lambda e, ot=ot, kt=kt, hl=hl, E=E: e.matmul(ot[0:65, :], lhsT=vaug[:, kt, hl, 0:65], rhs=E[:], start=(kt == 0), stop=(kt == NT - 1)),
                                   [b_vaug, b_E], [b_ot])
                                if kt == NT - 1:
                                    slot = (g * 2 + hl) % 2
                                    ots, b_ots = OTs[slot]
                                    while any(tg == slot for tg, _ in side):
                                        try:
                                            next(side[0][1])
                                        except StopIteration:
                                            side.pop(0)
                                    cp("dve", ots[0:65, m, :], ot[0:65, :], [b_ot], [b_ots])
                                    if m == 1:
                                        side.append((slot, a_epi(g, hl, ots, b_ots)))
                                        if hl == 1:
                                            side.append((None, a_fin(g)))
                            yield

                    for _ in a_main():
                        if side:
                            try:
                                next(side[0][1])
                            except StopIteration:
                                side.pop(0)
                    while side:
                        try:
                            next(side[0][1])
                        except StopIteration:
                            side.pop(0)
                    P.barrier()

            for c in range(3 if "B" in stages else 0):
                with ExitStack() as ph:
                    Wqk, b_Wqk = sbuf(ph, "B_Wqk", [128, 8, 256], BF16)
                    Wv, b_Wv = sbuf(ph, "B_Wv", [128, 8, 128], BF16)
                    Wg, b_Wg = sbuf(ph, "B_Wg", [128, 8, 128], BF16)
                    wl = w_in[l]
                    load_w(Wqk, b_Wqk, wl[:, OFF_QB + 128 * c:OFF_QB + 128 * c + 128], 128, 0)
                    load_w(Wqk, b_Wqk, wl[:, OFF_KB + 128 * c:OFF_KB + 128 * c + 128], 128, 128)
                    load_w(Wv, b_Wv, wl[:, OFF_VB + 128 * c:OFF_VB + 128 * c + 128], 128)
                    load_w(Wg, b_Wg, wl[:, OFF_GB + 128 * c:OFF_GB + 128 * c + 128], 128)
                    qT, b_qT = sbuf(ph, "B_qT", [128, S], BF16)
                    kT, b_kT = sbuf(ph, "B_kT", [128, S], BF16)
                    vB, b_vB = sbuf(ph, "B_vB", [128, 3, NT, 2, 66], BF16)
                    op("pool", lambda e: e.memset(vB[:, :, :, :, 64:65].rearrange("p a b c d -> p (a b c d)"), 1.0), [], [b_vB])
                    ph2 = ExitStack()
                    KB = 3
                    set_pools(mm=[0, 1, 2, 3, 4], tb=[5, 6, 7])
                    scr = make_scr(ph2, "B", KB)
                    qkb = [sbuf(ph2, f"B_qkb{i}", [128, 256], BF16) for i in range(KB)]

                    def b_proj_body(t):
                        ps, b_ps = ps_mm()
                        proj(tcols(t), Wqk, b_Wqk, 256, ps, b_ps)
                        yield
                        qk_t, b_qk = qkb[t % KB]
                        yield from norm_rope(ps[:, 0:256].rearrange("p (g d) -> p g d", d=64), b_ps, 4, 64, gB[:], b_gB, 8,
                                             rope[:, t, 8:16], rope[:, t, 16:24], qk_t[:].rearrange("p (g d) -> p g d", d=64), b_qk, scr[t % KB])
                        pb, b_pb = ps_tb()
                        for j in range(2):
                            op("pe", lambda e, j=j: e.transpose(pb[:, j * 128:(j + 1) * 128], qk_t[:, j * 128:(j + 1) * 128], ident_b[:]),
                               [b_qk, b_idb], [b_pb])
                        yield
                        cp("act", qT[:, tcols(t)], pb[:, 0:128], [b_pb], [b_qT])
                        cp("act", kT[:, tcols(t)], pb[:, 128:256], [b_pb], [b_kT])
                        yield

                    interleave((b_proj_body(t) for t in range(NT)), KB)
                    vi = 0
                    for gi, D in enumerate((1, 4, 16)):
                        ntl = S // D // 128
                        for r in range(D):
                            for j in range(ntl):
                                psv, b_psv = ps_mm()
                                c0 = r + D * 128 * j
                                proj(slice(c0, c0 + D * 127 + 1, D), Wv, b_Wv, 128, psv, b_psv)
                                cp("act" if vi % 2 else "dve", vB[:, gi, r * ntl + j, :, 0:64], psv[:, 0:128].rearrange("p (h d) -> p h d", d=64), [b_psv], [b_vB])
                                vi += 1
                    P.barrier()
                    ph2.close()
                    set_pools(mm=[0, 7], st=[1, 2, 3], ot=[4, 5, 6])
                    Et = [sbuf(ph, f"B_E{i}", [128, 384], BF16) for i in range(4)]
                    accs = [sbuf(ph, f"B_acc{i}", [65, 2048], F32) for i in range(2)]
                    ob, b_ob = sbuf(ph, "B_ob", [128, 16, 2, 64], F32)
                    KE = 3
                    rden = [sbuf(ph, f"B_rden{i}", [128, 1], F32) for i in range(KE)]
                    gsc = [sbuf(ph, f"B_gsc{i}", [128, 128], F32) for i in range(KE)]
                    ofin = [sbuf(ph, f"B_ofin{i}", [128, 128], BF16) for i in range(KE)]
                    side = []
                    cnt_e = [0]

                    def b_epi_tile(hl, acc, b_acc, t16):
                        i_ = cnt_e[0] % KE
                        cnt_e[0] += 1
                        psT, b_psT = ps_mm()
                        op("pe", lambda e: e.transpose(psT[:, 0:65], acc[0:65, t16 * 128:(t16 + 1) * 128], ident_f[0:65, 0:65]), [b_acc, b_idf], [b_psT])
                        yield
                        rd, b_rd = rden[i_]
                        op("dve", lambda e: e.reciprocal(out=rd[:], in_=psT[:, 64:65]), [b_psT], [b_rd])
                        yield
                        ts("dve", ob[:, t16, hl, :], psT[:, 0:64], rd[:, 0:1], ALU.mult, [b_psT, b_rd], [b_ob])
                        yield

                    def b_epi(hl, acc, b_acc):
                        for t16 in range(16):
                            yield from b_epi_tile(hl, acc, b_acc, t16)

                    def b_fin(u):
                        for t16 in range(16):
                            i_ = cnt_e[0] % KE
                            cnt_e[0] += 1
                            yield from gate_and_store(u * 16 + t16, ob[:, t16, :, :].rearrange("p h d -> p (h d)"), b_ob, Wg, b_Wg, 128, OCOL_B + 128 * c,
                                                      gsc[i_][0], gsc[i_][1], ofin[i_][0], ofin[i_][1])

                    def drain(cond):
                        while any(cond(tg) for tg, _ in side):
                            try:
                                next(side[0][1])
                            except StopIteration:
                                side.pop(0)

                    def b_main():
                        LA = 2
                        items = []
                        for u in range(S // 2048):
                            for hl in range(2):
                                for gi, D in enumerate((1, 4, 16)):
                                    per = 16 // D
                                    for r in range(D):
                                        for i in range(u * per, (u + 1) * per):
                                            items.append((u, hl, gi, D, r, i))
                        pend = []
                        for idx in range(len(items) + LA):
                            if idx < len(items):
                                u, hl, gi, D, r, i = items[idx]
                                base = 64 * hl
                                ntl = S // D // 128
                                qc0 = r + D * 128 * i
                                qcols = slice(qc0, qc0 + D * 127 + 1, D)
                                js = [j for j in (i - 1, i, i + 1) if 0 <= j < ntl]
                                stp, b_st = ps_st()
                                for j in js:
                                    jj = j - (i - 1)
                                    kc0 = r + D * 128 * j
                                    op("pe", lambda e, stp=stp, jj=jj, kc0=kc0, D=D, base=base, qcols=qcols: e.matmul(
                                        stp[:, jj * 128:(jj + 1) * 128], lhsT=kT[base:base + 64, slice(kc0, kc0 + D * 127 + 1, D)], rhs=qT[base:base + 64, qcols],
                                        start=True, stop=True), [b_kT, b_qT], [b_st])
                                lo = (js[0] - (i - 1)) * 128
                                hi = (js[-1] - (i - 1) + 1) * 128
                                E, b_E = Et[idx % 4]
                                op("act", lambda e, E=E, stp=stp, lo=lo, hi=hi: e.activation(out=E[:, lo:hi], in_=stp[:, lo:hi], func=AF.Exp, scale=0.125), [b_st], [b_E])
                                tt("pool", E[:, lo:hi], E[:, lo:hi], maskB[:, lo:hi], ALU.mult, [b_E, b_maskB], [b_E])
                                pend.append((u, hl, gi, D, r, i, js, E, b_E, qc0, idx))
                            if idx >= LA:
                                u, hl, gi, D, r, i, js, E, b_E, qc0, idx0 = pend.pop(0)
                                ntl = S // D // 128
                                slot = (u * 2 + hl) % 2
                                acc, b_acc = accs[slot]
                                first_item = (gi == 0 and r == 0 and i == u * 16)
                                if first_item:
                                    drain(lambda tg: tg == slot)
                                ot, b_ot = ps_ot()
                                for j in js:
                                    jj = j - (i - 1)
                                    op("pe", lambda e, ot=ot, gi=gi, tile=r * ntl + j, hl=hl, E=E, jj=jj, first=(j == js[0]), last=(j == js[-1]): e.matmul(
                                        ot[0:65, 0:128], lhsT=vB[:, gi, tile, hl, 0:65], rhs=E[:, jj * 128:(jj + 1) * 128], start=first, stop=last),
                                       [b_vB, b_E], [b_ot])
                                a0 = qc0 - 2048 * u
                                acc_ap = acc[0:65, a0:a0 + D * 127 + 1:D]
                                if gi == 0:
                                    cp("dve", acc_ap, ot[0:65, 0:128], [b_ot], [b_acc])
                                else:
                                    tt("dve", acc_ap, ot[0:65, 0:128], acc_ap, ALU.add, [b_ot, b_acc], [b_acc])
                                last_item = (gi == 2 and r == 15 and i == u)
                                if last_item:
                                    side.append((slot, b_epi(hl, acc, b_acc)))
                                    if hl == 1:
                                        side.append((None, b_fin(u)))
                            yield

                    for _ in b_main():
                        if side:
                            try:
                                next(side[0][1])
                            except StopIteration:
                                side.pop(0)
                    drain(lambda tg: True)
                    P.barrier()

            for c in range(2 if "C" in stages else 0):
                with ExitStack() as ph:
                    Wqk, b_Wqk = sbuf(ph, "C_Wqk", [128, 8, 192], BF16)
                    Wv, b_Wv = sbuf(ph, "C_Wv", [128, 8, 192], BF16)
                    Wg, b_Wg = sbuf(ph, "C_Wg", [128, 8, 192], BF16)
                    wl = w_in[l]
                    load_w(Wqk, b_Wqk, wl[:, OFF_QC + 96 * c:OFF_QC + 96 * c + 96], 96, 0)
                    load_w(Wqk, b_Wqk, wl[:, OFF_KC + 96 * c:OFF_KC + 96 * c + 96], 96, 96)
                    load_w(Wv, b_Wv, wl[:, OFF_VC + 192 * c:OFF_VC + 192 * c + 192], 192)
                    load_w(Wg, b_Wg, wl[:, OFF_GC + 192 * c:OFF_GC + 192 * c + 192], 192)
                    qkT, b_qkT = sbuf(ph, "C_qkT", [64, 4, S], BF16)
                    vC, b_vC = sbuf(ph, "C_vC", [128, NT, 2, 96], BF16)
                    Rst, b_Rst = sbuf(ph, "C_Rst", [64, NT, 2, 2, 96], BF16)
                    ph_mid = ExitStack()
                    kfb, b_kfb = sbuf(ph_mid, "C_kfb", [128, NT, 2, 2, 48], BF16)
                    Rs = [sbuf(ph_mid, f"C_R{i}", [64, 2, 96], F32) for i in range(2)]
                    cdt = [sbuf(ph_mid, f"C_cd{i}", [64, 2, 96], F32) for i in range(2)]
                    ph2 = ExitStack()
                    KC = 2
                    set_pools(mm=[0, 1, 2, 3, 4], tb=[5, 6, 7])
                    scr = make_scr(ph2, "C", KC)
                    qkr = [sbuf(ph2, f"C_qkr{i}", [128, 4, 48], F32) for i in range(KC)]
                    qkp = [sbuf(ph2, f"C_qkp{i}", [128, 4, 64], BF16) for i in range(KC)]
                    for i in range(KC):
                        op("pool", lambda e, i=i: e.memset(qkp[i][0][:].rearrange("p g d -> p (g d)"), 0.0), [], [qkp[i][1]])

                    def c_proj_body(t):
                        ps, b_ps = ps_mm()
                        proj(tcols(t), Wqk, b_Wqk, 192, ps, b_ps)
                        yield
                        qr, b_qr = qkr[t % KC]
                        qp, b_qp = qkp[t % KC]
                        yield from norm_rope(ps[:, 0:192].rearrange("p (g d) -> p g d", d=48), b_ps, 4, 48, None, None, 24,
                                             rope[:, t, 24:48], rope[:, t, 48:72], qr[:], b_qr, scr[t % KC])
                        ts("pool", qr[:, 2:4, :], qr[:, 2:4, :], 48 ** -0.5, ALU.mult, [b_qr], [b_qr])
                        yield
                        cp("act", qp[:, :, 0:48], qr[:], [b_qr], [b_qp])
                        tt("dve", kfb[:, t, 0, :, :], qr[:, 2:4, :], decs[:, 8 + 2 * c:10 + 2 * c].unsqueeze(2).to_broadcast([128, 2, 48]), ALU.mult, [b_qr, b_decs], [b_kfb])
                        tt("pool", kfb[:, t, 1, :, :], qr[:, 2:4, :], decs[:, 12 + 2 * c:14 + 2 * c].unsqueeze(2).to_broadcast([128, 2, 48]), ALU.mult, [b_qr, b_decs], [b_kfb])
                        yield
                        pb, b_pb = ps_tb()
                        for s_ in range(4):
                            op("pe", lambda e, s_=s_: e.transpose(pb[0:64, s_ * 128:(s_ + 1) * 128], qp[:, s_, :], ident_b[:]), [b_qp, b_idb], [b_pb])
                        yield
                        cp("dve", qkT[:, :, tcols(t)], pb[0:64, 0:512].rearrange("p (g n) -> p g n", n=128), [b_pb], [b_qkT])
                        yield
                        psv, b_psv = ps_mm()
                        proj(tcols(t), Wv, b_Wv, 192, psv, b_psv)
                        yield
                        cp("act", vC[:, t, :, :], psv[:, 0:192].rearrange("p (h d) -> p h d", d=96), [b_psv], [b_vC])
                        yield

                    interleave((c_proj_body(t) for t in range(NT)), KC)
                    P.barrier()
                    ph2.close()
                    set_pools(mm=[0, 1, 2, 3], st=[1, 2, 3], ot=[4, 5, 6, 7])

                    def c_scan(d_):
                        R_, b_R = Rs[d_]
                        cd_, b_cd = cdt[d_]
                        op("pool", lambda e: e.memset(R_[:].rearrange("p h d -> p (h d)"), 0.0), [], [b_R])
                        cp("dve", cd_[0:48, :, :], decs[0:48, 24 + 4 * d_ + 2 * c:26 + 4 * d_ + 2 * c].unsqueeze(2).to_broadcast([48, 2, 96]), [b_decs], [b_cd])
                        yield
                        order = range(NT) if d_ == 0 else range(NT - 1, -1, -1)
                        for n in order:
                            psU, b_psU = ps_mm()
                            for hl in range(2):
                                op("pe", lambda e, psU=psU, n=n, hl=hl: e.matmul(psU[0:48, hl * 96:(hl + 1) * 96], lhsT=kfb[:, n, d_, hl, :], rhs=vC[:, n, hl, :], start=True, stop=True),
                                   [b_kfb, b_vC], [b_psU])
                            cp("dve", Rst[0:48, n, d_, :, :], R_[0:48, :, :], [b_R], [b_Rst])
                            yield
                            tt("dve", R_[0:48, :, :], R_[0:48, :, :], cd_[0:48, :, :], ALU.mult, [b_R, b_cd], [b_R])
                            yield
                            tt("dve", R_[0:48, :, :], psU[0:48, 0:192].rearrange("p (h d) -> p h d", d=96), R_[0:48, :, :], ALU.add, [b_psU, b_R], [b_R])
                            yield

                    interleave((c_scan(d_) for d_ in range(2)), 2)
                    P.barrier()
                    ph_mid.close()
                    set_pools(mm=[6, 7], st=[0, 1, 2], ot=[3, 4, 5])
                    KO = 2
                    att = [sbuf(ph, f"C_att{i}", [128, 128], BF16) for i in range(2 * KO)]
                    oc = [sbuf(ph, f"C_oc{i}", [128, 2, 96], F32) for i in range(KO)]
                    osq = [sbuf(ph, f"C_osq{i}", [128, 2, 96], F32) for i in range(KO)]
                    oss = [sbuf(ph, f"C_oss{i}", [128, 2], F32) for i in range(KO)]
                    gsc = [sbuf(ph, f"C_gsc{i}", [128, 192], F32) for i in range(KO)]
                    ofin = [sbuf(ph, f"C_ofin{i}", [128, 192], BF16) for i in range(KO)]

                    def c_out_head(n, hl, oc_, b_oc):
                        head = 2 * c + hl
                        stp, b_st = ps_st()
                        op("pe", lambda e: e.matmul(stp[:, 0:128], lhsT=qkT[0:48, 2 + hl, tcols(n)], rhs=qkT[0:48, hl, tcols(n)], start=True, stop=True),
                           [b_qkT], [b_st])
                        yield
                        at_, b_at = att[(n % KO) * 2 + hl]
                        tt("dve", at_[:], stp[:, 0:128], DT[:, head, :], ALU.mult, [b_st, b_DT], [b_at])
                        yield
                        psO, b_psO = ps_ot()
                        op("pe", lambda e: e.matmul(psO[:, 0:96], lhsT=at_[:], rhs=vC[:, n, hl, :], start=True, stop=True), [b_at, b_vC], [b_psO])
                        for d_ in range(2):
                            op("pe", lambda e, d_=d_: e.matmul(psO[:, 96 * (d_ + 1):96 * (d_ + 2)], lhsT=qkT[0:48, hl, tcols(n)], rhs=Rst[0:48, n, d_, hl, :],
                                                              start=True, stop=True), [b_qkT, b_Rst], [b_psO])
                        yield
                        cp("act", oc_[:, hl, :], psO[:, 0:96], [b_psO], [b_oc])
                        yield
                        for d_ in range(2):
                            op("dve", lambda e, d_=d_: e.scalar_tensor_tensor(
                                out=oc_[:, hl, :], in0=psO[:, 96 * (d_ + 1):96 * (d_ + 2)], scalar=decs[:, 16 + 4 * d_ + head:17 + 4 * d_ + head], in1=oc_[:, hl, :],
                                op0=ALU.mult, op1=ALU.add), [b_psO, b_decs, b_oc], [b_oc])
                            yield

                    def c_out_body(n):
                        oc_, b_oc = oc[n % KO]
                        for hl in range(2):
                            yield from c_out_head(n, hl, oc_, b_oc)
                        sq_, b_sq = osq[n % KO]
                        ss_, b_ss = oss[n % KO]
                        tt("pool", sq_[:], oc_[:], oc_[:], ALU.mult, [b_oc], [b_sq])
                        yield
                        op("dve", lambda e: e.reduce_sum(out=ss_[:], in_=sq_[:], axis=AX.X), [b_sq], [b_ss])
                        yield
                        rstd_inplace(ss_[:], b_ss, 96.0)
                        yield
                        tt("dve", sq_[:], oc_[:], ss_[:].unsqueeze(2).to_broadcast([128, 2, 96]), ALU.mult, [b_oc, b_ss], [b_sq])
                        yield
                        tt("pool", sq_[:], sq_[:], spt[:, SP_GNC:SP_GNC + 96].unsqueeze(1).to_broadcast([128, 2, 96]), ALU.mult, [b_sq, b_spt], [b_sq])
                        yield
                        yield from gate_and_store(n, sq_[:].rearrange("p h d -> p (h d)"), b_sq, Wg, b_Wg, 192, OCOL_C + 192 * c,
                                                  gsc[n % KO][0], gsc[n % KO][1], ofin[n % KO][0], ofin[n % KO][1])

                    interleave((c_out_body(n) for n in range(NT)), KO)
                    P.barrier()

            if dbg == "o":
                with ExitStack() as ph:
                    ob16, b_ob16 = sbuf(ph, "dbg_ob", [128, 1024], BF16)
                    o32, b_o32 = sbuf(ph, "dbg_o32", [128, 1024], F32)
                    for t in range(NT):
                        P.dma("dbg_l", ob16[:], o_scr[t * 128:(t + 1) * 128, :], reads=[obufs[t]], writes=[b_ob16])
                        cp("dve", o32[:], ob16[:], [b_ob16], [b_o32])
                        P.dma("dbg_s", dbg_o[t * 128:(t + 1) * 128, :], o32[:], reads=[b_o32], writes=[xbufs[t]])
                break

            if "O" in stages:
                with ExitStack() as ph:
                    Wout, b_Wout = sbuf(ph, "Wout", [128, 8, 1024], BF16)
                    load_w(Wout, b_Wout, w_out[l], 1024)
                    KQ = 3
                    set_pools(mm=[0, 1, 2, 3, 4], tb=[5, 6, 7])
                    ott = [sbuf(ph, f"O_ot{i}", [128, 1024], BF16) for i in range(KQ)]
                    oTt = [sbuf(ph, f"O_oT{i}", [128, 8, 128], BF16) for i in range(KQ)]
                    xt = [sbuf(ph, f"O_xt{i}", [128, 1024], F32) for i in range(KQ)]
                    tm = [sbuf(ph, f"O_tm{i}", [128, 1024], F32) for i in range(KQ)]

                    def o_half(oT_t, b_oT, tm_t, b_tm, nh):
                        ps, b_ps = ps_mm()
                        for k in range(8):
                            op("pe", lambda e, k=k: e.matmul(ps[:, :], lhsT=oT_t[:, k, :], rhs=Wout[:, k, nh * 512:(nh + 1) * 512], start=(k == 0), stop=(k == 7)),
                               [b_oT, b_Wout], [b_ps])
                        yield
                        sl = slice(nh * 512, (nh + 1) * 512)
                        tt("dve", tm_t[:, sl], ps[:, :], gate_bc[:, sl], ALU.mult, [b_ps, b_gate], [b_tm])
                        yield

                    def o_body(t):
                        o_t, b_o = ott[t % KQ]
                        oT_t, b_oT = oTt[t % KQ]
                        x_t, b_x = xt[t % KQ]
                        tm_t, b_tm = tm[t % KQ]
                        P.dma("O_ot%d" % (t % KQ), o_t[:], o_scr[t * 128:(t + 1) * 128, :], reads=[obufs[t]], writes=[b_o])
                        P.dma("O_xt%d" % (t % KQ), x_t[:], xsrc[t * 128:(t + 1) * 128, :], reads=[xbufs[t]], writes=[b_x])
                        yield
                        pb, b_pb = ps_tb()
                        for k in range(8):
                            op("pe", lambda e, k=k: e.transpose(pb[:, k * 128:(k + 1) * 128], o_t[:, k * 128:(k + 1) * 128], ident_b[:]), [b_o, b_idb], [b_pb])
                        yield
                        cp("act", oT_t[:], pb[:, :].rearrange("p (k n) -> p k n", n=128), [b_pb], [b_oT])
                        yield
                        for nh in range(2):
                            yield from o_half(oT_t, b_oT, tm_t, b_tm, nh)
                        tt("pool", tm_t[:], tm_t[:], x_t[:], ALU.add, [b_tm, b_x], [b_tm])
                        yield
                        P.dma("O_st%d" % (t % KQ), y_out[t * 128:(t + 1) * 128, :], tm_t[:], reads=[b_tm], writes=[xbufs[t]])
                        yield

                    interleave((o_body(t) for t in range(NT)), KQ)
                    P.barrier()

        P.barrier()
        P.emit()
    return nc, P


def host_constants(S):
    NT = S // 128
    pos = np.arange(S, dtype=np.float32)

    def tab(theta, rot):
        inv = (np.float32(theta) ** (-np.arange(0, rot, 2, dtype=np.float32) / np.float32(rot))).astype(np.float32)
        ang = (pos[:, None] * inv[None, :]).astype(np.float32)
        return np.cos(ang).astype(np.float32), np.sin(ang).astype(np.float32)

    cA, sA = tab(ROT_THETA, 8)
    cB, sB = tab(ROT_THETA, 16)
    cC, sC = tab(RET_THETA, 48)
    rope = np.concatenate([cA, sA, cB, sB, cC, sC], axis=1)
    rope = np.ascontiguousarray(rope.reshape(NT, 128, 72).transpose(1, 0, 2)).astype(np.float32)
    a = np.arange(128)[:, None]
    b = np.arange(128)[None, :]
    maskB = np.concatenate([(a - b >= 64), (np.abs(a - b) <= 64), (b - a >= 64)], axis=1).astype(np.float32).astype(ml_dtypes.bfloat16)
    j = a
    i = b
    retc = np.zeros((128, 516), np.float32)
    retc[:, 0:128] = np.maximum(i - j, 0)
    retc[:, 128:256] = (i >= j)
    retc[:, 256:384] = np.maximum(j - i, 0)
    retc[:, 384:512] = (j > i)
    p = np.arange(128, dtype=np.float32)
    retc[:, 512] = p + 1
    retc[:, 513] = 127 - p
    retc[:, 514] = 128 - p
    retc[:, 515] = p
    kmask = np.zeros((128, 2), np.float32)
    kmask[:, 0] = ((np.arange(128) % 64) < 32)
    kmask[:, 1] = ((np.arange(128) % 64) >= 32)
    return {
        "rope": rope, "maskB": maskB, "retc": retc, "kmask": kmask,
        "ident_b": np.eye(128, dtype=np.float32).astype(ml_dtypes.bfloat16),
        "ident_f": np.eye(128, dtype=np.float32),
    }


def pack_small(inp, L):
    parts = [inp["qn_a"], inp["kn_a"], inp["lambda_q1"], inp["lambda_k1"], inp["lambda_q2"], inp["lambda_k2"],
             inp["subln_a"], inp["qn_b"], inp["kn_b"], np.asarray(inp["ret_decay"]).reshape(L, 8), inp["gn_c"]]
    return np.ascontiguousarray(np.concatenate([np.asarray(p_, np.float32).reshape(L, -1) for p_ in parts], axis=1))


def make_in_maps(inp, S, L, B):
    consts = host_constants(S)
    f = lambda k: np.ascontiguousarray(np.asarray(inp[k], np.float32))
    shared = {
        "w_ada": f("w_ada")[:L], "b_ada": f("b_ada")[:L], "norm_g": f("norm_g")[:L], "w_in": f("w_in")[:L], "w_out": f("w_out")[:L],
        "smallp": pack_small({k: np.asarray(v)[:L] for k, v in inp.items() if k not in ("x", "c")}, L),
    }
    shared.update(consts)
    maps = []
    x = f("x")
    c = f("c")
    for b in range(B):
        m = dict(shared)
        m["x"] = np.ascontiguousarray(x[b])
        m["cT"] = np.ascontiguousarray(c[b].reshape(8, 128).T)
        maps.append(m)
    return maps


_CACHE = {}


def kernel(**inputs):
    x = np.asarray(inputs["x"])
    B, S, _ = x.shape
    L = np.asarray(inputs["w_in"]).shape[0]
    key = (S, L)
    if key not in _CACHE:
        _CACHE[key] = build_program(S, L)[0]
    nc = _CACHE[key]
    maps = make_in_maps(inputs, S, L, B)
    in_maps = [maps[i % B] for i in range(8)]
    res = run_bass_kernel_spmd(nc, in_maps, core_ids=list(range(8)))
    out = np.stack([np.asarray(res.results[b]["y"], np.float32) for b in range(B)], axis=0)
    return out
```

```python
import math
from contextlib import ExitStack
import numpy as np
import ml_dtypes
import concourse.bass as bass
import concourse.mybir as mybir
from concourse.bass_utils import run_bass_kernel_spmd

F32 = mybir.dt.float32
BF16 = mybir.dt.bfloat16
ALU = mybir.AluOpType
AF = mybir.ActivationFunctionType
AX = mybir.AxisListType

D_MODEL = 1024
IN_COLS = 3712
EPS = 1e-6
ROT_THETA = 500000.0
RET_THETA = 10000.0
NSMALL = 488
SP_QNA, SP_KNA, SP_LQ1, SP_LK1, SP_LQ2, SP_LK2 = 0, 32, 64, 96, 128, 160
SP_SUB, SP_QNB, SP_KNB, SP_DEC, SP_GNC = 192, 256, 320, 384, 392
OFF_QA, OFF_KA, OFF_VA, OFF_GA = 0, 256, 512, 768
OFF_QB, OFF_KB, OFF_VB, OFF_GB = 1024, 1408, 1792, 2176
OFF_QC, OFF_KC, OFF_VC, OFF_GC = 2560, 2752, 2944, 3328
OCOL_A, OCOL_B, OCOL_C = 0, 256, 640


class Buf:
    __slots__ = ("name", "w", "r", "excl")

    def __init__(self, name, excl=False):
        self.name = name
        self.w = None
        self.r = {}
        self.excl = excl


class Prog:
    ENG = ("pe", "act", "dve", "pool", "sp")

    def __init__(self, nc, stack):
        self.nc = nc
        self.stack = stack
        self.q = {e: [] for e in self.ENG}
        self.cnt = {e: 0 for e in self.ENG}
        self.seen = {e: {} for e in self.ENG}
        self.sems = {}
        self.dcnt = {}
        for e in self.ENG:
            self.sems[e] = stack.enter_context(nc.semaphore("s_" + e))
        self.ninst = 0
        self.desc = {}
        self.total = 0

    def dsem(self, name):
        if name not in self.sems:
            self.sems[name] = self.stack.enter_context(self.nc.semaphore("d_" + name))
            self.dcnt[name] = 0
        return name

    def _wait(self, eng, k, v):
        if k == eng and eng == "pe":
            return
        if self.seen[eng].get(k, 0) < v:
            self.seen[eng][k] = v
            sem = self.sems[k]
            self.ninst += 1
            self.desc.setdefault(eng, []).append("wait %s>=%d" % (k, v))
            self.q[eng].append(lambda e, sem=sem, v=v: e.wait_ge(sem, v))

    def _deps(self, eng, reads, writes):
        need = {}
        for b in reads:
            if b.w is not None:
                k, v = b.w
                if need.get(k, 0) < v:
                    need[k] = v
            if b.excl:
                for k, v in b.r.items():
                    if k != eng and need.get(k, 0) < v:
                        need[k] = v
        for b in writes:
            if b.w is not None:
                k, v = b.w
                if need.get(k, 0) < v:
                    need[k] = v
            for k, v in b.r.items():
                if need.get(k, 0) < v:
                    need[k] = v
        for k, v in need.items():
            self._wait(eng, k, v)

    def op(self, eng, fn, reads=(), writes=()):
        self.total = getattr(self, "total", 0) + 1
        if self.total > DBG.get("cut", 10 ** 9):
            return
        self._deps(eng, reads, writes)
        if DBG.get("serial") and getattr(self, "last", None):
            self._wait(eng, *self.last)
        self.cnt[eng] += 1
        c = self.cnt[eng]
        self.last = (eng, c)
        sem = self.sems[eng]
        self.ninst += 1
        self.desc.setdefault(eng, []).append("op#%d (%s=%d)" % (self.total, eng, c))
        self.q[eng].append(lambda e, fn=fn, sem=sem: fn(e).then_inc(sem, 1))
        for b in writes:
            b.w = (eng, c)
            b.r = {}
        for b in reads:
            if b.w != (eng, c):
                b.r[eng] = c

    def dma(self, semname, out, in_, reads=(), writes=(), qeng="sp", **kw):
        self.total = getattr(self, "total", 0) + 1
        if self.total > DBG.get("cut", 10 ** 9):
            return
        self.dsem(semname)
        self._deps(qeng, reads, writes)
        if DBG.get("serial") and getattr(self, "last", None):
            self._wait(qeng, *self.last)
        self.dcnt[semname] += 16
        v = self.dcnt[semname]
        self.last = (semname, v)
        sem = self.sems[semname]
        self.ninst += 1
        self.desc.setdefault(qeng, []).append("dma#%d (%s=%d)" % (self.total, semname, v))
        self.q[qeng].append(
            lambda e, out=out, in_=in_, sem=sem, kw=kw: e.dma_start(out=out, in_=in_, **kw).then_inc(sem, 16))
        for b in writes:
            b.w = (semname, v)
            b.r = {}
        for b in reads:
            b.r[semname] = v

    def barrier(self):
        for e in self.ENG:
            for k in list(self.sems.keys()):
                v = self.cnt[k] if k in self.cnt else self.dcnt[k]
                if v > 0 and k != e:
                    self._wait(e, k, v)
            if e != "pe" and self.cnt[e] > 0:
                self._wait(e, e, self.cnt[e])

    def emit(self):
        nc = self.nc
        with nc.Block() as block:
            @block.tensor
            def _(e):
                for t in self.q["pe"]:
                    t(e)

            @block.scalar
            def _(e):
                for t in self.q["act"]:
                    t(e)

            @block.vector
            def _(e):
                for t in self.q["dve"]:
                    t(e)

            @block.gpsimd
            def _(e):
                for t in self.q["pool"]:
                    t(e)

            @block.sync
            def _(e):
                for t in self.q["sp"]:
                    t(e)


DBG = {}


def build_program(S, L, stages=("A", "B", "C", "O"), dbg=None):
    NT = S // 128
    nc = bass.Bass("TRN2", target_bir_lowering=False)
    dr = lambda name, shape, dt, kind="ExternalInput": nc.dram_tensor(name, shape, dt, kind=kind).ap()
    x_in = dr("x", [S, D_MODEL], F32)
    cT_in = dr("cT", [128, 8], F32)
    w_ada = dr("w_ada", [L, D_MODEL, 3 * D_MODEL], F32)
    b_ada = dr("b_ada", [L, 3 * D_MODEL], F32)
    norm_g = dr("norm_g", [L, D_MODEL], F32)
    w_in = dr("w_in", [L, D_MODEL, IN_COLS], F32)
    w_out = dr("w_out", [L, D_MODEL, D_MODEL], F32)
    smallp = dr("smallp", [L, NSMALL], F32)
    ident_b_d = dr("ident_b", [128, 128], BF16)
    ident_f_d = dr("ident_f", [128, 128], F32)
    rope_d = dr("rope", [128, NT, 72], F32)
    maskB_d = dr("maskB", [128, 384], BF16)
    retc_d = dr("retc", [128, 516], F32)
    kmask_d = dr("kmask", [128, 4], F32)
    y_out = dr("y", [S, D_MODEL], F32, kind="ExternalOutput")
    o_scr = dr("o_scr", [S, D_MODEL], BF16, kind="ExternalOutput")
    dbg_hT = dr("dbg_hT", [128, 8 * S], F32, kind="ExternalOutput") if dbg == "hT" else None
    dbg_o = dr("dbg_o", [S, D_MODEL], F32, kind="ExternalOutput") if dbg == "o" else None

    with ExitStack() as st:
        P = Prog(nc, st)

        uid = [0]

        def sbuf(stack, name, shape, dt):
            uid[0] += 1
            t = stack.enter_context(nc.sbuf_tensor("sb%d_%s" % (uid[0], name), shape, dt))
            return t, Buf(name)

        def op(eng, fn, reads=(), writes=()):
            if eng == "pool" and DBG.get("nopool"):
                eng = "dve"
            P.op(eng, fn, reads, writes)

        def cp(eng, out, in_, reads, writes):
            if eng == "act":
                op("act", lambda e: e.copy(out=out, in_=in_), reads, writes)
            else:
                op(eng, lambda e: e.tensor_copy(out=out, in_=in_), reads, writes)

        def tt(eng, out, in0, in1, alu, reads, writes):
            op(eng, lambda e: e.tensor_tensor(out=out, in0=in0, in1=in1, op=alu), reads, writes)

        def ts(eng, out, in0, s1, alu, reads, writes):
            op(eng, lambda e: e.tensor_scalar(out=out, in0=in0, scalar1=s1, scalar2=None, op0=alu), reads, writes)

        hT, b_hT = sbuf(st, "hT", [128, 8, S], BF16)
        rope, b_rope = sbuf(st, "rope", [128, NT, 72], F32)
        ident_b, b_idb = sbuf(st, "ident_b", [128, 128], BF16)
        ident_f, b_idf = sbuf(st, "ident_f", [128, 128], F32)
        maskB, b_maskB = sbuf(st, "maskB", [128, 384], BF16)
        retc, b_retc = sbuf(st, "retc", [128, 516], F32)
        kmask, b_kmask = sbuf(st, "kmask", [128, 4], F32)
        cs, b_cs = sbuf(st, "cs", [128, 8], F32)
        ones_row, b_ones = sbuf(st, "ones_row", [1, 128], F32)
        gate_bc, b_gate = sbuf(st, "gate_bc", [128, 1024], F32)
        spt, b_spt = sbuf(st, "spt", [128, NSMALL], F32)
        gA, b_gA = sbuf(st, "gA", [128, 8, 32], F32)
        gB, b_gB = sbuf(st, "gB", [128, 4, 64], F32)
        gS, b_gS = sbuf(st, "gS", [128, 2, 64], F32)
        lamt, b_lam = sbuf(st, "lamt", [128, 8], F32)
        decs, b_decs = sbuf(st, "decs", [128, 64], F32)
        DT, b_DT = sbuf(st, "DT", [128, 4, 128], F32)
        eps_t, b_eps = sbuf(st, "eps_t", [128, 1], F32)
        wstage = [sbuf(st, f"wstage{i}", [128, 8, 256], F32) for i in range(2)]
        state = {"ws": 0, "mm": 0, "st": 0, "ot": 0, "tb": 0, "scr": 0}

        psf = []
        psbv = []
        for i in range(8):
            t_ = st.enter_context(nc.psum_tensor(f"psf{i}", [128, 512], F32))
            b_ = Buf(f"psf{i}", True)
            psf.append((t_, b_))
            psbv.append((t_.bitcast(BF16), b_))
        pools = {"mm": [0, 1, 2, 3, 4, 5], "st": [1, 2, 3], "ot": [4, 5, 6, 7], "tb": [6, 7]}
        ppos = {"mm": 0, "st": 0, "ot": 0, "tb": 0}

        def set_pools(**kw):
            for k_, v_ in kw.items():
                pools[k_] = list(v_)
                ppos[k_] = 0

        def _rot(kind):
            lst = pools[kind]
            i = lst[ppos[kind] % len(lst)]
            ppos[kind] += 1
            return i

        def ps_mm():
            return psf[_rot("mm")]

        def ps_st():
            return psf[_rot("st")]

        def ps_ot():
            return psf[_rot("ot")]

        def ps_tb():
            return psbv[_rot("tb")]

        def interleave(gens, K):
            active = []
            it = iter(gens)
            more = True
            while True:
                while more and len(active) < K:
                    try:
                        active.append(next(it))
                    except StopIteration:
                        more = False
                if not active:
                    break
                for g_ in list(active):
                    try:
                        next(g_)
                    except StopIteration:
                        active.remove(g_)

        P.dma("c_rope", rope[:], rope_d[:, :, :], writes=[b_rope])
        P.dma("c_idb", ident_b[:], ident_b_d[:, :], writes=[b_idb])
        P.dma("c_idf", ident_f[:], ident_f_d[:, :], writes=[b_idf])
        P.dma("c_mb", maskB[:], maskB_d[:, :], writes=[b_maskB])
        P.dma("c_rc", retc[:], retc_d[:, :], writes=[b_retc])
        P.dma("c_km", kmask[:], kmask_d[:, :], writes=[b_kmask])
        P.dma("c_cs", cs[:], cT_in[:, :], writes=[b_cs])
        cse, b_cse = sbuf(st, "cse", [128, 8], F32)
        op("act", lambda e: e.activation(out=cse[:], in_=cs[:], func=AF.Exp, scale=-1.0), [b_cs], [b_cse])
        ts("dve", cse[:], cse[:], 1.0, ALU.add, [b_cse], [b_cse])
        op("dve", lambda e: e.reciprocal(out=cse[:], in_=cse[:]), [b_cse], [b_cse])
        tt("dve", cs[:], cs[:], cse[:], ALU.mult, [b_cs, b_cse], [b_cs])
        op("pool", lambda e: e.memset(ones_row[:], 1.0), [], [b_ones])
        op("pool", lambda e: e.memset(eps_t[:], EPS), [], [b_eps])

        def load_w(dst, b_dst, src_ap, ncols, col0=0):
            done = 0
            while done < ncols:
                n = min(256, ncols - done)
                i = state["ws"]
                state["ws"] ^= 1
                stg, b_stg = wstage[i]
                P.dma("ws%d" % i, stg[:, :, 0:n], src_ap[:, done:done + n].rearrange("(k p) n -> p k n", p=128), writes=[b_stg])
                cp("pool", dst[:, :, col0 + done:col0 + done + n], stg[:, :, 0:n], [b_stg], [b_dst])
                done += n

        def proj(t_cols, W, b_W, n, ps, b_ps, wcol0=0, pcol0=0):
            for k in range(8):
                op("pe", lambda e, k=k: e.matmul(ps[:, pcol0:pcol0 + n], lhsT=hT[:, k, t_cols], rhs=W[:, k, wcol0:wcol0 + n],
                                                start=(k == 0), stop=(k == 7)), [b_hT, b_W], [b_ps])

        def rstd_inplace(ss, b_ss, d):
            op("act", lambda e: e.activation(out=ss, in_=ss, func=AF.Ln, scale=1.0 / d, bias=eps_t[:, 0:1]), [b_ss, b_eps], [b_ss])
            op("act", lambda e: e.activation(out=ss, in_=ss, func=AF.Exp, scale=-0.5), [b_ss], [b_ss])

        def tcols(t):
            return slice(t * 128, (t + 1) * 128)

        def norm_rope(src3, b_src, G, d, gains, b_g, half, cos, sin, out3, b_out, scr):
            rot = 2 * half
            sq, b_sq = scr["sq"]
            xn, b_xn = scr["xn"]
            ssx, b_ssx = scr["ss"]
            xn3 = xn[:, 0:G * d].rearrange("p (g d) -> p g d", d=d)
            if gains is not None:
                sq3 = sq[:, 0:G * d].rearrange("p (g d) -> p g d", d=d)
                op("act", lambda e: e.activation(out=sq3, in_=src3, func=AF.Square), [b_src], [b_sq])
                yield
                op("dve", lambda e: e.reduce_sum(out=ssx[:, 0:G], in_=sq3, axis=AX.X), [b_sq], [b_ssx])
                yield
                rstd_inplace(ssx[:, 0:G], b_ssx, float(d))
                yield
                tt("dve", xn3, src3, ssx[:, 0:G].unsqueeze(2).to_broadcast([128, G, d]), ALU.mult, [b_src, b_ssx], [b_xn])
                yield
                tt("pool", xn3, xn3, gains, ALU.mult, [b_xn, b_g], [b_xn])
                yield
            else:
                cp("act", xn3, src3, [b_src], [b_xn])
                yield
            x1 = xn3[:, :, 0:half]
            x2 = xn3[:, :, half:rot]
            cb = cos.unsqueeze(1).to_broadcast([128, G, half])
            sb_ = sin.unsqueeze(1).to_broadcast([128, G, half])
            tv = []
            for i in range(4):
                tq, b_tq = scr["t%d" % i]
                tv.append((tq[:, 0:G * half].rearrange("p (g d) -> p g d", d=half), b_tq))
            tt("dve", tv[0][0], x1, cb, ALU.mult, [b_xn, b_rope], [tv[0][1]])
            tt("pool", tv[1][0], x2, sb_, ALU.mult, [b_xn, b_rope], [tv[1][1]])
            tt("pool", tv[2][0], x1, sb_, ALU.mult, [b_xn, b_rope], [tv[2][1]])
            tt("dve", tv[3][0], x2, cb, ALU.mult, [b_xn, b_rope], [tv[3][1]])
            if rot < d:
                cp("act", out3[:, :, rot:d], xn3[:, :, rot:d], [b_xn], [b_out])
            yield
            tt("dve", out3[:, :, 0:half], tv[0][0], tv[1][0], ALU.subtract, [tv[0][1], tv[1][1]], [b_out])
            tt("pool", out3[:, :, half:rot], tv[2][0], tv[3][0], ALU.add, [tv[2][1], tv[3][1]], [b_out])
            yield

        def make_scr(ph, tag, n):
            out = []
            for i in range(n):
                s = {}
                s["sq"] = sbuf(ph, f"{tag}sq{i}", [128, 256], F32)
                s["xn"] = sbuf(ph, f"{tag}xn{i}", [128, 256], F32)
                s["ss"] = sbuf(ph, f"{tag}ss{i}", [128, 8], F32)
                for j in range(4):
                    s["t%d" % j] = sbuf(ph, f"{tag}t{j}_{i}", [128, 96], F32)
                out.append(s)
            return out

        def gate_and_store(t, on2d, b_on, Wg, b_Wg, n, ocol, gsc, b_gsc, ofin, b_ofin):
            psg, b_psg = ps_mm()
            proj(tcols(t), Wg, b_Wg, n, psg, b_psg)
            yield
            op("act", lambda e: e.activation(out=gsc[:, 0:n], in_=psg[:, 0:n], func=AF.Exp, scale=-1.0), [b_psg], [b_gsc])
            yield
            ts("pool", gsc[:, 0:n], gsc[:, 0:n], 1.0, ALU.add, [b_gsc], [b_gsc])
            yield
            op("dve", lambda e: e.reciprocal(out=gsc[:, 0:n], in_=gsc[:, 0:n]), [b_gsc], [b_gsc])
            yield
            tt("dve", gsc[:, 0:n], psg[:, 0:n], gsc[:, 0:n], ALU.mult, [b_psg, b_gsc], [b_gsc])
            yield
            tt("dve", ofin[:, 0:n], on2d, gsc[:, 0:n], ALU.mult, [b_on, b_gsc], [b_ofin])
            yield
            P.dma("ofin_" + b_ofin.name, o_scr[t * 128:(t + 1) * 128, ocol:ocol + n], ofin[:, 0:n], reads=[b_ofin], writes=[obufs[t]])
            yield

        xbufs = [Buf(f"x{t}") for t in range(NT)]
        obufs = [Buf(f"o{t}") for t in range(NT)]

        for l in range(L):
            lam_init = 0.8 - 0.6 * math.exp(-0.3 * l)
            xsrc = x_in if l == 0 else y_out
            P.dma("spt", spt[:], smallp[l:l + 1, :].partition_broadcast(128), writes=[b_spt])
            cp("dve", gA[:, 0:4, :], spt[:, SP_QNA:SP_QNA + 32].unsqueeze(1).to_broadcast([128, 4, 32]), [b_spt], [b_gA])
            cp("dve", gA[:, 4:8, :], spt[:, SP_KNA:SP_KNA + 32].unsqueeze(1).to_broadcast([128, 4, 32]), [b_spt], [b_gA])
            cp("dve", gB[:, 0:2, :], spt[:, SP_QNB:SP_QNB + 64].unsqueeze(1).to_broadcast([128, 2, 64]), [b_spt], [b_gB])
            cp("dve", gB[:, 2:4, :], spt[:, SP_KNB:SP_KNB + 64].unsqueeze(1).to_broadcast([128, 2, 64]), [b_spt], [b_gB])
            ts("dve", gS[:], spt[:, SP_SUB:SP_SUB + 64].unsqueeze(1).to_broadcast([128, 2, 64]), 1.0 - lam_init, ALU.mult, [b_spt], [b_gS])
            tt("dve", decs[:, 0:32], spt[:, SP_LQ1:SP_LQ1 + 32], spt[:, SP_LK1:SP_LK1 + 32], ALU.mult, [b_spt], [b_decs])
            op("dve", lambda e: e.reduce_sum(out=lamt[:, 0:1], in_=decs[:, 0:32], axis=AX.X), [b_decs], [b_lam])
            tt("dve", decs[:, 32:64], spt[:, SP_LQ2:SP_LQ2 + 32], spt[:, SP_LK2:SP_LK2 + 32], ALU.mult, [b_spt], [b_decs])
            op("dve", lambda e: e.reduce_sum(out=lamt[:, 1:2], in_=decs[:, 32:64], axis=AX.X), [b_decs], [b_lam])
            op("act", lambda e: e.activation(out=lamt[:, 2:4], in_=lamt[:, 0:2], func=AF.Exp), [b_lam], [b_lam])
            tt("dve", lamt[:, 4:5], lamt[:, 3:4], lamt[:, 2:3], ALU.subtract, [b_lam], [b_lam])
            ts("dve", lamt[:, 6:7], lamt[:, 4:5], -lam_init, ALU.add, [b_lam], [b_lam])
            op("act", lambda e: e.activation(out=decs[:, 0:8], in_=spt[:, SP_DEC:SP_DEC + 8], func=AF.Exp, scale=-1.0), [b_spt, b_decs], [b_decs])
            ts("dve", decs[:, 0:8], decs[:, 0:8], 1.0, ALU.add, [b_decs], [b_decs])
            op("act", lambda e: e.activation(out=decs[:, 0:8], in_=decs[:, 0:8], func=AF.Ln), [b_decs], [b_decs])
            ts("dve", decs[:, 0:8], decs[:, 0:8], -1.0, ALU.mult, [b_decs], [b_decs])
            for (o0, i0, rc) in ((8, 0, 513), (12, 4, 515), (16, 0, 512), (20, 4, 514)):
                op("act", lambda e, o0=o0, i0=i0, rc=rc: e.activation(out=decs[:, o0:o0 + 4], in_=decs[:, i0:i0 + 4], func=AF.Exp, scale=retc[:, rc:rc + 1]),
                   [b_decs, b_retc], [b_decs])
            op("act", lambda e: e.activation(out=decs[:, 24:32], in_=decs[:, 0:8], func=AF.Exp, scale=128.0), [b_decs], [b_decs])
            with ExitStack() as ph:
                dtmp, b_dtmp = sbuf(ph, "dtmp", [128, 128], F32)
                for h in range(4):
                    op("act", lambda e, h=h: e.activation(out=DT[:, h, :], in_=retc[:, 0:128], func=AF.Exp, scale=decs[:, h:h + 1]), [b_decs, b_retc], [b_DT])
                    tt("dve", DT[:, h, :], DT[:, h, :], retc[:, 128:256], ALU.mult, [b_DT, b_retc], [b_DT])
                    op("act", lambda e, h=h: e.activation(out=dtmp[:], in_=retc[:, 256:384], func=AF.Exp, scale=decs[:, 4 + h:5 + h]), [b_decs, b_retc], [b_dtmp])
                    tt("dve", dtmp[:], dtmp[:], retc[:, 384:512], ALU.mult, [b_dtmp, b_retc], [b_dtmp])
                    tt("dve", DT[:, h, :], DT[:, h, :], dtmp[:], ALU.add, [b_dtmp, b_DT], [b_DT])
                P.barrier()

            ph_norm = ExitStack()
            gs_bc, b_gs = sbuf(ph_norm, "gs_bc", [128, 1024], F32)
            shift_bc, b_shift = sbuf(ph_norm, "shift_bc", [128, 1024], F32)
            ph_ada = ExitStack()
            modrow, b_modrow = sbuf(ph_ada, "modrow", [1, 3072], F32)
            bada, b_bada = sbuf(ph_ada, "bada", [1, 3072], F32)
            P.dma("bada", bada[:], b_ada[l:l + 1, :], writes=[b_bada])
            for nt in range(12):
                i = state["ws"]
                state["ws"] ^= 1
                stg, b_stg = wstage[i]
                P.dma("ws%d" % i, stg[:, :, :], w_ada[l, :, nt * 256:(nt + 1) * 256].rearrange("(k p) n -> p k n", p=128), writes=[b_stg])
                ps, b_ps = ps_mm()
                for k in range(8):
                    op("pe", lambda e, k=k, stg=stg, ps=ps: e.matmul(ps[0:1, 0:256], lhsT=cs[:, k:k + 1], rhs=stg[:, k, :], start=(k == 0), stop=(k == 7)),
                       [b_cs, b_stg], [b_ps])
                tt("dve", modrow[0:1, nt * 256:(nt + 1) * 256], ps[0:1, 0:256], bada[0:1, nt * 256:(nt + 1) * 256], ALU.add, [b_ps, b_bada], [b_modrow])
            with ExitStack() as ph:
                g_bc, b_gbc = sbuf(ph, "g_bc", [128, 1024], F32)
                P.dma("gbc", g_bc[:], norm_g[l:l + 1, :].partition_broadcast(128), writes=[b_gbc])
                for j in range(6):
                    ps, b_ps = ps_mm()
                    op("pe", lambda e, ps=ps, j=j: e.matmul(ps[:, :], lhsT=ones_row[0:1, :], rhs=modrow[0:1, j * 512:(j + 1) * 512], start=True, stop=True),
                       [b_ones, b_modrow], [b_ps])
                    sl = slice((j % 2) * 512, (j % 2 + 1) * 512)
                    if j < 2:
                        cp("act", shift_bc[:, sl], ps[:, :], [b_ps], [b_shift])
                    elif j < 4:
                        op("dve", lambda e, ps=ps, sl=sl: e.scalar_tensor_tensor(out=gs_bc[:, sl], in0=ps[:, :], scalar=1.0, in1=g_bc[:, sl], op0=ALU.add, op1=ALU.mult),
                           [b_ps, b_gbc], [b_gs])
                    else:
                        cp("act", gate_bc[:, sl], ps[:, :], [b_ps], [b_gate])
                P.barrier()
            ph_ada.close()

            with ExitStack() as ph:
                KN = 3
                set_pools(mm=[0, 1, 2, 3, 4, 5], tb=[5, 6, 7])
                xt = [sbuf(ph, f"xt{i}", [128, 1024], F32) for i in range(KN)]
                junk, b_junk = sbuf(ph, "junk", [128, 1024], BF16)
                h1 = [sbuf(ph, f"h1_{i}", [128, 1024], F32) for i in range(KN)]
                hb = [sbuf(ph, f"hb{i}", [128, 1024], BF16) for i in range(KN)]
                ssn = [sbuf(ph, f"ssn{i}", [128, 1], F32) for i in range(KN)]

                def norm_body(t):
                    x_t, b_x = xt[t % KN]
                    h_t, b_h = h1[t % KN]
                    hb_t, b_hb = hb[t % KN]
                    ss, b_ss = ssn[t % KN]
                    P.dma("xt%d" % (t % KN), x_t[:], xsrc[t * 128:(t + 1) * 128, :], reads=[xbufs[t]], writes=[b_x])
                    yield
                    op("act", lambda e: e.activation(out=junk[:], in_=x_t[:], func=AF.Square, accum_out=ss[:]), [b_x], [b_junk, b_ss])
                    yield
                    rstd_inplace(ss[:], b_ss, 1024.0)
                    yield
                    op("dve", lambda e: e.scalar_tensor_tensor(out=h_t[:], in0=x_t[:], scalar=ss[:, 0:1], in1=gs_bc[:], op0=ALU.mult, op1=ALU.mult),
                       [b_x, b_ss, b_gs], [b_h])
                    yield
                    tt("pool", hb_t[:], h_t[:], shift_bc[:], ALU.add, [b_h, b_shift], [b_hb])
                    yield
                    pb, b_pb = ps_tb()
                    for k in range(8):
                        op("pe", lambda e, k=k: e.transpose(pb[:, k * 128:(k + 1) * 128], hb_t[:, k * 128:(k + 1) * 128], ident_b[:]),
                           [b_hb, b_idb], [b_pb])
                    yield
                    cp("act" if t % 2 else "dve", hT[:, :, tcols(t)], pb[:, :].rearrange("p (k n) -> p k n", n=128), [b_pb], [b_hT])
                    yield

                interleave((norm_body(t) for t in range(NT)), KN)
                P.barrier()
            ph_norm.close()
            if dbg == "hT":
                with ExitStack() as ph:
                    d32, b_d32 = sbuf(ph, "d32", [128, 8 * S], F32)
                    cp("dve", d32[:], hT[:].rearrange("p k s -> p (k s)"), [b_hT], [b_d32])
                    P.dma("dbg", dbg_hT[:, :], d32[:], reads=[b_d32], writes=[xbufs[0]])
                break

            for c in range(2 if "A" in stages else 0):
                with ExitStack() as ph:
                    Wqk, b_Wqk = sbuf(ph, "A_Wqk", [128, 8, 256], BF16)
                    Wv, b_Wv = sbuf(ph, "A_Wv", [128, 8, 128], BF16)
                    Wg, b_Wg = sbuf(ph, "A_Wg", [128, 8, 128], BF16)
                    wl = w_in[l]
                    load_w(Wqk, b_Wqk, wl[:, OFF_QA + 128 * c:OFF_QA + 128 * c + 128], 128, 0)
                    load_w(Wqk, b_Wqk, wl[:, OFF_KA + 128 * c:OFF_KA + 128 * c + 128], 128, 128)
                    load_w(Wv, b_Wv, wl[:, OFF_VA + 128 * c:OFF_VA + 128 * c + 128], 128)
                    load_w(Wg, b_Wg, wl[:, OFF_GA + 128 * c:OFF_GA + 128 * c + 128], 128)
                    qT, b_qT = sbuf(ph, "A_qT", [128, S], BF16)
                    kps = [sbuf(ph, f"A_kp{j_}", [128, S], BF16) for j_ in range(4)]
                    vaug, b_vaug = sbuf(ph, "A_vaug", [128, NT, 2, 66], BF16)
                    if not DBG.get("A_nomemset"):
                        op("pool", lambda e: e.memset(vaug[:, :, :, 64:65], 1.0), [], [b_vaug])
                    ph2 = ExitStack()
                    KA = 3
                    set_pools(mm=[0, 1, 2, 3, 4], tb=[5, 6, 7])
                    scr = make_scr(ph2, "A", KA)
                    qkb = [sbuf(ph2, f"A_qkb{i}", [128, 256], BF16) for i in range(KA)]

                    def a_proj_body(t):
                        ps, b_ps = ps_mm()
                        proj(tcols(t), Wqk, b_Wqk, 256, ps, b_ps)
                        yield
                        qk_t, b_qk = qkb[t % KA]
                        yield from norm_rope(ps[:, 0:256].rearrange("p (g d) -> p g d", d=32), b_ps, 8, 32, gA[:], b_gA, 4,
                                             rope[:, t, 0:4], rope[:, t, 4:8], qk_t[:].rearrange("p (g d) -> p g d", d=32), b_qk, scr[t % KA])
                        pb, b_pb = ps_tb()
                        for j in range(2):
                            op("pe", lambda e, j=j: e.transpose(pb[:, j * 128:(j + 1) * 128], qk_t[:, j * 128:(j + 1) * 128], ident_b[:]),
                               [b_qk, b_idb], [b_pb])
                        yield
                        cp("act", qT[:, tcols(t)], pb[:, 0:128], [b_pb], [b_qT])
                        for j_ in range(4):
                            ts("dve", kps[j_][0][:, tcols(t)], pb[:, 128:256], kmask[:, j_:j_ + 1], ALU.mult, [b_pb, b_kmask], [kps[j_][1]])
                        yield
                        psv, b_psv = ps_mm()
                        proj(tcols(t), Wv, b_Wv, 128, psv, b_psv)
                        yield
                        cp("act", vaug[:, t, :, 0:64], psv[:, 0:128].rearrange("p (h d) -> p h d", d=64), [b_psv], [b_vaug])
                        yield

                    interleave((a_proj_body(t) for t in range(NT)), KA)
                    P.barrier()
                    ph2.close()
                    set_pools(mm=[0], st=[1, 2, 3], ot=[4, 5, 6, 7])
                    Et = [sbuf(ph, f"A_E{i}", [128, 512], BF16) for i in range(4)]
                    OTs = [sbuf(ph, f"A_OTs{i}", [65, 2, 512], F32) for i in range(2)]
                    opre = [sbuf(ph, f"A_opre{i}", [128, 4, 2, 64], F32) for i in range(2)]
                    KE = 3
                    tA = [sbuf(ph, f"A_tA{i}", [128, 64], F32) for i in range(KE)]
                    tB = [sbuf(ph, f"A_tB{i}", [128, 64], F32) for i in range(KE)]
                    rden = [sbuf(ph, f"A_rden{i}", [128, 2], F32) for i in range(KE)]
                    osq = [sbuf(ph, f"A_osq{i}", [128, 128], F32) for i in range(KE)]
                    oss = [sbuf(ph, f"A_oss{i}", [128, 2], F32) for i in range(KE)]
                    onn = [sbuf(ph, f"A_on{i}", [128, 128], F32) for i in range(KE)]
                    gsc = [sbuf(ph, f"A_gsc{i}", [128, 128], F32) for i in range(KE)]
                    ofin = [sbuf(ph, f"A_ofin{i}", [128, 128], BF16) for i in range(KE)]
                    side = []
                    cnt_e = [0]

                    def a_epi(g, hl, ots, b_ots):
                        for t4 in range(4):
                            yield from a_epi_tile(g, hl, ots, b_ots, t4)

                    def a_epi_tile(g, hl, ots, b_ots, t4):
                        opre_, b_opre = opre[g % 2]
                        if True:
                            i_ = cnt_e[0] % KE
                            cnt_e[0] += 1
                            psT, b_psT = ps_mm()
                            for m in range(2):
                                op("pe", lambda e, m=m: e.transpose(psT[:, m * 65:(m + 1) * 65], ots[0:65, m, t4 * 128:(t4 + 1) * 128], ident_f[0:65, 0:65]),
                                   [b_ots, b_idf], [b_psT])
                            yield
                            rd, b_rd = rden[i_]
                            ta, b_ta = tA[i_]
                            tb_, b_tb = tB[i_]
                            op("dve", lambda e: e.reciprocal(out=rd[:], in_=psT[:, 0:130].rearrange("p (m e) -> p m e", e=65)[:, :, 64]), [b_psT], [b_rd])
                            yield
                            ts("dve", ta[:], psT[:, 0:64], rd[:, 0:1], ALU.mult, [b_psT, b_rd], [b_ta])
                            yield
                            op("act", lambda e: e.activation(out=tb_[:], in_=psT[:, 65:129], func=AF.Copy, scale=rd[:, 1:2]), [b_psT, b_rd], [b_tb])
                            yield
                            op("dve", lambda e: e.scalar_tensor_tensor(out=opre_[:, t4, hl, :], in0=tb_[:], scalar=lamt[:, 6:7], in1=ta[:],
                                                                       op0=ALU.mult, op1=ALU.add), [b_tb, b_ta, b_lam], [b_opre])
                            yield

                    def a_fin(g):
                        for t4 in range(4):
                            yield from a_fin_tile(g, t4)

                    def a_fin_tile(g, t4):
                        opre_, b_opre = opre[g % 2]
                        if True:
                            t = g * 4 + t4
                            i_ = cnt_e[0] % KE
                            cnt_e[0] += 1
                            sq_, b_sq = osq[i_]
                            ss_, b_ss = oss[i_]
                            on_, b_on = onn[i_]
                            o2 = opre_[:, t4, :, :]
                            tt("pool", sq_[:].rearrange("p (h d) -> p h d", d=64), o2, o2, ALU.mult, [b_opre], [b_sq])
                            yield
                            op("dve", lambda e: e.reduce_sum(out=ss_[:], in_=sq_[:].rearrange("p (h d) -> p h d", d=64), axis=AX.X), [b_sq], [b_ss])
                            yield
                            rstd_inplace(ss_[:], b_ss, 64.0)
                            yield
                            on3 = on_[:].rearrange("p (h d) -> p h d", d=64)
                            tt("dve", on3, o2, ss_[:].unsqueeze(2).to_broadcast([128, 2, 64]), ALU.mult, [b_opre, b_ss], [b_on])
                            yield
                            tt("pool", on3, on3, gS[:], ALU.mult, [b_on, b_gS], [b_on])
                            yield
                            yield from gate_and_store(t, on_[:], b_on, Wg, b_Wg, 128, OCOL_A + 128 * c, gsc[i_][0], gsc[i_][1], ofin[i_][0], ofin[i_][1])

                    def a_main():
                        LA = 2
                        items = [(g, hl, m, kt) for g in range(S // 512) for hl in range(2) for m in range(2) for kt in range(NT)]
                        pend = []
                        otmap = {}
                        for idx in range(len(items) + LA):
                            if idx < len(items):
                                g, hl, m, kt = items[idx]
                                kp, b_kp = kps[2 * hl + m]
                                stp, b_st = ps_st()
                                op("pe", lambda e, stp=stp, kp=kp, kt=kt, g=g: e.matmul(
                                    stp[:, :], lhsT=kp[:, tcols(kt)], rhs=qT[:, g * 512:(g + 1) * 512], start=True, stop=True),
                                   [b_kp, b_qT], [b_st])
                                E, b_E = Et[idx % 4]
                                op("act", lambda e, E=E, stp=stp: e.activation(out=E[:], in_=stp[:], func=AF.Exp, scale=32 ** -0.5), [b_st], [b_E])
                                pend.append((g, hl, m, kt, E, b_E))
                            if idx >= LA:
                                g, hl, m, kt, E, b_E = pend.pop(0)
                                if kt == 0:
                                    otmap[(g, hl, m)] = ps_ot()
                                ot, b_ot = otmap[(g, hl, m)]
                                op("pe", lambda e, ot=ot, kt=kt, hl=hl, E=E: e.matmul(ot[0:65, :], lhsT=vaug[:, kt, hl, 0:65], rhs=E[:], start=(kt == 0), stop=(kt == NT - 1)),
                                   [b_vaug, b_E], [b_ot])
                                if kt == NT - 1:
                                    slot = (g * 2 + hl) % 2
                                    ots, b_ots = OTs[slot]
                                    while any(tg == slot for tg, _ in side):
                                        try:
                                            next(side[0][1])
                                        except StopIteration:
                                            side.pop(0)
                                    cp("dve", ots[0:65, m, :], ot[0:65, :], [b_ot], [b_ots])
                                    if m == 1:
                                        side.append((slot, a_epi(g, hl, ots, b_ots)))
                                        if hl == 1:
                                            side.append((None, a_fin(g)))
                            yield

                    for _ in a_main():
                        if side:
                            try:
                                next(side[0][1])
                            except StopIteration:
                                side.pop(0)
                    while side:
                        try:
                            next(side[0][1])
                        except StopIteration:
                            side.pop(0)
                    P.barrier()

            for c in range(3 if "B" in stages else 0):
                with ExitStack() as ph:
                    Wqk, b_Wqk = sbuf(ph, "B_Wqk", [128, 8, 256], BF16)
                    Wv, b_Wv = sbuf(ph, "B_Wv", [128, 8, 128], BF16)
                    Wg, b_Wg = sbuf(ph, "B_Wg", [128, 8, 128], BF16)
                    wl = w_in[l]
                    load_w(Wqk, b_Wqk, wl[:, OFF_QB + 128 * c:OFF_QB + 128 * c + 128], 128, 0)
                    load_w(Wqk, b_Wqk, wl[:, OFF_KB + 128 * c:OFF_KB + 128 * c + 128], 128, 128)
                    load_w(Wv, b_Wv, wl[:, OFF_VB + 128 * c:OFF_VB + 128 * c + 128], 128)
                    load_w(Wg, b_Wg, wl[:, OFF_GB + 128 * c:OFF_GB + 128 * c + 128], 128)
                    qT, b_qT = sbuf(ph, "B_qT", [128, S], BF16)
                    kT, b_kT = sbuf(ph, "B_kT", [128, S], BF16)
                    vB, b_vB = sbuf(ph, "B_vB", [128, 3, NT, 2, 66], BF16)
                    op("pool", lambda e: e.memset(vB[:, :, :, :, 64:65].rearrange("p a b c d -> p (a b c d)"), 1.0), [], [b_vB])
                    ph2 = ExitStack()
                    KB = 3
                    set_pools(mm=[0, 1, 2, 3, 4], tb=[5, 6, 7])
                    scr = make_scr(ph2, "B", KB)
                    qkb = [sbuf(ph2, f"B_qkb{i}", [128, 256], BF16) for i in range(KB)]

                    def b_proj_body(t):
                        ps, b_ps = ps_mm()
                        proj(tcols(t), Wqk, b_Wqk, 256, ps, b_ps)
                        yield
                        qk_t, b_qk = qkb[t % KB]
                        yield from norm_rope(ps[:, 0:256].rearrange("p (g d) -> p g d", d=64), b_ps, 4, 64, gB[:], b_gB, 8,
                                             rope[:, t, 8:16], rope[:, t, 16:24], qk_t[:].rearrange("p (g d) -> p g d", d=64), b_qk, scr[t % KB])
                        pb, b_pb = ps_tb()
                        for j in range(2):
                            op("pe", lambda e, j=j: e.transpose(pb[:, j * 128:(j + 1) * 128], qk_t[:, j * 128:(j + 1) * 128], ident_b[:]),
                               [b_qk, b_idb], [b_pb])
                        yield
                        cp("act", qT[:, tcols(t)], pb[:, 0:128], [b_pb], [b_qT])
                        cp("act", kT[:, tcols(t)], pb[:, 128:256], [b_pb], [b_kT])
                        yield

                    interleave((b_proj_body(t) for t in range(NT)), KB)
                    vi = 0
                    for gi, D in enumerate((1, 4, 16)):
                        ntl = S // D // 128
                        for r in range(D):
                            for j in range(ntl):
                                psv, b_psv = ps_mm()
                                c0 = r + D * 128 * j
                                proj(slice(c0, c0 + D * 127 + 1, D), Wv, b_Wv, 128, psv, b_psv)
                                cp("act" if vi % 2 else "dve", vB[:, gi, r * ntl + j, :, 0:64], psv[:, 0:128].rearrange("p (h d) -> p h d", d=64), [b_psv], [b_vB])
                                vi += 1
                    P.barrier()
                    ph2.close()
                    set_pools(mm=[0, 7], st=[1, 2, 3], ot=[4, 5, 6])
                    Et = [sbuf(ph, f"B_E{i}", [128, 384], BF16) for i in range(4)]
                    accs = [sbuf(ph, f"B_acc{i}", [65, 2048], F32) for i in range(2)]
                    ob, b_ob = sbuf(ph, "B_ob", [128, 16, 2, 64], F32)
                    KE = 3
                    rden = [sbuf(ph, f"B_rden{i}", [128, 1], F32) for i in range(KE)]
                    gsc = [sbuf(ph, f"B_gsc{i}", [128, 128], F32) for i in range(KE)]
                    ofin = [sbuf(ph, f"B_ofin{i}", [128, 128], BF16) for i in range(KE)]
                    side = []
                    cnt_e = [0]

                    def b_epi_tile(hl, acc, b_acc, t16):
                        i_ = cnt_e[0] % KE
                        cnt_e[0] += 1
                        psT, b_psT = ps_mm()
                        op("pe", lambda e: e.transpose(psT[:, 0:65], acc[0:65, t16 * 128:(t16 + 1) * 128], ident_f[0:65, 0:65]), [b_acc, b_idf], [b_psT])
                        yield
                        rd, b_rd = rden[i_]
                        op("dve", lambda e: e.reciprocal(out=rd[:], in_=psT[:, 64:65]), [b_psT], [b_rd])
                        yield
                        ts("dve", ob[:, t16, hl, :], psT[:, 0:64], rd[:, 0:1], ALU.mult, [b_psT, b_rd], [b_ob])
                        yield

                    def b_epi(hl, acc, b_acc):
                        for t16 in range(16):
                            yield from b_epi_tile(hl, acc, b_acc, t16)

                    def b_fin(u):
                        for t16 in range(16):
                            i_ = cnt_e[0] % KE
                            cnt_e[0] += 1
                            yield from gate_and_store(u * 16 + t16, ob[:, t16, :, :].rearrange("p h d -> p (h d)"), b_ob, Wg, b_Wg, 128, OCOL_B + 128 * c,
                                                      gsc[i_][0], gsc[i_][1], ofin[i_][0], ofin[i_][1])

                    def drain(cond):
                        while any(cond(tg) for tg, _ in side):
                            try:
                                next(side[0][1])
                            except StopIteration:
                                side.pop(0)

                    def b_main():
                        LA = 2
                        items = []
                        for u in range(S // 2048):
                            for hl in range(2):
                                for gi, D in enumerate((1, 4, 16)):
                                    per = 16 // D
                                    for r in range(D):
                                        for i in range(u * per, (u + 1) * per):
                                            items.append((u, hl, gi, D, r, i))
                        pend = []
                        for idx in range(len(items) + LA):
                            if idx < len(items):
                                u, hl, gi, D, r, i = items[idx]
                                base = 64 * hl
                                ntl = S // D // 128
                                qc0 = r + D * 128 * i
                                qcols = slice(qc0, qc0 + D * 127 + 1, D)
                                js = [j for j in (i - 1, i, i + 1) if 0 <= j < ntl]
                                stp, b_st = ps_st()
                                for j in js:
                                    jj = j - (i - 1)
                                    kc0 = r + D * 128 * j
                                    op("pe", lambda e, stp=stp, jj=jj, kc0=kc0, D=D, base=base, qcols=qcols: e.matmul(
                                        stp[:, jj * 128:(jj + 1) * 128], lhsT=kT[base:base + 64, slice(kc0, kc0 + D * 127 + 1, D)], rhs=qT[base:base + 64, qcols],
                                        start=True, stop=True), [b_kT, b_qT], [b_st])
                                lo = (js[0] - (i - 1)) * 128
                                hi = (js[-1] - (i - 1) + 1) * 128
                                E, b_E = Et[idx % 4]
                                op("act", lambda e, E=E, stp=stp, lo=lo, hi=hi: e.activation(out=E[:, lo:hi], in_=stp[:, lo:hi], func=AF.Exp, scale=0.125), [b_st], [b_E])
                                tt("pool", E[:, lo:hi], E[:, lo:hi], maskB[:, lo:hi], ALU.mult, [b_E, b_maskB], [b_E])
                                pend.append((u, hl, gi, D, r, i, js, E, b_E, qc0, idx))
                            if idx >= LA:
                                u, hl, gi, D, r, i, js, E, b_E, qc0, idx0 = pend.pop(0)
                                ntl = S // D // 128
                                slot = (u * 2 + hl) % 2
                                acc, b_acc = accs[slot]
                                first_item = (gi == 0 and r == 0 and i == u * 16)
                                if first_item:
                                    drain(lambda tg: tg == slot)
                                ot, b_ot = ps_ot()
                                for j in js:
                                    jj = j - (i - 1)
                                    op("pe", lambda e, ot=ot, gi=gi, tile=r * ntl + j, hl=hl, E=E, jj=jj, first=(j == js[0]), last=(j == js[-1]): e.matmul(
                                        ot[0:65, 0:128], lhsT=vB[:, gi, tile, hl, 0:65], rhs=E[:, jj * 128:(jj + 1) * 128], start=first, stop=last),
                                       [b_vB, b_E], [b_ot])
                                a0 = qc0 - 2048 * u
                                acc_ap = acc[0:65, a0:a0 + D * 127 + 1:D]
                                if gi == 0:
                                    cp("dve", acc_ap, ot[0:65, 0:128], [b_ot], [b_acc])
                                else:
                                    tt("dve", acc_ap, ot[0:65, 0:128], acc_ap, ALU.add, [b_ot, b_acc], [b_acc])
                                last_item = (gi == 2 and r == 15 and i == u)
                                if last_item:
                                    side.append((slot, b_epi(hl, acc, b_acc)))
                                    if hl == 1:
                                        side.append((None, b_fin(u)))
                            yield

                    for _ in b_main():
                        if side:
                            try:
                                next(side[0][1])
                            except StopIteration:
                                side.pop(0)
                    drain(lambda tg: True)
                    P.barrier()

            for c in range(2 if "C" in stages else 0):
                with ExitStack() as ph:
                    Wqk, b_Wqk = sbuf(ph, "C_Wqk", [128, 8, 192], BF16)
                    Wv, b_Wv = sbuf(ph, "C_Wv", [128, 8, 192], BF16)
                    Wg, b_Wg = sbuf(ph, "C_Wg", [128, 8, 192], BF16)
                    wl = w_in[l]
                    load_w(Wqk, b_Wqk, wl[:, OFF_QC + 96 * c:OFF_QC + 96 * c + 96], 96, 0)
                    load_w(Wqk, b_Wqk, wl[:, OFF_KC + 96 * c:OFF_KC + 96 * c + 96], 96, 96)
                    load_w(Wv, b_Wv, wl[:, OFF_VC + 192 * c:OFF_VC + 192 * c + 192], 192)
                    load_w(Wg, b_Wg, wl[:, OFF_GC + 192 * c:OFF_GC + 192 * c + 192], 192)
                    qkT, b_qkT = sbuf(ph, "C_qkT", [64, 4, S], BF16)
                    vC, b_vC = sbuf(ph, "C_vC", [128, NT, 2, 96], BF16)
                    Rst, b_Rst = sbuf(ph, "C_Rst", [64, NT, 2, 2, 96], BF16)
                    ph_mid = ExitStack()
                    kfb, b_kfb = sbuf(ph_mid, "C_kfb", [128, NT, 2, 2, 48], BF16)
                    Rs = [sbuf(ph_mid, f"C_R{i}", [64, 2, 96], F32) for i in range(2)]
                    cdt = [sbuf(ph_mid, f"C_cd{i}", [64, 2, 96], F32) for i in range(2)]
                    ph2 = ExitStack()
                    KC = 2
                    set_pools(mm=[0, 1, 2, 3, 4], tb=[5, 6, 7])
                    scr = make_scr(ph2, "C", KC)
                    qkr = [sbuf(ph2, f"C_qkr{i}", [128, 4, 48], F32) for i in range(KC)]
                    qkp = [sbuf(ph2, f"C_qkp{i}", [128, 4, 64], BF16) for i in range(KC)]
                    for i in range(KC):
                        op("pool", lambda e, i=i: e.memset(qkp[i][0][:].rearrange("p g d -> p (g d)"), 0.0), [], [qkp[i][1]])

                    def c_proj_body(t):
                        ps, b_ps = ps_mm()
                        proj(tcols(t), Wqk, b_Wqk, 192, ps, b_ps)
                        yield
                        qr, b_qr = qkr[t % KC]
                        qp, b_qp = qkp[t % KC]
                        yield from norm_rope(ps[:, 0:192].rearrange("p (g d) -> p g d", d=48), b_ps, 4, 48, None, None, 24,
                                             rope[:, t, 24:48], rope[:, t, 48:72], qr[:], b_qr, scr[t % KC])
                        ts("pool", qr[:, 2:4, :], qr[:, 2:4, :], 48 ** -0.5, ALU.mult, [b_qr], [b_qr])
                        yield
                        cp("act", qp[:, :, 0:48], qr[:], [b_qr], [b_qp])
                        tt("dve", kfb[:, t, 0, :, :], qr[:, 2:4, :], decs[:, 8 + 2 * c:10 + 2 * c].unsqueeze(2).to_broadcast([128, 2, 48]), ALU.mult, [b_qr, b_decs], [b_kfb])
                        tt("pool", kfb[:, t, 1, :, :], qr[:, 2:4, :], decs[:, 12 + 2 * c:14 + 2 * c].unsqueeze(2).to_broadcast([128, 2, 48]), ALU.mult, [b_qr, b_decs], [b_kfb])
                        yield
                        pb, b_pb = ps_tb()
                        for s_ in range(4):
                            op("pe", lambda e, s_=s_: e.transpose(pb[0:64, s_ * 128:(s_ + 1) * 128], qp[:, s_, :], ident_b[:]), [b_qp, b_idb], [b_pb])
                        yield
                        cp("dve", qkT[:, :, tcols(t)], pb[0:64, 0:512].rearrange("p (g n) -> p g n", n=128), [b_pb], [b_qkT])
                        yield
                        psv, b_psv = ps_mm()
                        proj(tcols(t), Wv, b_Wv, 192, psv, b_psv)
                        yield
                        cp("act", vC[:, t, :, :], psv[:, 0:192].rearrange("p (h d) -> p h d", d=96), [b_psv], [b_vC])
                        yield

                    interleave((c_proj_body(t) for t in range(NT)), KC)
                    P.barrier()
                    ph2.close()
                    set_pools(mm=[0, 1, 2, 3], st=[1, 2, 3], ot=[4, 5, 6, 7])

                    def c_scan(d_):
                        R_, b_R = Rs[d_]
                        cd_, b_cd = cdt[d_]
                        op("pool", lambda e: e.memset(R_[:].rearrange("p h d -> p (h d)"), 0.0), [], [b_R])
                        cp("dve", cd_[0:48, :, :], decs[0:48, 24 + 4 * d_ + 2 * c:26 + 4 * d_ + 2 * c].unsqueeze(2).to_broadcast([48, 2, 96]), [b_decs], [b_cd])
                        yield
                        order = range(NT) if d_ == 0 else range(NT - 1, -1, -1)
                        for n in order:
                            psU, b_psU = ps_mm()
                            for hl in range(2):
                                op("pe", lambda e, psU=psU, n=n, hl=hl: e.matmul(psU[0:48, hl * 96:(hl + 1) * 96], lhsT=kfb[:, n, d_, hl, :], rhs=vC[:, n, hl, :], start=True, stop=True),
                                   [b_kfb, b_vC], [b_psU])
                            cp("dve", Rst[0:48, n, d_, :, :], R_[0:48, :, :], [b_R], [b_Rst])
                            yield
                            tt("dve", R_[0:48, :, :], R_[0:48, :, :], cd_[0:48, :, :], ALU.mult, [b_R, b_cd], [b_R])
                            yield
                            tt("dve", R_[0:48, :, :], psU[0:48, 0:192].rearrange("p (h d) -> p h d", d=96), R_[0:48, :, :], ALU.add, [b_psU, b_R], [b_R])
                            yield

                    interleave((c_scan(d_) for d_ in range(2)), 2)
                    P.barrier()
                    ph_mid.close()
                    set_pools(mm=[6, 7], st=[0, 1, 2], ot=[3, 4, 5])
                    KO = 2
                    att = [sbuf(ph, f"C_att{i}", [128, 128], BF16) for i in range(2 * KO)]
                    oc = [sbuf(ph, f"C_oc{i}", [128, 2, 96], F32) for i in range(KO)]
                    osq = [sbuf(ph, f"C_osq{i}", [128, 2, 96], F32) for i in range(KO)]
                    oss = [sbuf(ph, f"C_oss{i}", [128, 2], F32) for i in range(KO)]
                    gsc = [sbuf(ph, f"C_gsc{i}", [128, 192], F32) for i in range(KO)]
                    ofin = [sbuf(ph, f"C_ofin{i}", [128, 192], BF16) for i in range(KO)]

                    def c_out_head(n, hl, oc_, b_oc):
                        head = 2 * c + hl
                        stp, b_st = ps_st()
                        op("pe", lambda e: e.matmul(stp[:, 0:128], lhsT=qkT[0:48, 2 + hl, tcols(n)], rhs=qkT[0:48, hl, tcols(n)], start=True, stop=True),
                           [b_qkT], [b_st])
                        yield
                        at_, b_at = att[(n % KO) * 2 + hl]
                        tt("dve", at_[:], stp[:, 0:128], DT[:, head, :], ALU.mult, [b_st, b_DT], [b_at])
                        yield
                        psO, b_psO = ps_ot()
                        op("pe", lambda e: e.matmul(psO[:, 0:96], lhsT=at_[:], rhs=vC[:, n, hl, :], start=True, stop=True), [b_at, b_vC], [b_psO])
                        for d_ in range(2):
                            op("pe", lambda e, d_=d_: e.matmul(psO[:, 96 * (d_ + 1):96 * (d_ + 2)], lhsT=qkT[0:48, hl, tcols(n)], rhs=Rst[0:48, n, d_, hl, :],
                                                              start=True, stop=True), [b_qkT, b_Rst], [b_psO])
                        yield
                        cp("act", oc_[:, hl, :], psO[:, 0:96], [b_psO], [b_oc])
                        yield
                        for d_ in range(2):
                            op("dve", lambda e, d_=d_: e.scalar_tensor_tensor(
                                out=oc_[:, hl, :], in0=psO[:, 96 * (d_ + 1):96 * (d_ + 2)], scalar=decs[:, 16 + 4 * d_ + head:17 + 4 * d_ + head], in1=oc_[:, hl, :],
                                op0=ALU.mult, op1=ALU.add), [b_psO, b_decs, b_oc], [b_oc])
                            yield

                    def c_out_body(n):
                        oc_, b_oc = oc[n % KO]
                        for hl in range(2):
                            yield from c_out_head(n, hl, oc_, b_oc)
                        sq_, b_sq = osq[n % KO]
                        ss_, b_ss = oss[n % KO]
                        tt("pool", sq_[:], oc_[:], oc_[:], ALU.mult, [b_oc], [b_sq])
                        yield
                        op("dve", lambda e: e.reduce_sum(out=ss_[:], in_=sq_[:], axis=AX.X), [b_sq], [b_ss])
                        yield
                        rstd_inplace(ss_[:], b_ss, 96.0)
                        yield
                        tt("dve", sq_[:], oc_[:], ss_[:].unsqueeze(2).to_broadcast([128, 2, 96]), ALU.mult, [b_oc, b_ss], [b_sq])
                        yield
                        tt("pool", sq_[:], sq_[:], spt[:, SP_GNC:SP_GNC + 96].unsqueeze(1).to_broadcast([128, 2, 96]), ALU.mult, [b_sq, b_spt], [b_sq])
                        yield
                        yield from gate_and_store(n, sq_[:].rearrange("p h d -> p (h d)"), b_sq, Wg, b_Wg, 192, OCOL_C + 192 * c,
                                                  gsc[n % KO][0], gsc[n % KO][1], ofin[n % KO][0], ofin[n % KO][1])

                    interleave((c_out_body(n) for n in range(NT)), KO)
                    P.barrier()

            if dbg == "o":
                with ExitStack() as ph:
                    ob16, b_ob16 = sbuf(ph, "dbg_ob", [128, 1024], BF16)
                    o32, b_o32 = sbuf(ph, "dbg_o32", [128, 1024], F32)
                    for t in range(NT):
                        P.dma("dbg_l", ob16[:], o_scr[t * 128:(t + 1) * 128, :], reads=[obufs[t]], writes=[b_ob16])
                        cp("dve", o32[:], ob16[:], [b_ob16], [b_o32])
                        P.dma("dbg_s", dbg_o[t * 128:(t + 1) * 128, :], o32[:], reads=[b_o32], writes=[xbufs[t]])
                break

            if "O" in stages:
                with ExitStack() as ph:
                    Wout, b_Wout = sbuf(ph, "Wout", [128, 8, 1024], BF16)
                    load_w(Wout, b_Wout, w_out[l], 1024)
                    KQ = 3
                    set_pools(mm=[0, 1, 2, 3, 4], tb=[5, 6, 7])
                    ott = [sbuf(ph, f"O_ot{i}", [128, 1024], BF16) for i in range(KQ)]
                    oTt = [sbuf(ph, f"O_oT{i}", [128, 8, 128], BF16) for i in range(KQ)]
                    xt = [sbuf(ph, f"O_xt{i}", [128, 1024], F32) for i in range(KQ)]
                    tm = [sbuf(ph, f"O_tm{i}", [128, 1024], F32) for i in range(KQ)]

                    def o_half(oT_t, b_oT, tm_t, b_tm, nh):
                        ps, b_ps = ps_mm()
                        for k in range(8):
                            op("pe", lambda e, k=k: e.matmul(ps[:, :], lhsT=oT_t[:, k, :], rhs=Wout[:, k, nh * 512:(nh + 1) * 512], start=(k == 0), stop=(k == 7)),
                               [b_oT, b_Wout], [b_ps])
                        yield
                        sl = slice(nh * 512, (nh + 1) * 512)
                        tt("dve", tm_t[:, sl], ps[:, :], gate_bc[:, sl], ALU.mult, [b_ps, b_gate], [b_tm])
                        yield

                    def o_body(t):
                        o_t, b_o = ott[t % KQ]
                        oT_t, b_oT = oTt[t % KQ]
                        x_t, b_x = xt[t % KQ]
                        tm_t, b_tm = tm[t % KQ]
                        P.dma("O_ot%d" % (t % KQ), o_t[:], o_scr[t * 128:(t + 1) * 128, :], reads=[obufs[t]], writes=[b_o])
                        P.dma("O_xt%d" % (t % KQ), x_t[:], xsrc[t * 128:(t + 1) * 128, :], reads=[xbufs[t]], writes=[b_x])
                        yield
                        pb, b_pb = ps_tb()
                        for k in range(8):
                            op("pe", lambda e, k=k: e.transpose(pb[:, k * 128:(k + 1) * 128], o_t[:, k * 128:(k + 1) * 128], ident_b[:]), [b_o, b_idb], [b_pb])
                        yield
                        cp("act", oT_t[:], pb[:, :].rearrange("p (k n) -> p k n", n=128), [b_pb], [b_oT])
                        yield
                        for nh in range(2):
                            yield from o_half(oT_t, b_oT, tm_t, b_tm, nh)
                        tt("pool", tm_t[:], tm_t[:], x_t[:], ALU.add, [b_tm, b_x], [b_tm])
                        yield
                        P.dma("O_st%d" % (t % KQ), y_out[t * 128:(t + 1) * 128, :], tm_t[:], reads=[b_tm], writes=[xbufs[t]])
                        yield

                    interleave((o_body(t) for t in range(NT)), KQ)
                    P.barrier()

        P.barrier()
        P.emit()
    return nc, P


def host_constants(S):
    NT = S // 128
    pos = np.arange(S, dtype=np.float32)

    def tab(theta, rot):
        inv = (np.float32(theta) ** (-np.arange(0, rot, 2, dtype=np.float32) / np.float32(rot))).astype(np.float32)
        ang = (pos[:, None] * inv[None, :]).astype(np.float32)
        return np.cos(ang).astype(np.float32), np.sin(ang).astype(np.float32)

    cA, sA = tab(ROT_THETA, 8)
    cB, sB = tab(ROT_THETA, 16)
    cC, sC = tab(RET_THETA, 48)
    rope = np.concatenate([cA, sA, cB, sB, cC, sC], axis=1)
    rope = np.ascontiguousarray(rope.reshape(NT, 128, 72).transpose(1, 0, 2)).astype(np.float32)
    a = np.arange(128)[:, None]
    b = np.arange(128)[None, :]
    maskB = np.concatenate([(a - b >= 64), (np.abs(a - b) <= 64), (b - a >= 64)], axis=1).astype(np.float32).astype(ml_dtypes.bfloat16)
    j = a
    i = b
    retc = np.zeros((128, 516), np.float32)
    retc[:, 0:128] = np.maximum(i - j, 0)
    retc[:, 128:256] = (i >= j)
    retc[:, 256:384] = np.maximum(j - i, 0)
    retc[:, 384:512] = (j > i)
    p = np.arange(128, dtype=np.float32)
    retc[:, 512] = p + 1
    retc[:, 513] = 127 - p
    retc[:, 514] = 128 - p
    retc[:, 515] = p
    kmask = np.zeros((128, 4), np.float32)
    for j_ in range(4):
        kmask[32 * j_:32 * j_ + 32, j_] = 1.0
    return {
        "rope": rope, "maskB": maskB, "retc": retc, "kmask": kmask,
        "ident_b": np.eye(128, dtype=np.float32).astype(ml_dtypes.bfloat16),
        "ident_f": np.eye(128, dtype=np.float32),
    }


def pack_small(inp, L):
    parts = [inp["qn_a"], inp["kn_a"], inp["lambda_q1"], inp["lambda_k1"], inp["lambda_q2"], inp["lambda_k2"],
             inp["subln_a"], inp["qn_b"], inp["kn_b"], np.asarray(inp["ret_decay"]).reshape(L, 8), inp["gn_c"]]
    return np.ascontiguousarray(np.concatenate([np.asarray(p_, np.float32).reshape(L, -1) for p_ in parts], axis=1))


def make_in_maps(inp, S, L, B):
    consts = host_constants(S)
    f = lambda k: np.ascontiguousarray(np.asarray(inp[k], np.float32))
    shared = {
        "w_ada": f("w_ada")[:L], "b_ada": f("b_ada")[:L], "norm_g": f("norm_g")[:L], "w_in": f("w_in")[:L], "w_out": f("w_out")[:L],
        "smallp": pack_small({k: np.asarray(v)[:L] for k, v in inp.items() if k not in ("x", "c")}, L),
    }
    shared.update(consts)
    maps = []
    x = f("x")
    c = f("c")
    for b in range(B):
        m = dict(shared)
        m["x"] = np.ascontiguousarray(x[b])
        m["cT"] = np.ascontiguousarray(c[b].reshape(8, 128).T)
        maps.append(m)
    return maps


_CACHE = {}


def kernel(**inputs):
    x = np.asarray(inputs["x"])
    B, S, _ = x.shape
    L = np.asarray(inputs["w_in"]).shape[0]
    key = (S, L)
    if key not in _CACHE:
        _CACHE[key] = build_program(S, L)[0]
    nc = _CACHE[key]
    maps = make_in_maps(inputs, S, L, B)
    in_maps = [maps[i % B] for i in range(8)]
    res = run_bass_kernel_spmd(nc, in_maps, core_ids=list(range(8)))
    out = np.stack([np.asarray(res.results[b]["y"], np.float32) for b in range(B)], axis=0)
    return out
```

```python
import math
from contextlib import ExitStack
import numpy as np
import ml_dtypes
import concourse.bass as bass
import concourse.mybir as mybir
from concourse.bass_utils import run_bass_kernel_spmd

F32 = mybir.dt.float32
BF16 = mybir.dt.bfloat16
ALU = mybir.AluOpType
AF = mybir.ActivationFunctionType
AX = mybir.AxisListType

D_MODEL = 1024
IN_COLS = 3712
EPS = 1e-6
ROT_THETA = 500000.0
RET_THETA = 10000.0
NSMALL = 488
SP_QNA, SP_KNA, SP_LQ1, SP_LK1, SP_LQ2, SP_LK2 = 0, 32, 64, 96, 128, 160
SP_SUB, SP_QNB, SP_KNB, SP_DEC, SP_GNC = 192, 256, 320, 384, 392
OFF_QA, OFF_KA, OFF_VA, OFF_GA = 0, 256, 512, 768
OFF_QB, OFF_KB, OFF_VB, OFF_GB = 1024, 1408, 1792, 2176
OFF_QC, OFF_KC, OFF_VC, OFF_GC = 2560, 2752, 2944, 3328
OCOL_A, OCOL_B, OCOL_C = 0, 256, 640


class Buf:
    __slots__ = ("name", "w", "r", "excl")

    def __init__(self, name, excl=False):
        self.name = name
        self.w = None
        self.r = {}
        self.excl = excl


class Prog:
    ENG = ("pe", "act", "dve", "pool", "sp")

    def __init__(self, nc, stack):
        self.nc = nc
        self.stack = stack
        self.q = {e: [] for e in self.ENG}
        self.cnt = {e: 0 for e in self.ENG}
        self.seen = {e: {} for e in self.ENG}
        self.sems = {}
        self.dcnt = {}
        for e in self.ENG:
            self.sems[e] = stack.enter_context(nc.semaphore("s_" + e))
        self.ninst = 0
        self.desc = {}
        self.total = 0

    def dsem(self, name):
        if name not in self.sems:
            self.sems[name] = self.stack.enter_context(self.nc.semaphore("d_" + name))
            self.dcnt[name] = 0
        return name

    def _wait(self, eng, k, v):
        if k == eng and eng == "pe":
            return
        if self.seen[eng].get(k, 0) < v:
            self.seen[eng][k] = v
            sem = self.sems[k]
            self.ninst += 1
            self.desc.setdefault(eng, []).append("wait %s>=%d" % (k, v))
            self.q[eng].append(lambda e, sem=sem, v=v: e.wait_ge(sem, v))

    def _deps(self, eng, reads, writes):
        need = {}
        for b in reads:
            if b.w is not None:
                k, v = b.w
                if need.get(k, 0) < v:
                    need[k] = v
            if b.excl:
                for k, v in b.r.items():
                    if k != eng and need.get(k, 0) < v:
                        need[k] = v
        for b in writes:
            if b.w is not None:
                k, v = b.w
                if need.get(k, 0) < v:
                    need[k] = v
            for k, v in b.r.items():
                if need.get(k, 0) < v:
                    need[k] = v
        for k, v in need.items():
            self._wait(eng, k, v)

    def op(self, eng, fn, reads=(), writes=()):
        self.total = getattr(self, "total", 0) + 1
        if self.total > DBG.get("cut", 10 ** 9):
            return
        self._deps(eng, reads, writes)
        if DBG.get("serial") and getattr(self, "last", None):
            self._wait(eng, *self.last)
        self.cnt[eng] += 1
        c = self.cnt[eng]
        self.last = (eng, c)
        sem = self.sems[eng]
        self.ninst += 1
        self.desc.setdefault(eng, []).append("op#%d (%s=%d)" % (self.total, eng, c))
        self.q[eng].append(lambda e, fn=fn, sem=sem: fn(e).then_inc(sem, 1))
        for b in writes:
            b.w = (eng, c)
            b.r = {}
        for b in reads:
            if b.w != (eng, c):
                b.r[eng] = c

    def dma(self, semname, out, in_, reads=(), writes=(), qeng="sp", **kw):
        self.total = getattr(self, "total", 0) + 1
        if self.total > DBG.get("cut", 10 ** 9):
            return
        self.dsem(semname)
        self._deps(qeng, reads, writes)
        if DBG.get("serial") and getattr(self, "last", None):
            self._wait(qeng, *self.last)
        self.dcnt[semname] += 16
        v = self.dcnt[semname]
        self.last = (semname, v)
        sem = self.sems[semname]
        self.ninst += 1
        self.desc.setdefault(qeng, []).append("dma#%d (%s=%d)" % (self.total, semname, v))
        self.q[qeng].append(
            lambda e, out=out, in_=in_, sem=sem, kw=kw: e.dma_start(out=out, in_=in_, **kw).then_inc(sem, 16))
        for b in writes:
            b.w = (semname, v)
            b.r = {}
        for b in reads:
            b.r[semname] = v

    def barrier(self):
        for e in self.ENG:
            for k in list(self.sems.keys()):
                v = self.cnt[k] if k in self.cnt else self.dcnt[k]
                if v > 0 and k != e:
                    self._wait(e, k, v)
            if e != "pe" and self.cnt[e] > 0:
                self._wait(e, e, self.cnt[e])

    def emit(self):
        nc = self.nc
        with nc.Block() as block:
            @block.tensor
            def _(e):
                for t in self.q["pe"]:
                    t(e)

            @block.scalar
            def _(e):
                for t in self.q["act"]:
                    t(e)

            @block.vector
            def _(e):
                for t in self.q["dve"]:
                    t(e)

            @block.gpsimd
            def _(e):
                for t in self.q["pool"]:
                    t(e)

            @block.sync
            def _(e):
                for t in self.q["sp"]:
                    t(e)


DBG = {}


def build_program(S, L, stages=("A", "B", "C", "O"), dbg=None):
    NT = S // 128
    nc = bass.Bass("TRN2", target_bir_lowering=False)
    dr = lambda name, shape, dt, kind="ExternalInput": nc.dram_tensor(name, shape, dt, kind=kind).ap()
    x_in = dr("x", [S, D_MODEL], F32)
    cT_in = dr("cT", [128, 8], F32)
    w_ada = dr("w_ada", [L, D_MODEL, 3 * D_MODEL], F32)
    b_ada = dr("b_ada", [L, 3 * D_MODEL], F32)
    norm_g = dr("norm_g", [L, D_MODEL], F32)
    w_in = dr("w_in", [L, D_MODEL, IN_COLS], F32)
    w_out = dr("w_out", [L, D_MODEL, D_MODEL], F32)
    smallp = dr("smallp", [L, NSMALL], F32)
    ident_b_d = dr("ident_b", [128, 128], BF16)
    ident_f_d = dr("ident_f", [128, 128], F32)
    rope_d = dr("rope", [128, NT, 72], F32)
    maskB_d = dr("maskB", [128, 384], BF16)
    retc_d = dr("retc", [128, 516], F32)
    kmask_d = dr("kmask", [128, 4], F32)
    y_out = dr("y", [S, D_MODEL], F32, kind="ExternalOutput")
    o_scr = dr("o_scr", [S, D_MODEL], BF16, kind="ExternalOutput")
    dbg_hT = dr("dbg_hT", [128, 8 * S], F32, kind="ExternalOutput") if dbg == "hT" else None
    dbg_o = dr("dbg_o", [S, D_MODEL], F32, kind="ExternalOutput") if dbg == "o" else None

    with ExitStack() as st:
        P = Prog(nc, st)

        uid = [0]

        def sbuf(stack, name, shape, dt):
            uid[0] += 1
            t = stack.enter_context(nc.sbuf_tensor("sb%d_%s" % (uid[0], name), shape, dt))
            return t, Buf(name)

        def op(eng, fn, reads=(), writes=()):
            if eng == "pool" and DBG.get("nopool"):
                eng = "dve"
            P.op(eng, fn, reads, writes)

        def cp(eng, out, in_, reads, writes):
            if eng == "act":
                op("act", lambda e: e.copy(out=out, in_=in_), reads, writes)
            else:
                op(eng, lambda e: e.tensor_copy(out=out, in_=in_), reads, writes)

        def tt(eng, out, in0, in1, alu, reads, writes):
            op(eng, lambda e: e.tensor_tensor(out=out, in0=in0, in1=in1, op=alu), reads, writes)

        def ts(eng, out, in0, s1, alu, reads, writes):
            op(eng, lambda e: e.tensor_scalar(out=out, in0=in0, scalar1=s1, scalar2=None, op0=alu), reads, writes)

        hT, b_hT = sbuf(st, "hT", [128, 8, S], BF16)
        rope, b_rope = sbuf(st, "rope", [128, NT, 72], F32)
        ident_b, b_idb = sbuf(st, "ident_b", [128, 128], BF16)
        ident_f, b_idf = sbuf(st, "ident_f", [128, 128], F32)
        maskB, b_maskB = sbuf(st, "maskB", [128, 384], BF16)
        retc, b_retc = sbuf(st, "retc", [128, 516], F32)
        kmask, b_kmask = sbuf(st, "kmask", [128, 4], F32)
        cs, b_cs = sbuf(st, "cs", [128, 8], F32)
        ones_row, b_ones = sbuf(st, "ones_row", [1, 128], F32)
        gate_bc, b_gate = sbuf(st, "gate_bc", [128, 1024], F32)
        spt, b_spt = sbuf(st, "spt", [128, NSMALL], F32)
        gA, b_gA = sbuf(st, "gA", [128, 8, 32], F32)
        gB, b_gB = sbuf(st, "gB", [128, 4, 64], F32)
        gS, b_gS = sbuf(st, "gS", [128, 2, 64], F32)
        lamt, b_lam = sbuf(st, "lamt", [128, 8], F32)
        decs, b_decs = sbuf(st, "decs", [128, 64], F32)
        DT, b_DT = sbuf(st, "DT", [128, 4, 128], F32)
        eps_t, b_eps = sbuf(st, "eps_t", [128, 1], F32)
        wstage = [sbuf(st, f"wstage{i}", [128, 8, 256], F32) for i in range(2)]
        state = {"ws": 0, "mm": 0, "st": 0, "ot": 0, "tb": 0, "scr": 0}

        psf = []
        psbv = []
        for i in range(8):
            t_ = st.enter_context(nc.psum_tensor(f"psf{i}", [128, 512], F32))
            b_ = Buf(f"psf{i}", True)
            psf.append((t_, b_))
            psbv.append((t_.bitcast(BF16), b_))
        pools = {"mm": [0, 1, 2, 3, 4, 5], "st": [1, 2, 3], "ot": [4, 5, 6, 7], "tb": [6, 7]}
        ppos = {"mm": 0, "st": 0, "ot": 0, "tb": 0}

        def set_pools(**kw):
            for k_, v_ in kw.items():
                pools[k_] = list(v_)
                ppos[k_] = 0

        def _rot(kind):
            lst = pools[kind]
            i = lst[ppos[kind] % len(lst)]
            ppos[kind] += 1
            return i

        def ps_mm():
            return psf[_rot("mm")]

        def ps_st():
            return psf[_rot("st")]

        def ps_ot():
            return psf[_rot("ot")]

        def ps_tb():
            return psbv[_rot("tb")]

        def interleave(gens, K):
            active = []
            it = iter(gens)
            more = True
            while True:
                while more and len(active) < K:
                    try:
                        active.append(next(it))
                    except StopIteration:
                        more = False
                if not active:
                    break
                for g_ in list(active):
                    try:
                        next(g_)
                    except StopIteration:
                        active.remove(g_)

        P.dma("c_rope", rope[:], rope_d[:, :, :], writes=[b_rope])
        P.dma("c_idb", ident_b[:], ident_b_d[:, :], writes=[b_idb])
        P.dma("c_idf", ident_f[:], ident_f_d[:, :], writes=[b_idf])
        P.dma("c_mb", maskB[:], maskB_d[:, :], writes=[b_maskB])
        P.dma("c_rc", retc[:], retc_d[:, :], writes=[b_retc])
        P.dma("c_km", kmask[:], kmask_d[:, :], writes=[b_kmask])
        P.dma("c_cs", cs[:], cT_in[:, :], writes=[b_cs])
        cse, b_cse = sbuf(st, "cse", [128, 8], F32)
        op("act", lambda e: e.activation(out=cse[:], in_=cs[:], func=AF.Exp, scale=-1.0), [b_cs], [b_cse])
        ts("dve", cse[:], cse[:], 1.0, ALU.add, [b_cse], [b_cse])
        op("dve", lambda e: e.reciprocal(out=cse[:], in_=cse[:]), [b_cse], [b_cse])
        tt("dve", cs[:], cs[:], cse[:], ALU.mult, [b_cs, b_cse], [b_cs])
        op("pool", lambda e: e.memset(ones_row[:], 1.0), [], [b_ones])
        op("pool", lambda e: e.memset(eps_t[:], EPS), [], [b_eps])

        def load_w(dst, b_dst, src_ap, ncols, col0=0):
            done = 0
            while done < ncols:
                n = min(256, ncols - done)
                i = state["ws"]
                state["ws"] ^= 1
                stg, b_stg = wstage[i]
                P.dma("ws%d" % i, stg[:, :, 0:n], src_ap[:, done:done + n].rearrange("(k p) n -> p k n", p=128), writes=[b_stg])
                cp("pool", dst[:, :, col0 + done:col0 + done + n], stg[:, :, 0:n], [b_stg], [b_dst])
                done += n

        def proj(t_cols, W, b_W, n, ps, b_ps, wcol0=0, pcol0=0):
            for k in range(8):
                op("pe", lambda e, k=k: e.matmul(ps[:, pcol0:pcol0 + n], lhsT=hT[:, k, t_cols], rhs=W[:, k, wcol0:wcol0 + n],
                                                start=(k == 0), stop=(k == 7)), [b_hT, b_W], [b_ps])

        def rstd_inplace(ss, b_ss, d):
            op("act", lambda e: e.activation(out=ss, in_=ss, func=AF.Ln, scale=1.0 / d, bias=eps_t[:, 0:1]), [b_ss, b_eps], [b_ss])
            op("act", lambda e: e.activation(out=ss, in_=ss, func=AF.Exp, scale=-0.5), [b_ss], [b_ss])

        def tcols(t):
            return slice(t * 128, (t + 1) * 128)

        def norm_rope(src3, b_src, G, d, gains, b_g, half, cos, sin, out3, b_out, scr):
            rot = 2 * half
            sq, b_sq = scr["sq"]
            xn, b_xn = scr["xn"]
            ssx, b_ssx = scr["ss"]
            xn3 = xn[:, 0:G * d].rearrange("p (g d) -> p g d", d=d)
            if gains is not None:
                sq3 = sq[:, 0:G * d].rearrange("p (g d) -> p g d", d=d)
                op("act", lambda e: e.activation(out=sq3, in_=src3, func=AF.Square), [b_src], [b_sq])
                yield
                op("dve", lambda e: e.reduce_sum(out=ssx[:, 0:G], in_=sq3, axis=AX.X), [b_sq], [b_ssx])
                yield
                rstd_inplace(ssx[:, 0:G], b_ssx, float(d))
                yield
                tt("dve", xn3, src3, ssx[:, 0:G].unsqueeze(2).to_broadcast([128, G, d]), ALU.mult, [b_src, b_ssx], [b_xn])
                yield
                tt("pool", xn3, xn3, gains, ALU.mult, [b_xn, b_g], [b_xn])
                yield
            else:
                cp("act", xn3, src3, [b_src], [b_xn])
                yield
            x1 = xn3[:, :, 0:half]
            x2 = xn3[:, :, half:rot]
            cb = cos.unsqueeze(1).to_broadcast([128, G, half])
            sb_ = sin.unsqueeze(1).to_broadcast([128, G, half])
            tv = []
            for i in range(4):
                tq, b_tq = scr["t%d" % i]
                tv.append((tq[:, 0:G * half].rearrange("p (g d) -> p g d", d=half), b_tq))
            tt("dve", tv[0][0], x1, cb, ALU.mult, [b_xn, b_rope], [tv[0][1]])
            tt("pool", tv[1][0], x2, sb_, ALU.mult, [b_xn, b_rope], [tv[1][1]])
            tt("pool", tv[2][0], x1, sb_, ALU.mult, [b_xn, b_rope], [tv[2][1]])
            tt("dve", tv[3][0], x2, cb, ALU.mult, [b_xn, b_rope], [tv[3][1]])
            if rot < d:
                cp("act", out3[:, :, rot:d], xn3[:, :, rot:d], [b_xn], [b_out])
            yield
            tt("dve", out3[:, :, 0:half], tv[0][0], tv[1][0], ALU.subtract, [tv[0][1], tv[1][1]], [b_out])
            tt("pool", out3[:, :, half:rot], tv[2][0], tv[3][0], ALU.add, [tv[2][1], tv[3][1]], [b_out])
            yield

        def make_scr(ph, tag, n):
            out = []
            for i in range(n):
                s = {}
                s["sq"] = sbuf(ph, f"{tag}sq{i}", [128, 256], F32)
                s["xn"] = sbuf(ph, f"{tag}xn{i}", [128, 256], F32)
                s["ss"] = sbuf(ph, f"{tag}ss{i}", [128, 8], F32)
                for j in range(4):
                    s["t%d" % j] = sbuf(ph, f"{tag}t{j}_{i}", [128, 96], F32)
                out.append(s)
            return out

        def gate_and_store(t, on2d, b_on, Wg, b_Wg, n, ocol, gsc, b_gsc, ofin, b_ofin):
            psg, b_psg = ps_mm()
            proj(tcols(t), Wg, b_Wg, n, psg, b_psg)
            yield
            op("act", lambda e: e.activation(out=gsc[:, 0:n], in_=psg[:, 0:n], func=AF.Exp, scale=-1.0), [b_psg], [b_gsc])
            yield
            ts("pool", gsc[:, 0:n], gsc[:, 0:n], 1.0, ALU.add, [b_gsc], [b_gsc])
            yield
            op("dve", lambda e: e.reciprocal(out=gsc[:, 0:n], in_=gsc[:, 0:n]), [b_gsc], [b_gsc])
            yield
            tt("dve", gsc[:, 0:n], psg[:, 0:n], gsc[:, 0:n], ALU.mult, [b_psg, b_gsc], [b_gsc])
            yield
            tt("dve", ofin[:, 0:n], on2d, gsc[:, 0:n], ALU.mult, [b_on, b_gsc], [b_ofin])
            yield
            P.dma("ofin_" + b_ofin.name, o_scr[t * 128:(t + 1) * 128, ocol:ocol + n], ofin[:, 0:n], reads=[b_ofin], writes=[obufs[t]])
            yield

        xbufs = [Buf(f"x{t}") for t in range(NT)]
        obufs = [Buf(f"o{t}") for t in range(NT)]

        for l in range(L):
            lam_init = 0.8 - 0.6 * math.exp(-0.3 * l)
            xsrc = x_in if l == 0 else y_out
            P.dma("spt", spt[:], smallp[l:l + 1, :].partition_broadcast(128), writes=[b_spt])
            cp("dve", gA[:, 0:4, :], spt[:, SP_QNA:SP_QNA + 32].unsqueeze(1).to_broadcast([128, 4, 32]), [b_spt], [b_gA])
            cp("dve", gA[:, 4:8, :], spt[:, SP_KNA:SP_KNA + 32].unsqueeze(1).to_broadcast([128, 4, 32]), [b_spt], [b_gA])
            cp("dve", gB[:, 0:2, :], spt[:, SP_QNB:SP_QNB + 64].unsqueeze(1).to_broadcast([128, 2, 64]), [b_spt], [b_gB])
            cp("dve", gB[:, 2:4, :], spt[:, SP_KNB:SP_KNB + 64].unsqueeze(1).to_broadcast([128, 2, 64]), [b_spt], [b_gB])
            ts("dve", gS[:], spt[:, SP_SUB:SP_SUB + 64].unsqueeze(1).to_broadcast([128, 2, 64]), 1.0 - lam_init, ALU.mult, [b_spt], [b_gS])
            tt("dve", decs[:, 0:32], spt[:, SP_LQ1:SP_LQ1 + 32], spt[:, SP_LK1:SP_LK1 + 32], ALU.mult, [b_spt], [b_decs])
            op("dve", lambda e: e.reduce_sum(out=lamt[:, 0:1], in_=decs[:, 0:32], axis=AX.X), [b_decs], [b_lam])
            tt("dve", decs[:, 32:64], spt[:, SP_LQ2:SP_LQ2 + 32], spt[:, SP_LK2:SP_LK2 + 32], ALU.mult, [b_spt], [b_decs])
            op("dve", lambda e: e.reduce_sum(out=lamt[:, 1:2], in_=decs[:, 32:64], axis=AX.X), [b_decs], [b_lam])
            op("act", lambda e: e.activation(out=lamt[:, 2:4], in_=lamt[:, 0:2], func=AF.Exp), [b_lam], [b_lam])
            tt("dve", lamt[:, 4:5], lamt[:, 3:4], lamt[:, 2:3], ALU.subtract, [b_lam], [b_lam])
            ts("dve", lamt[:, 6:7], lamt[:, 4:5], -lam_init, ALU.add, [b_lam], [b_lam])
            op("act", lambda e: e.activation(out=decs[:, 0:8], in_=spt[:, SP_DEC:SP_DEC + 8], func=AF.Exp, scale=-1.0), [b_spt, b_decs], [b_decs])
            ts("dve", decs[:, 0:8], decs[:, 0:8], 1.0, ALU.add, [b_decs], [b_decs])
            op("act", lambda e: e.activation(out=decs[:, 0:8], in_=decs[:, 0:8], func=AF.Ln), [b_decs], [b_decs])
            ts("dve", decs[:, 0:8], decs[:, 0:8], -1.0, ALU.mult, [b_decs], [b_decs])
            for (o0, i0, rc) in ((8, 0, 513), (12, 4, 515), (16, 0, 512), (20, 4, 514)):
                op("act", lambda e, o0=o0, i0=i0, rc=rc: e.activation(out=decs[:, o0:o0 + 4], in_=decs[:, i0:i0 + 4], func=AF.Exp, scale=retc[:, rc:rc + 1]),
                   [b_decs, b_retc], [b_decs])
            op("act", lambda e: e.activation(out=decs[:, 24:32], in_=decs[:, 0:8], func=AF.Exp, scale=128.0), [b_decs], [b_decs])
            with ExitStack() as ph:
                dtmp, b_dtmp = sbuf(ph, "dtmp", [128, 128], F32)
                for h in range(4):
                    op("act", lambda e, h=h: e.activation(out=DT[:, h, :], in_=retc[:, 0:128], func=AF.Exp, scale=decs[:, h:h + 1]), [b_decs, b_retc], [b_DT])
                    tt("dve", DT[:, h, :], DT[:, h, :], retc[:, 128:256], ALU.mult, [b_DT, b_retc], [b_DT])
                    op("act", lambda e, h=h: e.activation(out=dtmp[:], in_=retc[:, 256:384], func=AF.Exp, scale=decs[:, 4 + h:5 + h]), [b_decs, b_retc], [b_dtmp])
                    tt("dve", dtmp[:], dtmp[:], retc[:, 384:512], ALU.mult, [b_dtmp, b_retc], [b_dtmp])
                    tt("dve", DT[:, h, :], DT[:, h, :], dtmp[:], ALU.add, [b_dtmp, b_DT], [b_DT])
                P.barrier()

            ph_norm = ExitStack()
            gs_bc, b_gs = sbuf(ph_norm, "gs_bc", [128, 1024], F32)
            shift_bc, b_shift = sbuf(ph_norm, "shift_bc", [128, 1024], F32)
            ph_ada = ExitStack()
            modrow, b_modrow = sbuf(ph_ada, "modrow", [1, 3072], F32)
            bada, b_bada = sbuf(ph_ada, "bada", [1, 3072], F32)
            P.dma("bada", bada[:], b_ada[l:l + 1, :], writes=[b_bada])
            for nt in range(12):
                i = state["ws"]
                state["ws"] ^= 1
                stg, b_stg = wstage[i]
                P.dma("ws%d" % i, stg[:, :, :], w_ada[l, :, nt * 256:(nt + 1) * 256].rearrange("(k p) n -> p k n", p=128), writes=[b_stg])
                ps, b_ps = ps_mm()
                for k in range(8):
                    op("pe", lambda e, k=k, stg=stg, ps=ps: e.matmul(ps[0:1, 0:256], lhsT=cs[:, k:k + 1], rhs=stg[:, k, :], start=(k == 0), stop=(k == 7)),
                       [b_cs, b_stg], [b_ps])
                tt("dve", modrow[0:1, nt * 256:(nt + 1) * 256], ps[0:1, 0:256], bada[0:1, nt * 256:(nt + 1) * 256], ALU.add, [b_ps, b_bada], [b_modrow])
            with ExitStack() as ph:
                g_bc, b_gbc = sbuf(ph, "g_bc", [128, 1024], F32)
                P.dma("gbc", g_bc[:], norm_g[l:l + 1, :].partition_broadcast(128), writes=[b_gbc])
                for j in range(6):
                    ps, b_ps = ps_mm()
                    op("pe", lambda e, ps=ps, j=j: e.matmul(ps[:, :], lhsT=ones_row[0:1, :], rhs=modrow[0:1, j * 512:(j + 1) * 512], start=True, stop=True),
                       [b_ones, b_modrow], [b_ps])
                    sl = slice((j % 2) * 512, (j % 2 + 1) * 512)
                    if j < 2:
                        cp("act", shift_bc[:, sl], ps[:, :], [b_ps], [b_shift])
                    elif j < 4:
                        op("dve", lambda e, ps=ps, sl=sl: e.scalar_tensor_tensor(out=gs_bc[:, sl], in0=ps[:, :], scalar=1.0, in1=g_bc[:, sl], op0=ALU.add, op1=ALU.mult),
                           [b_ps, b_gbc], [b_gs])
                    else:
                        cp("act", gate_bc[:, sl], ps[:, :], [b_ps], [b_gate])
                P.barrier()
            ph_ada.close()

            with ExitStack() as ph:
                KN = 3
                set_pools(mm=[0, 1, 2, 3, 4, 5], tb=[5, 6, 7])
                xt = [sbuf(ph, f"xt{i}", [128, 1024], F32) for i in range(KN)]
                junk, b_junk = sbuf(ph, "junk", [128, 1024], BF16)
                h1 = [sbuf(ph, f"h1_{i}", [128, 1024], F32) for i in range(KN)]
                hb = [sbuf(ph, f"hb{i}", [128, 1024], BF16) for i in range(KN)]
                ssn = [sbuf(ph, f"ssn{i}", [128, 1], F32) for i in range(KN)]

                def norm_body(t):
                    x_t, b_x = xt[t % KN]
                    h_t, b_h = h1[t % KN]
                    hb_t, b_hb = hb[t % KN]
                    ss, b_ss = ssn[t % KN]
                    P.dma("xt%d" % (t % KN), x_t[:], xsrc[t * 128:(t + 1) * 128, :], reads=[xbufs[t]], writes=[b_x])
                    yield
                    op("act", lambda e: e.activation(out=junk[:], in_=x_t[:], func=AF.Square, accum_out=ss[:]), [b_x], [b_junk, b_ss])
                    yield
                    rstd_inplace(ss[:], b_ss, 1024.0)
                    yield
                    op("dve", lambda e: e.scalar_tensor_tensor(out=h_t[:], in0=x_t[:], scalar=ss[:, 0:1], in1=gs_bc[:], op0=ALU.mult, op1=ALU.mult),
                       [b_x, b_ss, b_gs], [b_h])
                    yield
                    tt("pool", hb_t[:], h_t[:], shift_bc[:], ALU.add, [b_h, b_shift], [b_hb])
                    yield
                    pb, b_pb = ps_tb()
                    for k in range(8):
                        op("pe", lambda e, k=k: e.transpose(pb[:, k * 128:(k + 1) * 128], hb_t[:, k * 128:(k + 1) * 128], ident_b[:]),
                           [b_hb, b_idb], [b_pb])
                    yield
                    cp("act" if t % 2 else "dve", hT[:, :, tcols(t)], pb[:, :].rearrange("p (k n) -> p k n", n=128), [b_pb], [b_hT])
                    yield

                interleave((norm_body(t) for t in range(NT)), KN)
                P.barrier()
            ph_norm.close()
            if dbg == "hT":
                with ExitStack() as ph:
                    d32, b_d32 = sbuf(ph, "d32", [128, 8 * S], F32)
                    cp("dve", d32[:], hT[:].rearrange("p k s -> p (k s)"), [b_hT], [b_d32])
                    P.dma("dbg", dbg_hT[:, :], d32[:], reads=[b_d32], writes=[xbufs[0]])
                break

            for c in range(2 if "A" in stages else 0):
                with ExitStack() as ph:
                    Wqk, b_Wqk = sbuf(ph, "A_Wqk", [128, 8, 256], BF16)
                    Wv, b_Wv = sbuf(ph, "A_Wv", [128, 8, 128], BF16)
                    Wg, b_Wg = sbuf(ph, "A_Wg", [128, 8, 128], BF16)
                    wl = w_in[l]
                    load_w(Wqk, b_Wqk, wl[:, OFF_QA + 128 * c:OFF_QA + 128 * c + 128], 128, 0)
                    load_w(Wqk, b_Wqk, wl[:, OFF_KA + 128 * c:OFF_KA + 128 * c + 128], 128, 128)
                    load_w(Wv, b_Wv, wl[:, OFF_VA + 128 * c:OFF_VA + 128 * c + 128], 128)
                    load_w(Wg, b_Wg, wl[:, OFF_GA + 128 * c:OFF_GA + 128 * c + 128], 128)
                    qT, b_qT = sbuf(ph, "A_qT", [128, S], BF16)
                    kps = [sbuf(ph, f"A_kp{j_}", [128, S], BF16) for j_ in range(4)]
                    vaug, b_vaug = sbuf(ph, "A_vaug", [128, NT, 2, 66], BF16)
                    if not DBG.get("A_nomemset"):
                        op("pool", lambda e: e.memset(vaug[:, :, :, 64:65], 1.0), [], [b_vaug])
                    ph2 = ExitStack()
                    KA = 3
                    set_pools(mm=[0, 1, 2, 3, 4], tb=[5, 6, 7])
                    scr = make_scr(ph2, "A", KA)
                    qkb = [sbuf(ph2, f"A_qkb{i}", [128, 256], BF16) for i in range(KA)]

                    def a_proj_body(t):
                        ps, b_ps = ps_mm()
                        proj(tcols(t), Wqk, b_Wqk, 256, ps, b_ps)
                        yield
                        qk_t, b_qk = qkb[t % KA]
                        yield from norm_rope(ps[:, 0:256].rearrange("p (g d) -> p g d", d=32), b_ps, 8, 32, gA[:], b_gA, 4,
                                             rope[:, t, 0:4], rope[:, t, 4:8], qk_t[:].rearrange("p (g d) -> p g d", d=32), b_qk, scr[t % KA])
                        pb, b_pb = ps_tb()
                        for j in range(2):
                            op("pe", lambda e, j=j: e.transpose(pb[:, j * 128:(j + 1) * 128], qk_t[:, j * 128:(j + 1) * 128], ident_b[:]),
                               [b_qk, b_idb], [b_pb])
                        yield
                        cp("act", qT[:, tcols(t)], pb[:, 0:128], [b_pb], [b_qT])
                        for j_ in range(4):
                            ts("dve", kps[j_][0][:, tcols(t)], pb[:, 128:256], kmask[:, j_:j_ + 1], ALU.mult, [b_pb, b_kmask], [kps[j_][1]])
                        yield
                        psv, b_psv = ps_mm()
                        proj(tcols(t), Wv, b_Wv, 128, psv, b_psv)
                        yield
                        cp("act", vaug[:, t, :, 0:64], psv[:, 0:128].rearrange("p (h d) -> p h d", d=64), [b_psv], [b_vaug])
                        yield

                    interleave((a_proj_body(t) for t in range(NT)), KA)
                    P.barrier()
                    ph2.close()
                    set_pools(mm=[0], st=[1, 2, 3], ot=[4, 5, 6, 7])
                    Et = [sbuf(ph, f"A_E{i}", [128, 512], BF16) for i in range(4)]
                    OTs = [sbuf(ph, f"A_OTs{i}", [65, 2, 512], F32) for i in range(2)]
                    opre = [sbuf(ph, f"A_opre{i}", [128, 4, 2, 64], F32) for i in range(2)]
                    KE = 3
                    tA = [sbuf(ph, f"A_tA{i}", [128, 64], F32) for i in range(KE)]
                    tB = [sbuf(ph, f"A_tB{i}", [128, 64], F32) for i in range(KE)]
                    rden = [sbuf(ph, f"A_rden{i}", [128, 2], F32) for i in range(KE)]
                    osq = [sbuf(ph, f"A_osq{i}", [128, 128], F32) for i in range(KE)]
                    oss = [sbuf(ph, f"A_oss{i}", [128, 2], F32) for i in range(KE)]
                    onn = [sbuf(ph, f"A_on{i}", [128, 128], F32) for i in range(KE)]
                    gsc = [sbuf(ph, f"A_gsc{i}", [128, 128], F32) for i in range(KE)]
                    ofin = [sbuf(ph, f"A_ofin{i}", [128, 128], BF16) for i in range(KE)]
                    side = []
                    cnt_e = [0]

                    def a_epi(g, hl, ots, b_ots):
                        for t4 in range(4):
                            yield from a_epi_tile(g, hl, ots, b_ots, t4)

                    def a_epi_tile(g, hl, ots, b_ots, t4):
                        opre_, b_opre = opre[g % 2]
                        if True:
                            i_ = cnt_e[0] % KE
                            cnt_e[0] += 1
                            psT, b_psT = ps_mm()
                            for m in range(2):
                                op("pe", lambda e, m=m: e.transpose(psT[:, m * 65:(m + 1) * 65], ots[0:65, m, t4 * 128:(t4 + 1) * 128], ident_f[0:65, 0:65]),
                                   [b_ots, b_idf], [b_psT])
                            yield
                            rd, b_rd = rden[i_]
                            ta, b_ta = tA[i_]
                            tb_, b_tb = tB[i_]
                            op("dve", lambda e: e.reciprocal(out=rd[:], in_=psT[:, 0:130].rearrange("p (m e) -> p m e", e=65)[:, :, 64]), [b_psT], [b_rd])
                            yield
                            ts("dve", ta[:], psT[:, 0:64], rd[:, 0:1], ALU.mult, [b_psT, b_rd], [b_ta])
                            yield
                            op("act", lambda e: e.activation(out=tb_[:], in_=psT[:, 65:129], func=AF.Copy, scale=rd[:, 1:2]), [b_psT, b_rd], [b_tb])
                            yield
                            op("dve", lambda e: e.scalar_tensor_tensor(out=opre_[:, t4, hl, :], in0=tb_[:], scalar=lamt[:, 6:7], in1=ta[:],
                                                                       op0=ALU.mult, op1=ALU.add), [b_tb, b_ta, b_lam], [b_opre])
                            yield

                    def a_fin(g):
                        for t4 in range(4):
                            yield from a_fin_tile(g, t4)

                    def a_fin_tile(g, t4):
                        opre_, b_opre = opre[g % 2]
                        if True:
                            t = g * 4 + t4
                            i_ = cnt_e[0] % KE
                            cnt_e[0] += 1
                            sq_, b_sq = osq[i_]
                            ss_, b_ss = oss[i_]
                            on_, b_on = onn[i_]
                            o2 = opre_[:, t4, :, :]
                            tt("pool", sq_[:].rearrange("p (h d) -> p h d", d=64), o2, o2, ALU.mult, [b_opre], [b_sq])
                            yield
                            op("dve", lambda e: e.reduce_sum(out=ss_[:], in_=sq_[:].rearrange("p (h d) -> p h d", d=64), axis=AX.X), [b_sq], [b_ss])
                            yield
                            rstd_inplace(ss_[:], b_ss, 64.0)
                            yield
                            on3 = on_[:].rearrange("p (h d) -> p h d", d=64)
                            tt("dve", on3, o2, ss_[:].unsqueeze(2).to_broadcast([128, 2, 64]), ALU.mult, [b_opre, b_ss], [b_on])
                            yield
                            tt("pool", on3, on3, gS[:], ALU.mult, [b_on, b_gS], [b_on])
                            yield
                            yield from gate_and_store(t, on_[:], b_on, Wg, b_Wg, 128, OCOL_A + 128 * c, gsc[i_][0], gsc[i_][1], ofin[i_][0], ofin[i_][1])

                    def a_main():
                        LA = 2
                        items = [(g, hl, m, kt) for g in range(S // 512) for hl in range(2) for m in range(2) for kt in range(NT)]
                        pend = []
                        otmap = {}
                        for idx in range(len(items) + LA):
                            if idx < len(items):
                                g, hl, m, kt = items[idx]
                                kp, b_kp = kps[2 * hl + m]
                                stp, b_st = ps_st()
                                op("pe", lambda e, stp=stp, kp=kp, kt=kt, g=g: e.matmul(
                                    stp[:, :], lhsT=kp[:, tcols(kt)], rhs=qT[:, g * 512:(g + 1) * 512], start=True, stop=True),
                                   [b_kp, b_qT], [b_st])
                                E, b_E = Et[idx % 4]
                                op("act", lambda e, E=E, stp=stp: e.activation(out=E[:], in_=stp[:], func=AF.Exp, scale=32 ** -0.5), [b_st], [b_E])
                                pend.append((g, hl, m, kt, E, b_E))
                            if idx >= LA:
                                g, hl, m, kt, E, b_E = pend.pop(0)
                                if kt == 0:
                                    otmap[(g, hl, m)] = ps_ot()
                                ot, b_ot = otmap[(g, hl, m)]
                                op("pe", lambda e, ot=ot, kt=kt, hl=hl, E=E: e.matmul(ot[0:65, :], lhsT=vaug[:, kt, hl, 0:65], rhs=E[:], start=(kt == 0), stop=(kt == NT - 1)),
                                   [b_vaug, b_E], [b_ot])
                                if kt == NT - 1:
                                    slot = (g * 2 + hl) % 2
                                    ots, b_ots = OTs[slot]
                                    while any(tg == slot for tg, _ in side):
                                        try:
                                            next(side[0][1])
                                        except StopIteration:
                                            side.pop(0)
                                    cp("dve", ots[0:65, m, :], ot[0:65, :], [b_ot], [b_ots])
                                    if m == 1:
                                        side.append((slot, a_epi(g, hl, ots, b_ots)))
                                        if hl == 1:
                                            side.append((None, a_fin(g)))
                            yield

                    for _ in a_main():
                        if side:
                            try:
                                next(side[0][1])
                            except StopIteration:
                                side.pop(0)
                    while side:
                        try:
                            next(side[0][1])
                        except StopIteration:
                            side.pop(0)
                    P.barrier()

            for c in range(3 if "B" in stages else 0):
                with ExitStack() as ph:
                    Wqk, b_Wqk = sbuf(ph, "B_Wqk", [128, 8, 256], BF16)
                    Wv, b_Wv = sbuf(ph, "B_Wv", [128, 8, 128], BF16)
                    Wg, b_Wg = sbuf(ph, "B_Wg", [128, 8, 128], BF16)
                    wl = w_in[l]
                    load_w(Wqk, b_Wqk, wl[:, OFF_QB + 128 * c:OFF_QB + 128 * c + 128], 128, 0)
                    load_w(Wqk, b_Wqk, wl[:, OFF_KB + 128 * c:OFF_KB + 128 * c + 128], 128, 128)
                    load_w(Wv, b_Wv, wl[:, OFF_VB + 128 * c:OFF_VB + 128 * c + 128], 128)
                    load_w(Wg, b_Wg, wl[:, OFF_GB + 128 * c:OFF_GB + 128 * c + 128], 128)
                    qT, b_qT = sbuf(ph, "B_qT", [128, S], BF16)
                    kT, b_kT = sbuf(ph, "B_kT", [128, S], BF16)
                    vB, b_vB = sbuf(ph, "B_vB", [128, 3, NT, 2, 66], BF16)
                    op("pool", lambda e: e.memset(vB[:, :, :, :, 64:65].rearrange("p a b c d -> p (a b c d)"), 1.0), [], [b_vB])
                    ph2 = ExitStack()
                    KB = 3
                    set_pools(mm=[0, 1, 2, 3, 4], tb=[5, 6, 7])
                    scr = make_scr(ph2, "B", KB)
                    qkb = [sbuf(ph2, f"B_qkb{i}", [128, 256], BF16) for i in range(KB)]

                    def b_proj_body(t):
                        ps, b_ps = ps_mm()
                        proj(tcols(t), Wqk, b_Wqk, 256, ps, b_ps)
                        yield
                        qk_t, b_qk = qkb[t % KB]
                        yield from norm_rope(ps[:, 0:256].rearrange("p (g d) -> p g d", d=64), b_ps, 4, 64, gB[:], b_gB, 8,
                                             rope[:, t, 8:16], rope[:, t, 16:24], qk_t[:].rearrange("p (g d) -> p g d", d=64), b_qk, scr[t % KB])
                        pb, b_pb = ps_tb()
                        for j in range(2):
                            op("pe", lambda e, j=j: e.transpose(pb[:, j * 128:(j + 1) * 128], qk_t[:, j * 128:(j + 1) * 128], ident_b[:]),
                               [b_qk, b_idb], [b_pb])
                        yield
                        cp("act", qT[:, tcols(t)], pb[:, 0:128], [b_pb], [b_qT])
                        cp("act", kT[:, tcols(t)], pb[:, 128:256], [b_pb], [b_kT])
                        yield

                    interleave((b_proj_body(t) for t in range(NT)), KB)
                    vi = 0
                    for gi, D in enumerate((1, 4, 16)):
                        ntl = S // D // 128
                        for r in range(D):
                            for j in range(ntl):
                                psv, b_psv = ps_mm()
                                c0 = r + D * 128 * j
                                proj(slice(c0, c0 + D * 127 + 1, D), Wv, b_Wv, 128, psv, b_psv)
                                cp("act" if vi % 2 else "dve", vB[:, gi, r * ntl + j, :, 0:64], psv[:, 0:128].rearrange("p (h d) -> p h d", d=64), [b_psv], [b_vB])
                                vi += 1
                    P.barrier()
                    ph2.close()
                    set_pools(mm=[0, 7], st=[1, 2, 3], ot=[4, 5, 6])
                    Et = [sbuf(ph, f"B_E{i}", [128, 384], BF16) for i in range(4)]
                    accs = [sbuf(ph, f"B_acc{i}", [65, 2048], F32) for i in range(2)]
                    ob, b_ob = sbuf(ph, "B_ob", [128, 16, 2, 64], F32)
                    KE = 3
                    rden = [sbuf(ph, f"B_rden{i}", [128, 1], F32) for i in range(KE)]
                    gsc = [sbuf(ph, f"B_gsc{i}", [128, 128], F32) for i in range(KE)]
                    ofin = [sbuf(ph, f"B_ofin{i}", [128, 128], BF16) for i in range(KE)]
                    side = []
                    cnt_e = [0]

                    def b_epi_tile(hl, acc, b_acc, t16):
                        i_ = cnt_e[0] % KE
                        cnt_e[0] += 1
                        psT, b_psT = ps_mm()
                        op("pe", lambda e: e.transpose(psT[:, 0:65], acc[0:65, t16 * 128:(t16 + 1) * 128], ident_f[0:65, 0:65]), [b_acc, b_idf], [b_psT])
                        yield
                        rd, b_rd = rden[i_]
                        op("dve", lambda e: e.reciprocal(out=rd[:], in_=psT[:, 64:65]), [b_psT], [b_rd])
                        yield
                        ts("dve", ob[:, t16, hl, :], psT[:, 0:64], rd[:, 0:1], ALU.mult, [b_psT, b_rd], [b_ob])
                        yield

                    def b_epi(hl, acc, b_acc):
                        for t16 in range(16):
                            yield from b_epi_tile(hl, acc, b_acc, t16)

                    def b_fin(u):
                        for t16 in range(16):
                            i_ = cnt_e[0] % KE
                            cnt_e[0] += 1
                            yield from gate_and_store(u * 16 + t16, ob[:, t16, :, :].rearrange("p h d -> p (h d)"), b_ob, Wg, b_Wg, 128, OCOL_B + 128 * c,
                                                      gsc[i_][0], gsc[i_][1], ofin[i_][0], ofin[i_][1])

                    def drain(cond):
                        while any(cond(tg) for tg, _ in side):
                            try:
                                next(side[0][1])
                            except StopIteration:
                                side.pop(0)

                    def b_main():
                        LA = 2
                        items = []
                        for u in range(S // 2048):
                            for hl in range(2):
                                for gi, D in enumerate((1, 4, 16)):
                                    per = 16 // D
                                    for r in range(D):
                                        for i in range(u * per, (u + 1) * per):
                                            items.append((u, hl, gi, D, r, i))
                        pend = []
                        for idx in range(len(items) + LA):
                            if idx < len(items):
                                u, hl, gi, D, r, i = items[idx]
                                base = 64 * hl
                                ntl = S // D // 128
                                qc0 = r + D * 128 * i
                                qcols = slice(qc0, qc0 + D * 127 + 1, D)
                                js = [j for j in (i - 1, i, i + 1) if 0 <= j < ntl]
                                stp, b_st = ps_st()
                                for j in js:
                                    jj = j - (i - 1)
                                    kc0 = r + D * 128 * j
                                    op("pe", lambda e, stp=stp, jj=jj, kc0=kc0, D=D, base=base, qcols=qcols: e.matmul(
                                        stp[:, jj * 128:(jj + 1) * 128], lhsT=kT[base:base + 64, slice(kc0, kc0 + D * 127 + 1, D)], rhs=qT[base:base + 64, qcols],
                                        start=True, stop=True), [b_kT, b_qT], [b_st])
                                lo = (js[0] - (i - 1)) * 128
                                hi = (js[-1] - (i - 1) + 1) * 128
                                E, b_E = Et[idx % 4]
                                op("act", lambda e, E=E, stp=stp, lo=lo, hi=hi: e.activation(out=E[:, lo:hi], in_=stp[:, lo:hi], func=AF.Exp, scale=0.125), [b_st], [b_E])
                                tt("pool" if idx % 3 == 0 else "dve", E[:, lo:hi], E[:, lo:hi], maskB[:, lo:hi], ALU.mult, [b_E, b_maskB], [b_E])
                                pend.append((u, hl, gi, D, r, i, js, E, b_E, qc0, idx))
                            if idx >= LA:
                                u, hl, gi, D, r, i, js, E, b_E, qc0, idx0 = pend.pop(0)
                                ntl = S // D // 128
                                slot = (u * 2 + hl) % 2
                                acc, b_acc = accs[slot]
                                first_item = (gi == 0 and r == 0 and i == u * 16)
                                if first_item:
                                    drain(lambda tg: tg == slot)
                                ot, b_ot = ps_ot()
                                for j in js:
                                    jj = j - (i - 1)
                                    op("pe", lambda e, ot=ot, gi=gi, tile=r * ntl + j, hl=hl, E=E, jj=jj, first=(j == js[0]), last=(j == js[-1]): e.matmul(
                                        ot[0:65, 0:128], lhsT=vB[:, gi, tile, hl, 0:65], rhs=E[:, jj * 128:(jj + 1) * 128], start=first, stop=last),
                                       [b_vB, b_E], [b_ot])
                                a0 = qc0 - 2048 * u
                                acc_ap = acc[0:65, a0:a0 + D * 127 + 1:D]
                                if gi == 0:
                                    cp("dve", acc_ap, ot[0:65, 0:128], [b_ot], [b_acc])
                                else:
                                    tt("dve", acc_ap, ot[0:65, 0:128], acc_ap, ALU.add, [b_ot, b_acc], [b_acc])
                                last_item = (gi == 2 and r == 15 and i == u)
                                if last_item:
                                    side.append((slot, b_epi(hl, acc, b_acc)))
                                    if hl == 1:
                                        side.append((None, b_fin(u)))
                            yield

                    for _ in b_main():
                        if side:
                            try:
                                next(side[0][1])
                            except StopIteration:
                                side.pop(0)
                    drain(lambda tg: True)
                    P.barrier()

            for c in range(2 if "C" in stages else 0):
                with ExitStack() as ph:
                    Wqk, b_Wqk = sbuf(ph, "C_Wqk", [128, 8, 192], BF16)
                    Wv, b_Wv = sbuf(ph, "C_Wv", [128, 8, 192], BF16)
                    Wg, b_Wg = sbuf(ph, "C_Wg", [128, 8, 192], BF16)
                    wl = w_in[l]
                    load_w(Wqk, b_Wqk, wl[:, OFF_QC + 96 * c:OFF_QC + 96 * c + 96], 96, 0)
                    load_w(Wqk, b_Wqk, wl[:, OFF_KC + 96 * c:OFF_KC + 96 * c + 96], 96, 96)
                    load_w(Wv, b_Wv, wl[:, OFF_VC + 192 * c:OFF_VC + 192 * c + 192], 192)
                    load_w(Wg, b_Wg, wl[:, OFF_GC + 192 * c:OFF_GC + 192 * c + 192], 192)
                    qkT, b_qkT = sbuf(ph, "C_qkT", [64, 4, S], BF16)
                    vC, b_vC = sbuf(ph, "C_vC", [128, NT, 2, 96], BF16)
                    Rst, b_Rst = sbuf(ph, "C_Rst", [64, NT, 2, 2, 96], BF16)
                    ph_mid = ExitStack()
                    kfb, b_kfb = sbuf(ph_mid, "C_kfb", [128, NT, 2, 2, 48], BF16)
                    Rs = [sbuf(ph_mid, f"C_R{i}", [64, 2, 96], F32) for i in range(2)]
                    cdt = [sbuf(ph_mid, f"C_cd{i}", [64, 2, 96], F32) for i in range(2)]
                    ph2 = ExitStack()
                    KC = 2
                    set_pools(mm=[0, 1, 2, 3, 4], tb=[5, 6, 7])
                    scr = make_scr(ph2, "C", KC)
                    qkr = [sbuf(ph2, f"C_qkr{i}", [128, 4, 48], F32) for i in range(KC)]
                    qkp = [sbuf(ph2, f"C_qkp{i}", [128, 4, 64], BF16) for i in range(KC)]
                    for i in range(KC):
                        op("pool", lambda e, i=i: e.memset(qkp[i][0][:].rearrange("p g d -> p (g d)"), 0.0), [], [qkp[i][1]])

                    def c_proj_body(t):
                        ps, b_ps = ps_mm()
                        proj(tcols(t), Wqk, b_Wqk, 192, ps, b_ps)
                        yield
                        qr, b_qr = qkr[t % KC]
                        qp, b_qp = qkp[t % KC]
                        yield from norm_rope(ps[:, 0:192].rearrange("p (g d) -> p g d", d=48), b_ps, 4, 48, None, None, 24,
                                             rope[:, t, 24:48], rope[:, t, 48:72], qr[:], b_qr, scr[t % KC])
                        ts("pool", qr[:, 2:4, :], qr[:, 2:4, :], 48 ** -0.5, ALU.mult, [b_qr], [b_qr])
                        yield
                        cp("act", qp[:, :, 0:48], qr[:], [b_qr], [b_qp])
                        tt("dve", kfb[:, t, 0, :, :], qr[:, 2:4, :], decs[:, 8 + 2 * c:10 + 2 * c].unsqueeze(2).to_broadcast([128, 2, 48]), ALU.mult, [b_qr, b_decs], [b_kfb])
                        tt("pool", kfb[:, t, 1, :, :], qr[:, 2:4, :], decs[:, 12 + 2 * c:14 + 2 * c].unsqueeze(2).to_broadcast([128, 2, 48]), ALU.mult, [b_qr, b_decs], [b_kfb])
                        yield
                        pb, b_pb = ps_tb()
                        for s_ in range(4):
                            op("pe", lambda e, s_=s_: e.transpose(pb[0:64, s_ * 128:(s_ + 1) * 128], qp[:, s_, :], ident_b[:]), [b_qp, b_idb], [b_pb])
                        yield
                        cp("dve", qkT[:, :, tcols(t)], pb[0:64, 0:512].rearrange("p (g n) -> p g n", n=128), [b_pb], [b_qkT])
                        yield
                        psv, b_psv = ps_mm()
                        proj(tcols(t), Wv, b_Wv, 192, psv, b_psv)
                        yield
                        cp("act", vC[:, t, :, :], psv[:, 0:192].rearrange("p (h d) -> p h d", d=96), [b_psv], [b_vC])
                        yield

                    interleave((c_proj_body(t) for t in range(NT)), KC)
                    P.barrier()
                    ph2.close()
                    set_pools(mm=[0, 1, 2, 3], st=[1, 2, 3], ot=[4, 5, 6, 7])

                    def c_scan(d_):
                        R_, b_R = Rs[d_]
                        cd_, b_cd = cdt[d_]
                        op("pool", lambda e: e.memset(R_[:].rearrange("p h d -> p (h d)"), 0.0), [], [b_R])
                        cp("dve", cd_[0:48, :, :], decs[0:48, 24 + 4 * d_ + 2 * c:26 + 4 * d_ + 2 * c].unsqueeze(2).to_broadcast([48, 2, 96]), [b_decs], [b_cd])
                        yield
                        order = range(NT) if d_ == 0 else range(NT - 1, -1, -1)
                        for n in order:
                            psU, b_psU = ps_mm()
                            for hl in range(2):
                                op("pe", lambda e, psU=psU, n=n, hl=hl: e.matmul(psU[0:48, hl * 96:(hl + 1) * 96], lhsT=kfb[:, n, d_, hl, :], rhs=vC[:, n, hl, :], start=True, stop=True),
                                   [b_kfb, b_vC], [b_psU])
                            cp("dve", Rst[0:48, n, d_, :, :], R_[0:48, :, :], [b_R], [b_Rst])
                            yield
                            tt("dve", R_[0:48, :, :], R_[0:48, :, :], cd_[0:48, :, :], ALU.mult, [b_R, b_cd], [b_R])
                            yield
                            tt("dve", R_[0:48, :, :], psU[0:48, 0:192].rearrange("p (h d) -> p h d", d=96), R_[0:48, :, :], ALU.add, [b_psU, b_R], [b_R])
                            yield

                    interleave((c_scan(d_) for d_ in range(2)), 2)
                    P.barrier()
                    ph_mid.close()
                    set_pools(mm=[6, 7], st=[0, 1, 2], ot=[3, 4, 5])
                    KO = 2
                    att = [sbuf(ph, f"C_att{i}", [128, 128], BF16) for i in range(2 * KO)]
                    oc = [sbuf(ph, f"C_oc{i}", [128, 2, 96], F32) for i in range(KO)]
                    osq = [sbuf(ph, f"C_osq{i}", [128, 2, 96], F32) for i in range(KO)]
                    oss = [sbuf(ph, f"C_oss{i}", [128, 2], F32) for i in range(KO)]
                    gsc = [sbuf(ph, f"C_gsc{i}", [128, 192], F32) for i in range(KO)]
                    ofin = [sbuf(ph, f"C_ofin{i}", [128, 192], BF16) for i in range(KO)]

                    def c_out_head(n, hl, oc_, b_oc):
                        head = 2 * c + hl
                        stp, b_st = ps_st()
                        op("pe", lambda e: e.matmul(stp[:, 0:128], lhsT=qkT[0:48, 2 + hl, tcols(n)], rhs=qkT[0:48, hl, tcols(n)], start=True, stop=True),
                           [b_qkT], [b_st])
                        yield
                        at_, b_at = att[(n % KO) * 2 + hl]
                        tt("dve", at_[:], stp[:, 0:128], DT[:, head, :], ALU.mult, [b_st, b_DT], [b_at])
                        yield
                        psO, b_psO = ps_ot()
                        op("pe", lambda e: e.matmul(psO[:, 0:96], lhsT=at_[:], rhs=vC[:, n, hl, :], start=True, stop=True), [b_at, b_vC], [b_psO])
                        for d_ in range(2):
                            op("pe", lambda e, d_=d_: e.matmul(psO[:, 96 * (d_ + 1):96 * (d_ + 2)], lhsT=qkT[0:48, hl, tcols(n)], rhs=Rst[0:48, n, d_, hl, :],
                                                              start=True, stop=True), [b_qkT, b_Rst], [b_psO])
                        yield
                        cp("act", oc_[:, hl, :], psO[:, 0:96], [b_psO], [b_oc])
                        yield
                        for d_ in range(2):
                            op("dve", lambda e, d_=d_: e.scalar_tensor_tensor(
                                out=oc_[:, hl, :], in0=psO[:, 96 * (d_ + 1):96 * (d_ + 2)], scalar=decs[:, 16 + 4 * d_ + head:17 + 4 * d_ + head], in1=oc_[:, hl, :],
                                op0=ALU.mult, op1=ALU.add), [b_psO, b_decs, b_oc], [b_oc])
                            yield

                    def c_out_body(n):
                        oc_, b_oc = oc[n % KO]
                        for hl in range(2):
                            yield from c_out_head(n, hl, oc_, b_oc)
                        sq_, b_sq = osq[n % KO]
                        ss_, b_ss = oss[n % KO]
                        tt("pool", sq_[:], oc_[:], oc_[:], ALU.mult, [b_oc], [b_sq])
                        yield
                        op("dve", lambda e: e.reduce_sum(out=ss_[:], in_=sq_[:], axis=AX.X), [b_sq], [b_ss])
                        yield
                        rstd_inplace(ss_[:], b_ss, 96.0)
                        yield
                        tt("dve", sq_[:], oc_[:], ss_[:].unsqueeze(2).to_broadcast([128, 2, 96]), ALU.mult, [b_oc, b_ss], [b_sq])
                        yield
                        tt("pool", sq_[:], sq_[:], spt[:, SP_GNC:SP_GNC + 96].unsqueeze(1).to_broadcast([128, 2, 96]), ALU.mult, [b_sq, b_spt], [b_sq])
                        yield
                        yield from gate_and_store(n, sq_[:].rearrange("p h d -> p (h d)"), b_sq, Wg, b_Wg, 192, OCOL_C + 192 * c,
                                                  gsc[n % KO][0], gsc[n % KO][1], ofin[n % KO][0], ofin[n % KO][1])

                    interleave((c_out_body(n) for n in range(NT)), KO)
                    P.barrier()

            if dbg == "o":
                with ExitStack() as ph:
                    ob16, b_ob16 = sbuf(ph, "dbg_ob", [128, 1024], BF16)
                    o32, b_o32 = sbuf(ph, "dbg_o32", [128, 1024], F32)
                    for t in range(NT):
                        P.dma("dbg_l", ob16[:], o_scr[t * 128:(t + 1) * 128, :], reads=[obufs[t]], writes=[b_ob16])
                        cp("dve", o32[:], ob16[:], [b_ob16], [b_o32])
                        P.dma("dbg_s", dbg_o[t * 128:(t + 1) * 128, :], o32[:], reads=[b_o32], writes=[xbufs[t]])
                break

            if "O" in stages:
                with ExitStack() as ph:
                    Wout, b_Wout = sbuf(ph, "Wout", [128, 8, 1024], BF16)
                    load_w(Wout, b_Wout, w_out[l], 1024)
                    KQ = 3
                    set_pools(mm=[0, 1, 2, 3, 4], tb=[5, 6, 7])
                    ott = [sbuf(ph, f"O_ot{i}", [128, 1024], BF16) for i in range(KQ)]
                    oTt = [sbuf(ph, f"O_oT{i}", [128, 8, 128], BF16) for i in range(KQ)]
                    xt = [sbuf(ph, f"O_xt{i}", [128, 1024], F32) for i in range(KQ)]
                    tm = [sbuf(ph, f"O_tm{i}", [128, 1024], F32) for i in range(KQ)]

                    def o_half(oT_t, b_oT, tm_t, b_tm, nh):
                        ps, b_ps = ps_mm()
                        for k in range(8):
                            op("pe", lambda e, k=k: e.matmul(ps[:, :], lhsT=oT_t[:, k, :], rhs=Wout[:, k, nh * 512:(nh + 1) * 512], start=(k == 0), stop=(k == 7)),
                               [b_oT, b_Wout], [b_ps])
                        yield
                        sl = slice(nh * 512, (nh + 1) * 512)
                        tt("dve", tm_t[:, sl], ps[:, :], gate_bc[:, sl], ALU.mult, [b_ps, b_gate], [b_tm])
                        yield

                    def o_body(t):
                        o_t, b_o = ott[t % KQ]
                        oT_t, b_oT = oTt[t % KQ]
                        x_t, b_x = xt[t % KQ]
                        tm_t, b_tm = tm[t % KQ]
                        P.dma("O_ot%d" % (t % KQ), o_t[:], o_scr[t * 128:(t + 1) * 128, :], reads=[obufs[t]], writes=[b_o])
                        P.dma("O_xt%d" % (t % KQ), x_t[:], xsrc[t * 128:(t + 1) * 128, :], reads=[xbufs[t]], writes=[b_x])
                        yield
                        pb, b_pb = ps_tb()
                        for k in range(8):
                            op("pe", lambda e, k=k: e.transpose(pb[:, k * 128:(k + 1) * 128], o_t[:, k * 128:(k + 1) * 128], ident_b[:]), [b_o, b_idb], [b_pb])
                        yield
                        cp("act", oT_t[:], pb[:, :].rearrange("p (k n) -> p k n", n=128), [b_pb], [b_oT])
                        yield
                        for nh in range(2):
                            yield from o_half(oT_t, b_oT, tm_t, b_tm, nh)
                        tt("pool", tm_t[:], tm_t[:], x_t[:], ALU.add, [b_tm, b_x], [b_tm])
                        yield
                        P.dma("O_st%d" % (t % KQ), y_out[t * 128:(t + 1) * 128, :], tm_t[:], reads=[b_tm], writes=[xbufs[t]])
                        yield

                    interleave((o_body(t) for t in range(NT)), KQ)
                    P.barrier()

        P.barrier()
        P.emit()
    return nc, P


def host_constants(S):
    NT = S // 128
    pos = np.arange(S, dtype=np.float32)

    def tab(theta, rot):
        inv = (np.float32(theta) ** (-np.arange(0, rot, 2, dtype=np.float32) / np.float32(rot))).astype(np.float32)
        ang = (pos[:, None] * inv[None, :]).astype(np.float32)
        return np.cos(ang).astype(np.float32), np.sin(ang).astype(np.float32)

    cA, sA = tab(ROT_THETA, 8)
    cB, sB = tab(ROT_THETA, 16)
    cC, sC = tab(RET_THETA, 48)
    rope = np.concatenate([cA, sA, cB, sB, cC, sC], axis=1)
    rope = np.ascontiguousarray(rope.reshape(NT, 128, 72).transpose(1, 0, 2)).astype(np.float32)
    a = np.arange(128)[:, None]
    b = np.arange(128)[None, :]
    maskB = np.concatenate([(a - b >= 64), (np.abs(a - b) <= 64), (b - a >= 64)], axis=1).astype(np.float32).astype(ml_dtypes.bfloat16)
    j = a
    i = b
    retc = np.zeros((128, 516), np.float32)
    retc[:, 0:128] = np.maximum(i - j, 0)
    retc[:, 128:256] = (i >= j)
    retc[:, 256:384] = np.maximum(j - i, 0)
    retc[:, 384:512] = (j > i)
    p = np.arange(128, dtype=np.float32)
    retc[:, 512] = p + 1
    retc[:, 513] = 127 - p
    retc[:, 514] = 128 - p
    retc[:, 515] = p
    kmask = np.zeros((128, 4), np.float32)
    for j_ in range(4):
        kmask[32 * j_:32 * j_ + 32, j_] = 1.0
    return {
        "rope": rope, "maskB": maskB, "retc": retc, "kmask": kmask,
        "ident_b": np.eye(128, dtype=np.float32).astype(ml_dtypes.bfloat16),
        "ident_f": np.eye(128, dtype=np.float32),
    }


def pack_small(inp, L):
    parts = [inp["qn_a"], inp["kn_a"], inp["lambda_q1"], inp["lambda_k1"], inp["lambda_q2"], inp["lambda_k2"],
             inp["subln_a"], inp["qn_b"], inp["kn_b"], np.asarray(inp["ret_decay"]).reshape(L, 8), inp["gn_c"]]
    return np.ascontiguousarray(np.concatenate([np.asarray(p_, np.float32).reshape(L, -1) for p_ in parts], axis=1))


def make_in_maps(inp, S, L, B):
    consts = host_constants(S)
    f = lambda k: np.ascontiguousarray(np.asarray(inp[k], np.float32))
    shared = {
        "w_ada": f("w_ada")[:L], "b_ada": f("b_ada")[:L], "norm_g": f("norm_g")[:L], "w_in": f("w_in")[:L], "w_out": f("w_out")[:L],
        "smallp": pack_small({k: np.asarray(v)[:L] for k, v in inp.items() if k not in ("x", "c")}, L),
    }
    shared.update(consts)
    maps = []
    x = f("x")
    c = f("c")
    for b in range(B):
        m = dict(shared)
        m["x"] = np.ascontiguousarray(x[b])
        m["cT"] = np.ascontiguousarray(c[b].reshape(8, 128).T)
        maps.append(m)
    return maps


_CACHE = {}


def kernel(**inputs):
    x = np.asarray(inputs["x"])
    B, S, _ = x.shape
    L = np.asarray(inputs["w_in"]).shape[0]
    key = (S, L)
    if key not in _CACHE:
        _CACHE[key] = build_program(S, L)[0]
    nc = _CACHE[key]
    maps = make_in_maps(inputs, S, L, B)
    in_maps = [maps[i % B] for i in range(8)]
    res = run_bass_kernel_spmd(nc, in_maps, core_ids=list(range(8)))
    out = np.stack([np.asarray(res.results[b]["y"], np.float32) for b in range(B)], axis=0)
    return out
```

```python
import math
from contextlib import ExitStack
import numpy as np
import ml_dtypes
import concourse.bass as bass
import concourse.mybir as mybir
from concourse.bass_utils import run_bass_kernel_spmd

F32 = mybir.dt.float32
BF16 = mybir.dt.bfloat16
ALU = mybir.AluOpType
AF = mybir.ActivationFunctionType
AX = mybir.AxisListType

D_MODEL = 1024
IN_COLS = 3712
EPS = 1e-6
ROT_THETA = 500000.0
RET_THETA = 10000.0
NSMALL = 488
SP_QNA, SP_KNA, SP_LQ1, SP_LK1, SP_LQ2, SP_LK2 = 0, 32, 64, 96, 128, 160
SP_SUB, SP_QNB, SP_KNB, SP_DEC, SP_GNC = 192, 256, 320, 384, 392
OFF_QA, OFF_KA, OFF_VA, OFF_GA = 0, 256, 512, 768
OFF_QB, OFF_KB, OFF_VB, OFF_GB = 1024, 1408, 1792, 2176
OFF_QC, OFF_KC, OFF_VC, OFF_GC = 2560, 2752, 2944, 3328
OCOL_A, OCOL_B, OCOL_C = 0, 256, 640


class Buf:
    __slots__ = ("name", "w", "r", "excl")

    def __init__(self, name, excl=False):
        self.name = name
        self.w = None
        self.r = {}
        self.excl = excl


class Prog:
    ENG = ("pe", "act", "dve", "pool", "sp")

    def __init__(self, nc, stack):
        self.nc = nc
        self.stack = stack
        self.q = {e: [] for e in self.ENG}
        self.cnt = {e: 0 for e in self.ENG}
        self.seen = {e: {} for e in self.ENG}
        self.sems = {}
        self.dcnt = {}
        for e in self.ENG:
            self.sems[e] = stack.enter_context(nc.semaphore("s_" + e))
        self.ninst = 0
        self.desc = {}
        self.total = 0

    def dsem(self, name):
        if name not in self.sems:
            self.sems[name] = self.stack.enter_context(self.nc.semaphore("d_" + name))
            self.dcnt[name] = 0
        return name

    def _wait(self, eng, k, v):
        if k == eng and eng == "pe":
            return
        if self.seen[eng].get(k, 0) < v:
            self.seen[eng][k] = v
            sem = self.sems[k]
            self.ninst += 1
            self.desc.setdefault(eng, []).append("wait %s>=%d" % (k, v))
            self.q[eng].append(lambda e, sem=sem, v=v: e.wait_ge(sem, v))

    def _deps(self, eng, reads, writes):
        need = {}
        for b in reads:
            if b.w is not None:
                k, v = b.w
                if need.get(k, 0) < v:
                    need[k] = v
            if b.excl:
                for k, v in b.r.items():
                    if k != eng and need.get(k, 0) < v:
                        need[k] = v
        for b in writes:
            if b.w is not None:
                k, v = b.w
                if need.get(k, 0) < v:
                    need[k] = v
            for k, v in b.r.items():
                if need.get(k, 0) < v:
                    need[k] = v
        for k, v in need.items():
            self._wait(eng, k, v)

    def op(self, eng, fn, reads=(), writes=()):
        self.total = getattr(self, "total", 0) + 1
        if self.total > DBG.get("cut", 10 ** 9):
            return
        self._deps(eng, reads, writes)
        if DBG.get("serial") and getattr(self, "last", None):
            self._wait(eng, *self.last)
        self.cnt[eng] += 1
        c = self.cnt[eng]
        self.last = (eng, c)
        sem = self.sems[eng]
        self.ninst += 1
        self.desc.setdefault(eng, []).append("op#%d (%s=%d)" % (self.total, eng, c))
        self.q[eng].append(lambda e, fn=fn, sem=sem: fn(e).then_inc(sem, 1))
        for b in writes:
            b.w = (eng, c)
            b.r = {}
        for b in reads:
            if b.w != (eng, c):
                b.r[eng] = c

    def dma(self, semname, out, in_, reads=(), writes=(), qeng="sp", **kw):
        self.total = getattr(self, "total", 0) + 1
        if self.total > DBG.get("cut", 10 ** 9):
            return
        self.dsem(semname)
        self._deps(qeng, reads, writes)
        if DBG.get("serial") and getattr(self, "last", None):
            self._wait(qeng, *self.last)
        self.dcnt[semname] += 16
        v = self.dcnt[semname]
        self.last = (semname, v)
        sem = self.sems[semname]
        self.ninst += 1
        self.desc.setdefault(qeng, []).append("dma#%d (%s=%d)" % (self.total, semname, v))
        self.q[qeng].append(
            lambda e, out=out, in_=in_, sem=sem, kw=kw: e.dma_start(out=out, in_=in_, **kw).then_inc(sem, 16))
        for b in writes:
            b.w = (semname, v)
            b.r = {}
        for b in reads:
            b.r[semname] = v

    def barrier(self):
        for e in self.ENG:
            for k in list(self.sems.keys()):
                v = self.cnt[k] if k in self.cnt else self.dcnt[k]
                if v > 0 and k != e:
                    self._wait(e, k, v)
            if e != "pe" and self.cnt[e] > 0:
                self._wait(e, e, self.cnt[e])

    def emit(self):
        nc = self.nc
        with nc.Block() as block:
            @block.tensor
            def _(e):
                for t in self.q["pe"]:
                    t(e)

            @block.scalar
            def _(e):
                for t in self.q["act"]:
                    t(e)

            @block.vector
            def _(e):
                for t in self.q["dve"]:
                    t(e)

            @block.gpsimd
            def _(e):
                for t in self.q["pool"]:
                    t(e)

            @block.sync
            def _(e):
                for t in self.q["sp"]:
                    t(e)


DBG = {}


def build_program(S, L, stages=("A", "B", "C", "O"), dbg=None):
    NT = S // 128
    nc = bass.Bass("TRN2", target_bir_lowering=False)
    dr = lambda name, shape, dt, kind="ExternalInput": nc.dram_tensor(name, shape, dt, kind=kind).ap()
    x_in = dr("x", [S, D_MODEL], F32)
    cT_in = dr("cT", [128, 8], F32)
    w_ada = dr("w_ada", [L, D_MODEL, 3 * D_MODEL], F32)
    b_ada = dr("b_ada", [L, 3 * D_MODEL], F32)
    norm_g = dr("norm_g", [L, D_MODEL], F32)
    w_in = dr("w_in", [L, D_MODEL, IN_COLS], F32)
    w_out = dr("w_out", [L, D_MODEL, D_MODEL], F32)
    smallp = dr("smallp", [L, NSMALL], F32)
    ident_b_d = dr("ident_b", [128, 128], BF16)
    ident_f_d = dr("ident_f", [128, 128], F32)
    rope_d = dr("rope", [128, NT, 72], F32)
    maskB_d = dr("maskB", [128, 384], BF16)
    retc_d = dr("retc", [128, 516], F32)
    kmask_d = dr("kmask", [128, 4], F32)
    y_out = dr("y", [S, D_MODEL], F32, kind="ExternalOutput")
    o_scr = dr("o_scr", [S, D_MODEL], BF16, kind="ExternalOutput")
    dbg_hT = dr("dbg_hT", [128, 8 * S], F32, kind="ExternalOutput") if dbg == "hT" else None
    dbg_o = dr("dbg_o", [S, D_MODEL], F32, kind="ExternalOutput") if dbg == "o" else None

    with ExitStack() as st:
        P = Prog(nc, st)

        uid = [0]

        def sbuf(stack, name, shape, dt):
            uid[0] += 1
            t = stack.enter_context(nc.sbuf_tensor("sb%d_%s" % (uid[0], name), shape, dt))
            return t, Buf(name)

        def op(eng, fn, reads=(), writes=()):
            if eng == "pool" and DBG.get("nopool"):
                eng = "dve"
            P.op(eng, fn, reads, writes)

        def cp(eng, out, in_, reads, writes):
            if eng == "act":
                op("act", lambda e: e.copy(out=out, in_=in_), reads, writes)
            else:
                op(eng, lambda e: e.tensor_copy(out=out, in_=in_), reads, writes)

        def tt(eng, out, in0, in1, alu, reads, writes):
            op(eng, lambda e: e.tensor_tensor(out=out, in0=in0, in1=in1, op=alu), reads, writes)

        def ts(eng, out, in0, s1, alu, reads, writes):
            op(eng, lambda e: e.tensor_scalar(out=out, in0=in0, scalar1=s1, scalar2=None, op0=alu), reads, writes)

        hT, b_hT = sbuf(st, "hT", [128, 8, S], BF16)
        rope, b_rope = sbuf(st, "rope", [128, NT, 72], F32)
        ident_b, b_idb = sbuf(st, "ident_b", [128, 128], BF16)
        ident_f, b_idf = sbuf(st, "ident_f", [128, 128], F32)
        maskB, b_maskB = sbuf(st, "maskB", [128, 384], BF16)
        retc, b_retc = sbuf(st, "retc", [128, 516], F32)
        kmask, b_kmask = sbuf(st, "kmask", [128, 4], F32)
        cs, b_cs = sbuf(st, "cs", [128, 8], F32)
        ones_row, b_ones = sbuf(st, "ones_row", [1, 128], F32)
        gate_bc, b_gate = sbuf(st, "gate_bc", [128, 1024], F32)
        spt, b_spt = sbuf(st, "spt", [128, NSMALL], F32)
        gA, b_gA = sbuf(st, "gA", [128, 8, 32], F32)
        gB, b_gB = sbuf(st, "gB", [128, 4, 64], F32)
        gS, b_gS = sbuf(st, "gS", [128, 2, 64], F32)
        lamt, b_lam = sbuf(st, "lamt", [128, 8], F32)
        decs, b_decs = sbuf(st, "decs", [128, 64], F32)
        DT, b_DT = sbuf(st, "DT", [128, 4, 128], F32)
        eps_t, b_eps = sbuf(st, "eps_t", [128, 1], F32)
        wstage = [sbuf(st, f"wstage{i}", [128, 8, 256], F32) for i in range(2)]
        state = {"ws": 0, "mm": 0, "st": 0, "ot": 0, "tb": 0, "scr": 0}

        psf = []
        psbv = []
        for i in range(8):
            t_ = st.enter_context(nc.psum_tensor(f"psf{i}", [128, 512], F32))
            b_ = Buf(f"psf{i}", True)
            psf.append((t_, b_))
            psbv.append((t_.bitcast(BF16), b_))
        pools = {"mm": [0, 1, 2, 3, 4, 5], "st": [1, 2, 3], "ot": [4, 5, 6, 7], "tb": [6, 7]}
        ppos = {"mm": 0, "st": 0, "ot": 0, "tb": 0}

        def set_pools(**kw):
            for k_, v_ in kw.items():
                pools[k_] = list(v_)
                ppos[k_] = 0

        def _rot(kind):
            lst = pools[kind]
            i = lst[ppos[kind] % len(lst)]
            ppos[kind] += 1
            return i

        def ps_mm():
            return psf[_rot("mm")]

        def ps_st():
            return psf[_rot("st")]

        def ps_ot():
            return psf[_rot("ot")]

        def ps_tb():
            return psbv[_rot("tb")]

        def interleave(gens, K):
            active = []
            it = iter(gens)
            more = True
            while True:
                while more and len(active) < K:
                    try:
                        active.append(next(it))
                    except StopIteration:
                        more = False
                if not active:
                    break
                for g_ in list(active):
                    try:
                        next(g_)
                    except StopIteration:
                        active.remove(g_)

        P.dma("c_rope", rope[:], rope_d[:, :, :], writes=[b_rope])
        P.dma("c_idb", ident_b[:], ident_b_d[:, :], writes=[b_idb])
        P.dma("c_idf", ident_f[:], ident_f_d[:, :], writes=[b_idf])
        P.dma("c_mb", maskB[:], maskB_d[:, :], writes=[b_maskB])
        P.dma("c_rc", retc[:], retc_d[:, :], writes=[b_retc])
        P.dma("c_km", kmask[:], kmask_d[:, :], writes=[b_kmask])
        P.dma("c_cs", cs[:], cT_in[:, :], writes=[b_cs])
        cse, b_cse = sbuf(st, "cse", [128, 8], F32)
        op("act", lambda e: e.activation(out=cse[:], in_=cs[:], func=AF.Exp, scale=-1.0), [b_cs], [b_cse])
        ts("dve", cse[:], cse[:], 1.0, ALU.add, [b_cse], [b_cse])
        op("dve", lambda e: e.reciprocal(out=cse[:], in_=cse[:]), [b_cse], [b_cse])
        tt("dve", cs[:], cs[:], cse[:], ALU.mult, [b_cs, b_cse], [b_cs])
        op("pool", lambda e: e.memset(ones_row[:], 1.0), [], [b_ones])
        op("pool", lambda e: e.memset(eps_t[:], EPS), [], [b_eps])
        one_t, b_one = sbuf(st, "one_t", [128, 1], F32)
        op("pool", lambda e: e.memset(one_t[:], 1.0), [], [b_one])

        def load_w(dst, b_dst, src_ap, ncols, col0=0):
            done = 0
            while done < ncols:
                n = min(256, ncols - done)
                i = state["ws"]
                state["ws"] ^= 1
                stg, b_stg = wstage[i]
                P.dma("ws%d" % i, stg[:, :, 0:n], src_ap[:, done:done + n].rearrange("(k p) n -> p k n", p=128), writes=[b_stg])
                cp("act" if i else "dve", dst[:, :, col0 + done:col0 + done + n], stg[:, :, 0:n], [b_stg], [b_dst])
                done += n

        def proj(t_cols, W, b_W, n, ps, b_ps, wcol0=0, pcol0=0):
            for k in range(8):
                op("pe", lambda e, k=k: e.matmul(ps[:, pcol0:pcol0 + n], lhsT=hT[:, k, t_cols], rhs=W[:, k, wcol0:wcol0 + n],
                                                start=(k == 0), stop=(k == 7)), [b_hT, b_W], [b_ps])

        def rstd_inplace(ss, b_ss, d):
            op("act", lambda e: e.activation(out=ss, in_=ss, func=AF.Ln, scale=1.0 / d, bias=eps_t[:, 0:1]), [b_ss, b_eps], [b_ss])
            op("act", lambda e: e.activation(out=ss, in_=ss, func=AF.Exp, scale=-0.5), [b_ss], [b_ss])

        def tcols(t):
            return slice(t * 128, (t + 1) * 128)

        def norm_rope(src3, b_src, G, d, gains, b_g, half, cos, sin, out3, b_out, scr):
            rot = 2 * half
            sq, b_sq = scr["sq"]
            xn, b_xn = scr["xn"]
            ssx, b_ssx = scr["ss"]
            xn3 = xn[:, 0:G * d].rearrange("p (g d) -> p g d", d=d)
            if gains is not None:
                sq3 = sq[:, 0:G * d].rearrange("p (g d) -> p g d", d=d)
                op("act", lambda e: e.activation(out=sq3, in_=src3, func=AF.Square), [b_src], [b_sq])
                yield
                op("dve", lambda e: e.reduce_sum(out=ssx[:, 0:G], in_=sq3, axis=AX.X), [b_sq], [b_ssx])
                yield
                rstd_inplace(ssx[:, 0:G], b_ssx, float(d))
                yield
                tt("dve", xn3, src3, ssx[:, 0:G].unsqueeze(2).to_broadcast([128, G, d]), ALU.mult, [b_src, b_ssx], [b_xn])
                yield
                tt("pool", xn3, xn3, gains, ALU.mult, [b_xn, b_g], [b_xn])
                yield
            else:
                cp("act", xn3, src3, [b_src], [b_xn])
                yield
            x1 = xn3[:, :, 0:half]
            x2 = xn3[:, :, half:rot]
            cb = cos.unsqueeze(1).to_broadcast([128, G, half])
            sb_ = sin.unsqueeze(1).to_broadcast([128, G, half])
            tv = []
            for i in range(4):
                tq, b_tq = scr["t%d" % i]
                tv.append((tq[:, 0:G * half].rearrange("p (g d) -> p g d", d=half), b_tq))
            tt("dve", tv[0][0], x1, cb, ALU.mult, [b_xn, b_rope], [tv[0][1]])
            tt("pool", tv[1][0], x2, sb_, ALU.mult, [b_xn, b_rope], [tv[1][1]])
            tt("pool", tv[2][0], x1, sb_, ALU.mult, [b_xn, b_rope], [tv[2][1]])
            tt("dve", tv[3][0], x2, cb, ALU.mult, [b_xn, b_rope], [tv[3][1]])
            if rot < d:
                cp("act", out3[:, :, rot:d], xn3[:, :, rot:d], [b_xn], [b_out])
            yield
            tt("dve", out3[:, :, 0:half], tv[0][0], tv[1][0], ALU.subtract, [tv[0][1], tv[1][1]], [b_out])
            tt("pool", out3[:, :, half:rot], tv[2][0], tv[3][0], ALU.add, [tv[2][1], tv[3][1]], [b_out])
            yield

        def make_scr(ph, tag, n):
            out = []
            for i in range(n):
                s = {}
                s["sq"] = sbuf(ph, f"{tag}sq{i}", [128, 256], F32)
                s["xn"] = sbuf(ph, f"{tag}xn{i}", [128, 256], F32)
                s["ss"] = sbuf(ph, f"{tag}ss{i}", [128, 8], F32)
                for j in range(4):
                    s["t%d" % j] = sbuf(ph, f"{tag}t{j}_{i}", [128, 96], F32)
                out.append(s)
            return out

        def gate_and_store(t, on2d, b_on, Wg, b_Wg, n, ocol, gsc, b_gsc, ofin, b_ofin):
            psg, b_psg = ps_mm()
            proj(tcols(t), Wg, b_Wg, n, psg, b_psg)
            yield
            op("act", lambda e: e.activation(out=gsc[:, 0:n], in_=psg[:, 0:n], func=AF.Exp, scale=-1.0), [b_psg], [b_gsc])
            yield
            op("act", lambda e: e.activation(out=gsc[:, 0:n], in_=gsc[:, 0:n], func=AF.Ln, bias=one_t[:, 0:1]), [b_gsc, b_one], [b_gsc])
            yield
            op("act", lambda e: e.activation(out=gsc[:, 0:n], in_=gsc[:, 0:n], func=AF.Exp, scale=-1.0), [b_gsc], [b_gsc])
            yield
            tt("dve", gsc[:, 0:n], psg[:, 0:n], gsc[:, 0:n], ALU.mult, [b_psg, b_gsc], [b_gsc])
            yield
            tt("dve", ofin[:, 0:n], on2d, gsc[:, 0:n], ALU.mult, [b_on, b_gsc], [b_ofin])
            yield
            P.dma("ofin_" + b_ofin.name, o_scr[t * 128:(t + 1) * 128, ocol:ocol + n], ofin[:, 0:n], reads=[b_ofin], writes=[obufs[t]])
            yield

        xbufs = [Buf(f"x{t}") for t in range(NT)]
        obufs = [Buf(f"o{t}") for t in range(NT)]

        for l in range(L):
            lam_init = 0.8 - 0.6 * math.exp(-0.3 * l)
            xsrc = x_in if l == 0 else y_out
            P.dma("spt", spt[:], smallp[l:l + 1, :].partition_broadcast(128), writes=[b_spt])
            cp("dve", gA[:, 0:4, :], spt[:, SP_QNA:SP_QNA + 32].unsqueeze(1).to_broadcast([128, 4, 32]), [b_spt], [b_gA])
            cp("dve", gA[:, 4:8, :], spt[:, SP_KNA:SP_KNA + 32].unsqueeze(1).to_broadcast([128, 4, 32]), [b_spt], [b_gA])
            cp("dve", gB[:, 0:2, :], spt[:, SP_QNB:SP_QNB + 64].unsqueeze(1).to_broadcast([128, 2, 64]), [b_spt], [b_gB])
            cp("dve", gB[:, 2:4, :], spt[:, SP_KNB:SP_KNB + 64].unsqueeze(1).to_broadcast([128, 2, 64]), [b_spt], [b_gB])
            ts("dve", gS[:], spt[:, SP_SUB:SP_SUB + 64].unsqueeze(1).to_broadcast([128, 2, 64]), 1.0 - lam_init, ALU.mult, [b_spt], [b_gS])
            tt("dve", decs[:, 0:32], spt[:, SP_LQ1:SP_LQ1 + 32], spt[:, SP_LK1:SP_LK1 + 32], ALU.mult, [b_spt], [b_decs])
            op("dve", lambda e: e.reduce_sum(out=lamt[:, 0:1], in_=decs[:, 0:32], axis=AX.X), [b_decs], [b_lam])
            tt("dve", decs[:, 32:64], spt[:, SP_LQ2:SP_LQ2 + 32], spt[:, SP_LK2:SP_LK2 + 32], ALU.mult, [b_spt], [b_decs])
            op("dve", lambda e: e.reduce_sum(out=lamt[:, 1:2], in_=decs[:, 32:64], axis=AX.X), [b_decs], [b_lam])
            op("act", lambda e: e.activation(out=lamt[:, 2:4], in_=lamt[:, 0:2], func=AF.Exp), [b_lam], [b_lam])
            tt("dve", lamt[:, 4:5], lamt[:, 3:4], lamt[:, 2:3], ALU.subtract, [b_lam], [b_lam])
            ts("dve", lamt[:, 6:7], lamt[:, 4:5], -lam_init, ALU.add, [b_lam], [b_lam])
            op("act", lambda e: e.activation(out=decs[:, 0:8], in_=spt[:, SP_DEC:SP_DEC + 8], func=AF.Exp, scale=-1.0), [b_spt, b_decs], [b_decs])
            ts("dve", decs[:, 0:8], decs[:, 0:8], 1.0, ALU.add, [b_decs], [b_decs])
            op("act", lambda e: e.activation(out=decs[:, 0:8], in_=decs[:, 0:8], func=AF.Ln), [b_decs], [b_decs])
            ts("dve", decs[:, 0:8], decs[:, 0:8], -1.0, ALU.mult, [b_decs], [b_decs])
            for (o0, i0, rc) in ((8, 0, 513), (12, 4, 515), (16, 0, 512), (20, 4, 514)):
                op("act", lambda e, o0=o0, i0=i0, rc=rc: e.activation(out=decs[:, o0:o0 + 4], in_=decs[:, i0:i0 + 4], func=AF.Exp, scale=retc[:, rc:rc + 1]),
                   [b_decs, b_retc], [b_decs])
            op("act", lambda e: e.activation(out=decs[:, 24:32], in_=decs[:, 0:8], func=AF.Exp, scale=128.0), [b_decs], [b_decs])
            with ExitStack() as ph:
                dtmp, b_dtmp = sbuf(ph, "dtmp", [128, 128], F32)
                for h in range(4):
                    op("act", lambda e, h=h: e.activation(out=DT[:, h, :], in_=retc[:, 0:128], func=AF.Exp, scale=decs[:, h:h + 1]), [b_decs, b_retc], [b_DT])
                    tt("dve", DT[:, h, :], DT[:, h, :], retc[:, 128:256], ALU.mult, [b_DT, b_retc], [b_DT])
                    op("act", lambda e, h=h: e.activation(out=dtmp[:], in_=retc[:, 256:384], func=AF.Exp, scale=decs[:, 4 + h:5 + h]), [b_decs, b_retc], [b_dtmp])
                    tt("dve", dtmp[:], dtmp[:], retc[:, 384:512], ALU.mult, [b_dtmp, b_retc], [b_dtmp])
                    tt("dve", DT[:, h, :], DT[:, h, :], dtmp[:], ALU.add, [b_dtmp, b_DT], [b_DT])
                P.barrier()

            ph_norm = ExitStack()
            gs_bc, b_gs = sbuf(ph_norm, "gs_bc", [128, 1024], F32)
            shift_bc, b_shift = sbuf(ph_norm, "shift_bc", [128, 1024], F32)
            ph_ada = ExitStack()
            modrow, b_modrow = sbuf(ph_ada, "modrow", [1, 3072], F32)
            bada, b_bada = sbuf(ph_ada, "bada", [1, 3072], F32)
            P.dma("bada", bada[:], b_ada[l:l + 1, :], writes=[b_bada])
            for nt in range(12):
                i = state["ws"]
                state["ws"] ^= 1
                stg, b_stg = wstage[i]
                P.dma("ws%d" % i, stg[:, :, :], w_ada[l, :, nt * 256:(nt + 1) * 256].rearrange("(k p) n -> p k n", p=128), writes=[b_stg])
                ps, b_ps = ps_mm()
                for k in range(8):
                    op("pe", lambda e, k=k, stg=stg, ps=ps: e.matmul(ps[0:1, 0:256], lhsT=cs[:, k:k + 1], rhs=stg[:, k, :], start=(k == 0), stop=(k == 7)),
                       [b_cs, b_stg], [b_ps])
                tt("dve", modrow[0:1, nt * 256:(nt + 1) * 256], ps[0:1, 0:256], bada[0:1, nt * 256:(nt + 1) * 256], ALU.add, [b_ps, b_bada], [b_modrow])
            with ExitStack() as ph:
                g_bc, b_gbc = sbuf(ph, "g_bc", [128, 1024], F32)
                P.dma("gbc", g_bc[:], norm_g[l:l + 1, :].partition_broadcast(128), writes=[b_gbc])
                for j in range(6):
                    ps, b_ps = ps_mm()
                    op("pe", lambda e, ps=ps, j=j: e.matmul(ps[:, :], lhsT=ones_row[0:1, :], rhs=modrow[0:1, j * 512:(j + 1) * 512], start=True, stop=True),
                       [b_ones, b_modrow], [b_ps])
                    sl = slice((j % 2) * 512, (j % 2 + 1) * 512)
                    if j < 2:
                        cp("act", shift_bc[:, sl], ps[:, :], [b_ps], [b_shift])
                    elif j < 4:
                        op("dve", lambda e, ps=ps, sl=sl: e.scalar_tensor_tensor(out=gs_bc[:, sl], in0=ps[:, :], scalar=1.0, in1=g_bc[:, sl], op0=ALU.add, op1=ALU.mult),
                           [b_ps, b_gbc], [b_gs])
                    else:
                        cp("act", gate_bc[:, sl], ps[:, :], [b_ps], [b_gate])
                P.barrier()
            ph_ada.close()

            with ExitStack() as ph:
                KN = 3
                set_pools(mm=[0, 1, 2, 3, 4, 5], tb=[5, 6, 7])
                xt = [sbuf(ph, f"xt{i}", [128, 1024], F32) for i in range(KN)]
                junk, b_junk = sbuf(ph, "junk", [128, 1024], BF16)
                h1 = [sbuf(ph, f"h1_{i}", [128, 1024], F32) for i in range(KN)]
                hb = [sbuf(ph, f"hb{i}", [128, 1024], BF16) for i in range(KN)]
                ssn = [sbuf(ph, f"ssn{i}", [128, 1], F32) for i in range(KN)]

                def norm_body(t):
                    x_t, b_x = xt[t % KN]
                    h_t, b_h = h1[t % KN]
                    hb_t, b_hb = hb[t % KN]
                    ss, b_ss = ssn[t % KN]
                    P.dma("xt%d" % (t % KN), x_t[:], xsrc[t * 128:(t + 1) * 128, :], reads=[xbufs[t]], writes=[b_x])
                    yield
                    op("act", lambda e: e.activation(out=junk[:], in_=x_t[:], func=AF.Square, accum_out=ss[:]), [b_x], [b_junk, b_ss])
                    yield
                    rstd_inplace(ss[:], b_ss, 1024.0)
                    yield
                    op("dve", lambda e: e.scalar_tensor_tensor(out=h_t[:], in0=x_t[:], scalar=ss[:, 0:1], in1=gs_bc[:], op0=ALU.mult, op1=ALU.mult),
                       [b_x, b_ss, b_gs], [b_h])
                    yield
                    tt("pool" if t % 2 else "dve", hb_t[:], h_t[:], shift_bc[:], ALU.add, [b_h, b_shift], [b_hb])
                    yield
                    pb, b_pb = ps_tb()
                    for k in range(8):
                        op("pe", lambda e, k=k: e.transpose(pb[:, k * 128:(k + 1) * 128], hb_t[:, k * 128:(k + 1) * 128], ident_b[:]),
                           [b_hb, b_idb], [b_pb])
                    yield
                    cp("act" if t % 2 else "dve", hT[:, :, tcols(t)], pb[:, :].rearrange("p (k n) -> p k n", n=128), [b_pb], [b_hT])
                    yield

                interleave((norm_body(t) for t in range(NT)), KN)
                P.barrier()
            ph_norm.close()
            if dbg == "hT":
                with ExitStack() as ph:
                    d32, b_d32 = sbuf(ph, "d32", [128, 8 * S], F32)
                    cp("dve", d32[:], hT[:].rearrange("p k s -> p (k s)"), [b_hT], [b_d32])
                    P.dma("dbg", dbg_hT[:, :], d32[:], reads=[b_d32], writes=[xbufs[0]])
                break

            for c in range(2 if "A" in stages else 0):
                with ExitStack() as ph:
                    Wqk, b_Wqk = sbuf(ph, "A_Wqk", [128, 8, 256], BF16)
                    Wv, b_Wv = sbuf(ph, "A_Wv", [128, 8, 128], BF16)
                    Wg, b_Wg = sbuf(ph, "A_Wg", [128, 8, 128], BF16)
                    wl = w_in[l]
                    load_w(Wqk, b_Wqk, wl[:, OFF_QA + 128 * c:OFF_QA + 128 * c + 128], 128, 0)
                    load_w(Wqk, b_Wqk, wl[:, OFF_KA + 128 * c:OFF_KA + 128 * c + 128], 128, 128)
                    load_w(Wv, b_Wv, wl[:, OFF_VA + 128 * c:OFF_VA + 128 * c + 128], 128)
                    load_w(Wg, b_Wg, wl[:, OFF_GA + 128 * c:OFF_GA + 128 * c + 128], 128)
                    qT, b_qT = sbuf(ph, "A_qT", [128, S], BF16)
                    kps = [sbuf(ph, f"A_kp{j_}", [128, S], BF16) for j_ in range(4)]
                    vaug, b_vaug = sbuf(ph, "A_vaug", [128, NT, 2, 66], BF16)
                    if not DBG.get("A_nomemset"):
                        op("pool", lambda e: e.memset(vaug[:, :, :, 64:65], 1.0), [], [b_vaug])
                    ph2 = ExitStack()
                    KA = 3
                    set_pools(mm=[0, 1, 2, 3, 4], tb=[5, 6, 7])
                    scr = make_scr(ph2, "A", KA)
                    qkb = [sbuf(ph2, f"A_qkb{i}", [128, 256], BF16) for i in range(KA)]

                    def a_proj_body(t):
                        ps, b_ps = ps_mm()
                        proj(tcols(t), Wqk, b_Wqk, 256, ps, b_ps)
                        yield
                        qk_t, b_qk = qkb[t % KA]
                        yield from norm_rope(ps[:, 0:256].rearrange("p (g d) -> p g d", d=32), b_ps, 8, 32, gA[:], b_gA, 4,
                                             rope[:, t, 0:4], rope[:, t, 4:8], qk_t[:].rearrange("p (g d) -> p g d", d=32), b_qk, scr[t % KA])
                        pb, b_pb = ps_tb()
                        for j in range(2):
                            op("pe", lambda e, j=j: e.transpose(pb[:, j * 128:(j + 1) * 128], qk_t[:, j * 128:(j + 1) * 128], ident_b[:]),
                               [b_qk, b_idb], [b_pb])
                        yield
                        cp("act", qT[:, tcols(t)], pb[:, 0:128], [b_pb], [b_qT])
                        for j_ in range(4):
                            ts("dve", kps[j_][0][:, tcols(t)], pb[:, 128:256], kmask[:, j_:j_ + 1], ALU.mult, [b_pb, b_kmask], [kps[j_][1]])
                        yield
                        psv, b_psv = ps_mm()
                        proj(tcols(t), Wv, b_Wv, 128, psv, b_psv)
                        yield
                        cp("act", vaug[:, t, :, 0:64], psv[:, 0:128].rearrange("p (h d) -> p h d", d=64), [b_psv], [b_vaug])
                        yield

                    interleave((a_proj_body(t) for t in range(NT)), KA)
                    P.barrier()
                    ph2.close()
                    set_pools(mm=[0], st=[1, 2, 3], ot=[4, 5, 6, 7])
                    Et = [sbuf(ph, f"A_E{i}", [128, 512], BF16) for i in range(4)]
                    OTs = [sbuf(ph, f"A_OTs{i}", [65, 2, 512], F32) for i in range(2)]
                    opre = [sbuf(ph, f"A_opre{i}", [128, 4, 2, 64], F32) for i in range(2)]
                    KE = 3
                    tA = [sbuf(ph, f"A_tA{i}", [128, 64], F32) for i in range(KE)]
                    tB = [sbuf(ph, f"A_tB{i}", [128, 64], F32) for i in range(KE)]
                    rden = [sbuf(ph, f"A_rden{i}", [128, 2], F32) for i in range(KE)]
                    osq = [sbuf(ph, f"A_osq{i}", [128, 128], F32) for i in range(KE)]
                    oss = [sbuf(ph, f"A_oss{i}", [128, 2], F32) for i in range(KE)]
                    onn = [sbuf(ph, f"A_on{i}", [128, 128], F32) for i in range(KE)]
                    gsc = [sbuf(ph, f"A_gsc{i}", [128, 128], F32) for i in range(KE)]
                    ofin = [sbuf(ph, f"A_ofin{i}", [128, 128], BF16) for i in range(KE)]
                    side = []
                    cnt_e = [0]

                    def a_epi(g, hl, ots, b_ots):
                        for t4 in range(4):
                            yield from a_epi_tile(g, hl, ots, b_ots, t4)

                    def a_epi_tile(g, hl, ots, b_ots, t4):
                        opre_, b_opre = opre[g % 2]
                        if True:
                            i_ = cnt_e[0] % KE
                            cnt_e[0] += 1
                            psT, b_psT = ps_mm()
                            for m in range(2):
                                op("pe", lambda e, m=m: e.transpose(psT[:, m * 65:(m + 1) * 65], ots[0:65, m, t4 * 128:(t4 + 1) * 128], ident_f[0:65, 0:65]),
                                   [b_ots, b_idf], [b_psT])
                            yield
                            rd, b_rd = rden[i_]
                            ta, b_ta = tA[i_]
                            tb_, b_tb = tB[i_]
                            op("dve", lambda e: e.reciprocal(out=rd[:], in_=psT[:, 0:130].rearrange("p (m e) -> p m e", e=65)[:, :, 64]), [b_psT], [b_rd])
                            yield
                            ts("dve", ta[:], psT[:, 0:64], rd[:, 0:1], ALU.mult, [b_psT, b_rd], [b_ta])
                            yield
                            op("act", lambda e: e.activation(out=tb_[:], in_=psT[:, 65:129], func=AF.Copy, scale=rd[:, 1:2]), [b_psT, b_rd], [b_tb])
                            yield
                            op("dve", lambda e: e.scalar_tensor_tensor(out=opre_[:, t4, hl, :], in0=tb_[:], scalar=lamt[:, 6:7], in1=ta[:],
                                                                       op0=ALU.mult, op1=ALU.add), [b_tb, b_ta, b_lam], [b_opre])
                            yield

                    def a_fin(g):
                        for t4 in range(4):
                            yield from a_fin_tile(g, t4)

                    def a_fin_tile(g, t4):
                        opre_, b_opre = opre[g % 2]
                        if True:
                            t = g * 4 + t4
                            i_ = cnt_e[0] % KE
                            cnt_e[0] += 1
                            sq_, b_sq = osq[i_]
                            ss_, b_ss = oss[i_]
                            on_, b_on = onn[i_]
                            o2 = opre_[:, t4, :, :]
                            tt("pool", sq_[:].rearrange("p (h d) -> p h d", d=64), o2, o2, ALU.mult, [b_opre], [b_sq])
                            yield
                            op("dve", lambda e: e.reduce_sum(out=ss_[:], in_=sq_[:].rearrange("p (h d) -> p h d", d=64), axis=AX.X), [b_sq], [b_ss])
                            yield
                            rstd_inplace(ss_[:], b_ss, 64.0)
                            yield
                            on3 = on_[:].rearrange("p (h d) -> p h d", d=64)
                            tt("dve", on3, o2, ss_[:].unsqueeze(2).to_broadcast([128, 2, 64]), ALU.mult, [b_opre, b_ss], [b_on])
                            yield
                            tt("pool", on3, on3, gS[:], ALU.mult, [b_on, b_gS], [b_on])
                            yield
                            yield from gate_and_store(t, on_[:], b_on, Wg, b_Wg, 128, OCOL_A + 128 * c, gsc[i_][0], gsc[i_][1], ofin[i_][0], ofin[i_][1])

                    def a_main():
                        LA = 2
                        items = [(g, hl, m, kt) for g in range(S // 512) for hl in range(2) for m in range(2) for kt in range(NT)]
                        pend = []
                        otmap = {}
                        for idx in range(len(items) + LA):
                            if idx < len(items):
                                g, hl, m, kt = items[idx]
                                kp, b_kp = kps[2 * hl + m]
                                stp, b_st = ps_st()
                                op("pe", lambda e, stp=stp, kp=kp, kt=kt, g=g: e.matmul(
                                    stp[:, :], lhsT=kp[:, tcols(kt)], rhs=qT[:, g * 512:(g + 1) * 512], start=True, stop=True),
                                   [b_kp, b_qT], [b_st])
                                E, b_E = Et[idx % 4]
                                op("act", lambda e, E=E, stp=stp: e.activation(out=E[:], in_=stp[:], func=AF.Exp, scale=32 ** -0.5), [b_st], [b_E])
                                pend.append((g, hl, m, kt, E, b_E))
                            if idx >= LA:
                                g, hl, m, kt, E, b_E = pend.pop(0)
                                if kt == 0:
                                    otmap[(g, hl, m)] = ps_ot()
                                ot, b_ot = otmap[(g, hl, m)]
                                op("pe", lambda e, ot=ot, kt=kt, hl=hl, E=E: e.matmul(ot[0:65, :], lhsT=vaug[:, kt, hl, 0:65], rhs=E[:], start=(kt == 0), stop=(kt == NT - 1)),
                                   [b_vaug, b_E], [b_ot])
                                if kt == NT - 1:
                                    slot = (g * 2 + hl) % 2
                                    ots, b_ots = OTs[slot]
                                    while any(tg == slot for tg, _ in side):
                                        try:
                                            next(side[0][1])
                                        except StopIteration:
                                            side.pop(0)
                                    cp("dve", ots[0:65, m, :], ot[0:65, :], [b_ot], [b_ots])
                                    if m == 1:
                                        side.append((slot, a_epi(g, hl, ots, b_ots)))
                                        if hl == 1:
                                            side.append((None, a_fin(g)))
                            yield

                    for _ in a_main():
                        if side:
                            try:
                                next(side[0][1])
                            except StopIteration:
                                side.pop(0)
                    while side:
                        try:
                            next(side[0][1])
                        except StopIteration:
                            side.pop(0)
                    P.barrier()

            for c in range(3 if "B" in stages else 0):
                with ExitStack() as ph:
                    Wqk, b_Wqk = sbuf(ph, "B_Wqk", [128, 8, 256], BF16)
                    Wv, b_Wv = sbuf(ph, "B_Wv", [128, 8, 128], BF16)
                    Wg, b_Wg = sbuf(ph, "B_Wg", [128, 8, 128], BF16)
                    wl = w_in[l]
                    load_w(Wqk, b_Wqk, wl[:, OFF_QB + 128 * c:OFF_QB + 128 * c + 128], 128, 0)
                    load_w(Wqk, b_Wqk, wl[:, OFF_KB + 128 * c:OFF_KB + 128 * c + 128], 128, 128)
                    load_w(Wv, b_Wv, wl[:, OFF_VB + 128 * c:OFF_VB + 128 * c + 128], 128)
                    load_w(Wg, b_Wg, wl[:, OFF_GB + 128 * c:OFF_GB + 128 * c + 128], 128)
                    qT, b_qT = sbuf(ph, "B_qT", [128, S], BF16)
                    kT, b_kT = sbuf(ph, "B_kT", [128, S], BF16)
                    vB, b_vB = sbuf(ph, "B_vB", [128, 3, NT, 2, 66], BF16)
                    op("pool", lambda e: e.memset(vB[:, :, :, :, 64:65].rearrange("p a b c d -> p (a b c d)"), 1.0), [], [b_vB])
                    ph2 = ExitStack()
                    KB = 3
                    set_pools(mm=[0, 1, 2, 3, 4], tb=[5, 6, 7])
                    scr = make_scr(ph2, "B", KB)
                    qkb = [sbuf(ph2, f"B_qkb{i}", [128, 256], BF16) for i in range(KB)]

                    def b_proj_body(t):
                        ps, b_ps = ps_mm()
                        proj(tcols(t), Wqk, b_Wqk, 256, ps, b_ps)
                        yield
                        qk_t, b_qk = qkb[t % KB]
                        yield from norm_rope(ps[:, 0:256].rearrange("p (g d) -> p g d", d=64), b_ps, 4, 64, gB[:], b_gB, 8,
                                             rope[:, t, 8:16], rope[:, t, 16:24], qk_t[:].rearrange("p (g d) -> p g d", d=64), b_qk, scr[t % KB])
                        pb, b_pb = ps_tb()
                        for j in range(2):
                            op("pe", lambda e, j=j: e.transpose(pb[:, j * 128:(j + 1) * 128], qk_t[:, j * 128:(j + 1) * 128], ident_b[:]),
                               [b_qk, b_idb], [b_pb])
                        yield
                        cp("act", qT[:, tcols(t)], pb[:, 0:128], [b_pb], [b_qT])
                        cp("act", kT[:, tcols(t)], pb[:, 128:256], [b_pb], [b_kT])
                        yield

                    interleave((b_proj_body(t) for t in range(NT)), KB)
                    vi = 0
                    for gi, D in enumerate((1, 4, 16)):
                        ntl = S // D // 128
                        for r in range(D):
                            for j in range(ntl):
                                psv, b_psv = ps_mm()
                                c0 = r + D * 128 * j
                                proj(slice(c0, c0 + D * 127 + 1, D), Wv, b_Wv, 128, psv, b_psv)
                                cp("act" if vi % 2 else "dve", vB[:, gi, r * ntl + j, :, 0:64], psv[:, 0:128].rearrange("p (h d) -> p h d", d=64), [b_psv], [b_vB])
                                vi += 1
                    P.barrier()
                    ph2.close()
                    set_pools(mm=[0, 7], st=[1, 2, 3], ot=[4, 5, 6])
                    Et = [sbuf(ph, f"B_E{i}", [128, 384], BF16) for i in range(4)]
                    accs = [sbuf(ph, f"B_acc{i}", [65, 2048], F32) for i in range(2)]
                    ob, b_ob = sbuf(ph, "B_ob", [128, 16, 2, 64], F32)
                    KE = 3
                    rden = [sbuf(ph, f"B_rden{i}", [128, 1], F32) for i in range(KE)]
                    gsc = [sbuf(ph, f"B_gsc{i}", [128, 128], F32) for i in range(KE)]
                    ofin = [sbuf(ph, f"B_ofin{i}", [128, 128], BF16) for i in range(KE)]
                    side = []
                    cnt_e = [0]

                    def b_epi_tile(hl, acc, b_acc, t16):
                        i_ = cnt_e[0] % KE
                        cnt_e[0] += 1
                        psT, b_psT = ps_mm()
                        op("pe", lambda e: e.transpose(psT[:, 0:65], acc[0:65, t16 * 128:(t16 + 1) * 128], ident_f[0:65, 0:65]), [b_acc, b_idf], [b_psT])
                        yield
                        rd, b_rd = rden[i_]
                        op("dve", lambda e: e.reciprocal(out=rd[:], in_=psT[:, 64:65]), [b_psT], [b_rd])
                        yield
                        ts("dve", ob[:, t16, hl, :], psT[:, 0:64], rd[:, 0:1], ALU.mult, [b_psT, b_rd], [b_ob])
                        yield

                    def b_epi(hl, acc, b_acc):
                        for t16 in range(16):
                            yield from b_epi_tile(hl, acc, b_acc, t16)

                    def b_fin(u):
                        for t16 in range(16):
                            i_ = cnt_e[0] % KE
                            cnt_e[0] += 1
                            yield from gate_and_store(u * 16 + t16, ob[:, t16, :, :].rearrange("p h d -> p (h d)"), b_ob, Wg, b_Wg, 128, OCOL_B + 128 * c,
                                                      gsc[i_][0], gsc[i_][1], ofin[i_][0], ofin[i_][1])

                    def drain(cond):
                        while any(cond(tg) for tg, _ in side):
                            try:
                                next(side[0][1])
                            except StopIteration:
                                side.pop(0)

                    def b_main():
                        LA = 2
                        items = []
                        for u in range(S // 2048):
                            for hl in range(2):
                                for gi, D in enumerate((1, 4, 16)):
                                    per = 16 // D
                                    for r in range(D):
                                        for i in range(u * per, (u + 1) * per):
                                            items.append((u, hl, gi, D, r, i))
                        pend = []
                        for idx in range(len(items) + LA):
                            if idx < len(items):
                                u, hl, gi, D, r, i = items[idx]
                                base = 64 * hl
                                ntl = S // D // 128
                                qc0 = r + D * 128 * i
                                qcols = slice(qc0, qc0 + D * 127 + 1, D)
                                js = [j for j in (i - 1, i, i + 1) if 0 <= j < ntl]
                                stp, b_st = ps_st()
                                for j in js:
                                    jj = j - (i - 1)
                                    kc0 = r + D * 128 * j
                                    op("pe", lambda e, stp=stp, jj=jj, kc0=kc0, D=D, base=base, qcols=qcols: e.matmul(
                                        stp[:, jj * 128:(jj + 1) * 128], lhsT=kT[base:base + 64, slice(kc0, kc0 + D * 127 + 1, D)], rhs=qT[base:base + 64, qcols],
                                        start=True, stop=True), [b_kT, b_qT], [b_st])
                                lo = (js[0] - (i - 1)) * 128
                                hi = (js[-1] - (i - 1) + 1) * 128
                                E, b_E = Et[idx % 4]
                                op("act", lambda e, E=E, stp=stp, lo=lo, hi=hi: e.activation(out=E[:, lo:hi], in_=stp[:, lo:hi], func=AF.Exp, scale=0.125), [b_st], [b_E])
                                tt("pool" if idx % 3 == 0 else "dve", E[:, lo:hi], E[:, lo:hi], maskB[:, lo:hi], ALU.mult, [b_E, b_maskB], [b_E])
                                pend.append((u, hl, gi, D, r, i, js, E, b_E, qc0, idx))
                            if idx >= LA:
                                u, hl, gi, D, r, i, js, E, b_E, qc0, idx0 = pend.pop(0)
                                ntl = S // D // 128
                                slot = (u * 2 + hl) % 2
                                acc, b_acc = accs[slot]
                                first_item = (gi == 0 and r == 0 and i == u * 16)
                                if first_item:
                                    drain(lambda tg: tg == slot)
                                ot, b_ot = ps_ot()
                                for j in js:
                                    jj = j - (i - 1)
                                    op("pe", lambda e, ot=ot, gi=gi, tile=r * ntl + j, hl=hl, E=E, jj=jj, first=(j == js[0]), last=(j == js[-1]): e.matmul(
                                        ot[0:65, 0:128], lhsT=vB[:, gi, tile, hl, 0:65], rhs=E[:, jj * 128:(jj + 1) * 128], start=first, stop=last),
                                       [b_vB, b_E], [b_ot])
                                a0 = qc0 - 2048 * u
                                acc_ap = acc[0:65, a0:a0 + D * 127 + 1:D]
                                if gi == 0:
                                    cp("dve", acc_ap, ot[0:65, 0:128], [b_ot], [b_acc])
                                else:
                                    tt("dve", acc_ap, ot[0:65, 0:128], acc_ap, ALU.add, [b_ot, b_acc], [b_acc])
                                last_item = (gi == 2 and r == 15 and i == u)
                                if last_item:
                                    side.append((slot, b_epi(hl, acc, b_acc)))
                                    if hl == 1:
                                        side.append((None, b_fin(u)))
                            yield

                    for _ in b_main():
                        if side:
                            try:
                                next(side[0][1])
                            except StopIteration:
                                side.pop(0)
                    drain(lambda tg: True)
                    P.barrier()

            for c in range(2 if "C" in stages else 0):
                with ExitStack() as ph:
                    Wqk, b_Wqk = sbuf(ph, "C_Wqk", [128, 8, 192], BF16)
                    Wv, b_Wv = sbuf(ph, "C_Wv", [128, 8, 192], BF16)
                    Wg, b_Wg = sbuf(ph, "C_Wg", [128, 8, 192], BF16)
                    wl = w_in[l]
                    load_w(Wqk, b_Wqk, wl[:, OFF_QC + 96 * c:OFF_QC + 96 * c + 96], 96, 0)
                    load_w(Wqk, b_Wqk, wl[:, OFF_KC + 96 * c:OFF_KC + 96 * c + 96], 96, 96)
                    load_w(Wv, b_Wv, wl[:, OFF_VC + 192 * c:OFF_VC + 192 * c + 192], 192)
                    load_w(Wg, b_Wg, wl[:, OFF_GC + 192 * c:OFF_GC + 192 * c + 192], 192)
                    qkT, b_qkT = sbuf(ph, "C_qkT", [64, 4, S], BF16)
                    vC, b_vC = sbuf(ph, "C_vC", [128, NT, 2, 96], BF16)
                    Rst, b_Rst = sbuf(ph, "C_Rst", [64, NT, 2, 2, 96], BF16)
                    ph_mid = ExitStack()
                    kfb, b_kfb = sbuf(ph_mid, "C_kfb", [128, NT, 2, 2, 48], BF16)
                    Rs = [sbuf(ph_mid, f"C_R{i}", [64, 2, 96], F32) for i in range(2)]
                    cdt = [sbuf(ph_mid, f"C_cd{i}", [64, 2, 96], F32) for i in range(2)]
                    ph2 = ExitStack()
                    KC = 2
                    set_pools(mm=[0, 1, 2, 3, 4], tb=[5, 6, 7])
                    scr = make_scr(ph2, "C", KC)
                    qkr = [sbuf(ph2, f"C_qkr{i}", [128, 4, 48], F32) for i in range(KC)]
                    qkp = [sbuf(ph2, f"C_qkp{i}", [128, 4, 64], BF16) for i in range(KC)]
                    for i in range(KC):
                        op("pool", lambda e, i=i: e.memset(qkp[i][0][:].rearrange("p g d -> p (g d)"), 0.0), [], [qkp[i][1]])

                    def c_proj_body(t):
                        ps, b_ps = ps_mm()
                        proj(tcols(t), Wqk, b_Wqk, 192, ps, b_ps)
                        yield
                        qr, b_qr = qkr[t % KC]
                        qp, b_qp = qkp[t % KC]
                        yield from norm_rope(ps[:, 0:192].rearrange("p (g d) -> p g d", d=48), b_ps, 4, 48, None, None, 24,
                                             rope[:, t, 24:48], rope[:, t, 48:72], qr[:], b_qr, scr[t % KC])
                        ts("pool", qr[:, 2:4, :], qr[:, 2:4, :], 48 ** -0.5, ALU.mult, [b_qr], [b_qr])
                        yield
                        cp("act", qp[:, :, 0:48], qr[:], [b_qr], [b_qp])
                        tt("dve", kfb[:, t, 0, :, :], qr[:, 2:4, :], decs[:, 8 + 2 * c:10 + 2 * c].unsqueeze(2).to_broadcast([128, 2, 48]), ALU.mult, [b_qr, b_decs], [b_kfb])
                        tt("pool", kfb[:, t, 1, :, :], qr[:, 2:4, :], decs[:, 12 + 2 * c:14 + 2 * c].unsqueeze(2).to_broadcast([128, 2, 48]), ALU.mult, [b_qr, b_decs], [b_kfb])
                        yield
                        pb, b_pb = ps_tb()
                        for s_ in range(4):
                            op("pe", lambda e, s_=s_: e.transpose(pb[0:64, s_ * 128:(s_ + 1) * 128], qp[:, s_, :], ident_b[:]), [b_qp, b_idb], [b_pb])
                        yield
                        cp("dve", qkT[:, :, tcols(t)], pb[0:64, 0:512].rearrange("p (g n) -> p g n", n=128), [b_pb], [b_qkT])
                        yield
                        psv, b_psv = ps_mm()
                        proj(tcols(t), Wv, b_Wv, 192, psv, b_psv)
                        yield
                        cp("act", vC[:, t, :, :], psv[:, 0:192].rearrange("p (h d) -> p h d", d=96), [b_psv], [b_vC])
                        yield

                    interleave((c_proj_body(t) for t in range(NT)), KC)
                    P.barrier()
                    ph2.close()
                    set_pools(mm=[0, 1, 2, 3], st=[1, 2, 3], ot=[4, 5, 6, 7])

                    def c_scan(d_):
                        R_, b_R = Rs[d_]
                        cd_, b_cd = cdt[d_]
                        op("pool", lambda e: e.memset(R_[:].rearrange("p h d -> p (h d)"), 0.0), [], [b_R])
                        cp("dve", cd_[0:48, :, :], decs[0:48, 24 + 4 * d_ + 2 * c:26 + 4 * d_ + 2 * c].unsqueeze(2).to_broadcast([48, 2, 96]), [b_decs], [b_cd])
                        yield
                        order = range(NT) if d_ == 0 else range(NT - 1, -1, -1)
                        for n in order:
                            psU, b_psU = ps_mm()
                            for hl in range(2):
                                op("pe", lambda e, psU=psU, n=n, hl=hl: e.matmul(psU[0:48, hl * 96:(hl + 1) * 96], lhsT=kfb[:, n, d_, hl, :], rhs=vC[:, n, hl, :], start=True, stop=True),
                                   [b_kfb, b_vC], [b_psU])
                            cp("dve", Rst[0:48, n, d_, :, :], R_[0:48, :, :], [b_R], [b_Rst])
                            yield
                            tt("dve", R_[0:48, :, :], R_[0:48, :, :], cd_[0:48, :, :], ALU.mult, [b_R, b_cd], [b_R])
                            yield
                            tt("dve", R_[0:48, :, :], psU[0:48, 0:192].rearrange("p (h d) -> p h d", d=96), R_[0:48, :, :], ALU.add, [b_psU, b_R], [b_R])
                            yield

                    interleave((c_scan(d_) for d_ in range(2)), 2)
                    P.barrier()
                    ph_mid.close()
                    set_pools(mm=[6, 7], st=[0, 1, 2], ot=[3, 4, 5])
                    KO = 2
                    att = [sbuf(ph, f"C_att{i}", [128, 128], BF16) for i in range(2 * KO)]
                    oc = [sbuf(ph, f"C_oc{i}", [128, 2, 96], F32) for i in range(KO)]
                    osq = [sbuf(ph, f"C_osq{i}", [128, 2, 96], F32) for i in range(KO)]
                    oss = [sbuf(ph, f"C_oss{i}", [128, 2], F32) for i in range(KO)]
                    gsc = [sbuf(ph, f"C_gsc{i}", [128, 192], F32) for i in range(KO)]
                    ofin = [sbuf(ph, f"C_ofin{i}", [128, 192], BF16) for i in range(KO)]

                    def c_out_head(n, hl, oc_, b_oc):
                        head = 2 * c + hl
                        stp, b_st = ps_st()
                        op("pe", lambda e: e.matmul(stp[:, 0:128], lhsT=qkT[0:48, 2 + hl, tcols(n)], rhs=qkT[0:48, hl, tcols(n)], start=True, stop=True),
                           [b_qkT], [b_st])
                        yield
                        at_, b_at = att[(n % KO) * 2 + hl]
                        tt("dve", at_[:], stp[:, 0:128], DT[:, head, :], ALU.mult, [b_st, b_DT], [b_at])
                        yield
                        psO, b_psO = ps_ot()
                        op("pe", lambda e: e.matmul(psO[:, 0:96], lhsT=at_[:], rhs=vC[:, n, hl, :], start=True, stop=True), [b_at, b_vC], [b_psO])
                        for d_ in range(2):
                            op("pe", lambda e, d_=d_: e.matmul(psO[:, 96 * (d_ + 1):96 * (d_ + 2)], lhsT=qkT[0:48, hl, tcols(n)], rhs=Rst[0:48, n, d_, hl, :],
                                                              start=True, stop=True), [b_qkT, b_Rst], [b_psO])
                        yield
                        cp("act", oc_[:, hl, :], psO[:, 0:96], [b_psO], [b_oc])
                        yield
                        for d_ in range(2):
                            op("dve", lambda e, d_=d_: e.scalar_tensor_tensor(
                                out=oc_[:, hl, :], in0=psO[:, 96 * (d_ + 1):96 * (d_ + 2)], scalar=decs[:, 16 + 4 * d_ + head:17 + 4 * d_ + head], in1=oc_[:, hl, :],
                                op0=ALU.mult, op1=ALU.add), [b_psO, b_decs, b_oc], [b_oc])
                            yield

                    def c_out_body(n):
                        oc_, b_oc = oc[n % KO]
                        for hl in range(2):
                            yield from c_out_head(n, hl, oc_, b_oc)
                        sq_, b_sq = osq[n % KO]
                        ss_, b_ss = oss[n % KO]
                        tt("pool", sq_[:], oc_[:], oc_[:], ALU.mult, [b_oc], [b_sq])
                        yield
                        op("dve", lambda e: e.reduce_sum(out=ss_[:], in_=sq_[:], axis=AX.X), [b_sq], [b_ss])
                        yield
                        rstd_inplace(ss_[:], b_ss, 96.0)
                        yield
                        tt("dve", sq_[:], oc_[:], ss_[:].unsqueeze(2).to_broadcast([128, 2, 96]), ALU.mult, [b_oc, b_ss], [b_sq])
                        yield
                        tt("pool", sq_[:], sq_[:], spt[:, SP_GNC:SP_GNC + 96].unsqueeze(1).to_broadcast([128, 2, 96]), ALU.mult, [b_sq, b_spt], [b_sq])
                        yield
                        yield from gate_and_store(n, sq_[:].rearrange("p h d -> p (h d)"), b_sq, Wg, b_Wg, 192, OCOL_C + 192 * c,
                                                  gsc[n % KO][0], gsc[n % KO][1], ofin[n % KO][0], ofin[n % KO][1])

                    interleave((c_out_body(n) for n in range(NT)), KO)
                    P.barrier()

            if dbg == "o":
                with ExitStack() as ph:
                    ob16, b_ob16 = sbuf(ph, "dbg_ob", [128, 1024], BF16)
                    o32, b_o32 = sbuf(ph, "dbg_o32", [128, 1024], F32)
                    for t in range(NT):
                        P.dma("dbg_l", ob16[:], o_scr[t * 128:(t + 1) * 128, :], reads=[obufs[t]], writes=[b_ob16])
                        cp("dve", o32[:], ob16[:], [b_ob16], [b_o32])
                        P.dma("dbg_s", dbg_o[t * 128:(t + 1) * 128, :], o32[:], reads=[b_o32], writes=[xbufs[t]])
                break

            if "O" in stages:
                with ExitStack() as ph:
                    Wout, b_Wout = sbuf(ph, "Wout", [128, 8, 1024], BF16)
                    load_w(Wout, b_Wout, w_out[l], 1024)
                    KQ = 3
                    set_pools(mm=[0, 1, 2, 3, 4], tb=[5, 6, 7])
                    ott = [sbuf(ph, f"O_ot{i}", [128, 1024], BF16) for i in range(KQ)]
                    oTt = [sbuf(ph, f"O_oT{i}", [128, 8, 128], BF16) for i in range(KQ)]
                    xt = [sbuf(ph, f"O_xt{i}", [128, 1024], F32) for i in range(KQ)]
                    tm = [sbuf(ph, f"O_tm{i}", [128, 1024], F32) for i in range(KQ)]

                    def o_half(oT_t, b_oT, tm_t, b_tm, nh):
                        ps, b_ps = ps_mm()
                        for k in range(8):
                            op("pe", lambda e, k=k: e.matmul(ps[:, :], lhsT=oT_t[:, k, :], rhs=Wout[:, k, nh * 512:(nh + 1) * 512], start=(k == 0), stop=(k == 7)),
                               [b_oT, b_Wout], [b_ps])
                        yield
                        sl = slice(nh * 512, (nh + 1) * 512)
                        tt("dve", tm_t[:, sl], ps[:, :], gate_bc[:, sl], ALU.mult, [b_ps, b_gate], [b_tm])
                        yield

                    def o_body(t):
                        o_t, b_o = ott[t % KQ]
                        oT_t, b_oT = oTt[t % KQ]
                        x_t, b_x = xt[t % KQ]
                        tm_t, b_tm = tm[t % KQ]
                        P.dma("O_ot%d" % (t % KQ), o_t[:], o_scr[t * 128:(t + 1) * 128, :], reads=[obufs[t]], writes=[b_o])
                        P.dma("O_xt%d" % (t % KQ), x_t[:], xsrc[t * 128:(t + 1) * 128, :], reads=[xbufs[t]], writes=[b_x])
                        yield
                        pb, b_pb = ps_tb()
                        for k in range(8):
                            op("pe", lambda e, k=k: e.transpose(pb[:, k * 128:(k + 1) * 128], o_t[:, k * 128:(k + 1) * 128], ident_b[:]), [b_o, b_idb], [b_pb])
                        yield
                        cp("act", oT_t[:], pb[:, :].rearrange("p (k n) -> p k n", n=128), [b_pb], [b_oT])
                        yield
                        for nh in range(2):
                            yield from o_half(oT_t, b_oT, tm_t, b_tm, nh)
                        tt("pool" if t % 2 else "dve", tm_t[:], tm_t[:], x_t[:], ALU.add, [b_tm, b_x], [b_tm])
                        yield
                        P.dma("O_st%d" % (t % KQ), y_out[t * 128:(t + 1) * 128, :], tm_t[:], reads=[b_tm], writes=[xbufs[t]])
                        yield

                    interleave((o_body(t) for t in range(NT)), KQ)
                    P.barrier()

        P.barrier()
        P.emit()
    return nc, P


def host_constants(S):
    NT = S // 128
    pos = np.arange(S, dtype=np.float32)

    def tab(theta, rot):
        inv = (np.float32(theta) ** (-np.arange(0, rot, 2, dtype=np.float32) / np.float32(rot))).astype(np.float32)
        ang = (pos[:, None] * inv[None, :]).astype(np.float32)
        return np.cos(ang).astype(np.float32), np.sin(ang).astype(np.float32)

    cA, sA = tab(ROT_THETA, 8)
    cB, sB = tab(ROT_THETA, 16)
    cC, sC = tab(RET_THETA, 48)
    rope = np.concatenate([cA, sA, cB, sB, cC, sC], axis=1)
    rope = np.ascontiguousarray(rope.reshape(NT, 128, 72).transpose(1, 0, 2)).astype(np.float32)
    a = np.arange(128)[:, None]
    b = np.arange(128)[None, :]
    maskB = np.concatenate([(a - b >= 64), (np.abs(a - b) <= 64), (b - a >= 64)], axis=1).astype(np.float32).astype(ml_dtypes.bfloat16)
    j = a
    i = b
    retc = np.zeros((128, 516), np.float32)
    retc[:, 0:128] = np.maximum(i - j, 0)
    retc[:, 128:256] = (i >= j)
    retc[:, 256:384] = np.maximum(j - i, 0)
    retc[:, 384:512] = (j > i)
    p = np.arange(128, dtype=np.float32)
    retc[:, 512] = p + 1
    retc[:, 513] = 127 - p
    retc[:, 514] = 128 - p
    retc[:, 515] = p
    kmask = np.zeros((128, 4), np.float32)
    for j_ in range(4):
        kmask[32 * j_:32 * j_ + 32, j_] = 1.0
    return {
        "rope": rope, "maskB": maskB, "retc": retc, "kmask": kmask,
        "ident_b": np.eye(128, dtype=np.float32).astype(ml_dtypes.bfloat16),
        "ident_f": np.eye(128, dtype=np.float32),
    }


def pack_small(inp, L):
    parts = [inp["qn_a"], inp["kn_a"], inp["lambda_q1"], inp["lambda_k1"], inp["lambda_q2"], inp["lambda_k2"],
             inp["subln_a"], inp["qn_b"], inp["kn_b"], np.asarray(inp["ret_decay"]).reshape(L, 8), inp["gn_c"]]
    return np.ascontiguousarray(np.concatenate([np.asarray(p_, np.float32).reshape(L, -1) for p_ in parts], axis=1))


def make_in_maps(inp, S, L, B):
    consts = host_constants(S)
    f = lambda k: np.ascontiguousarray(np.asarray(inp[k], np.float32))
    shared = {
        "w_ada": f("w_ada")[:L], "b_ada": f("b_ada")[:L], "norm_g": f("norm_g")[:L], "w_in": f("w_in")[:L], "w_out": f("w_out")[:L],
        "smallp": pack_small({k: np.asarray(v)[:L] for k, v in inp.items() if k not in ("x", "c")}, L),
    }
    shared.update(consts)
    maps = []
    x = f("x")
    c = f("c")
    for b in range(B):
        m = dict(shared)
        m["x"] = np.ascontiguousarray(x[b])
        m["cT"] = np.ascontiguousarray(c[b].reshape(8, 128).T)
        maps.append(m)
    return maps


_CACHE = {}


def kernel(**inputs):
    x = np.asarray(inputs["x"])
    B, S, _ = x.shape
    L = np.asarray(inputs["w_in"]).shape[0]
    key = (S, L)
    if key not in _CACHE:
        _CACHE[key] = build_program(S, L)[0]
    nc = _CACHE[key]
    maps = make_in_maps(inputs, S, L, B)
    in_maps = [maps[i % B] for i in range(8)]
    res = run_bass_kernel_spmd(nc, in_maps, core_ids=list(range(8)))
    out = np.stack([np.asarray(res.results[b]["y"], np.float32) for b in range(B)], axis=0)
    return out
```

```python
import math
from contextlib import ExitStack
import numpy as np
import ml_dtypes
import concourse.bass as bass
import concourse.mybir as mybir
from concourse.bass_utils import run_bass_kernel_spmd

F32 = mybir.dt.float32
BF16 = mybir.dt.bfloat16
ALU = mybir.AluOpType
AF = mybir.ActivationFunctionType
AX = mybir.AxisListType

D_MODEL = 1024
IN_COLS = 3712
EPS = 1e-6
ROT_THETA = 500000.0
RET_THETA = 10000.0
NSMALL = 488
SP_QNA, SP_KNA, SP_LQ1, SP_LK1, SP_LQ2, SP_LK2 = 0, 32, 64, 96, 128, 160
SP_SUB, SP_QNB, SP_KNB, SP_DEC, SP_GNC = 192, 256, 320, 384, 392
OFF_QA, OFF_KA, OFF_VA, OFF_GA = 0, 256, 512, 768
OFF_QB, OFF_KB, OFF_VB, OFF_GB = 1024, 1408, 1792, 2176
OFF_QC, OFF_KC, OFF_VC, OFF_GC = 2560, 2752, 2944, 3328
OCOL_A, OCOL_B, OCOL_C = 0, 256, 640


class Buf:
    __slots__ = ("name", "w", "r", "excl")

    def __init__(self, name, excl=False):
        self.name = name
        self.w = None
        self.r = {}
        self.excl = excl


class Prog:
    ENG = ("pe", "act", "dve", "pool", "sp")

    def __init__(self, nc, stack):
        self.nc = nc
        self.stack = stack
        self.q = {e: [] for e in self.ENG}
        self.cnt = {e: 0 for e in self.ENG}
        self.seen = {e: {} for e in self.ENG}
        self.sems = {}
        self.dcnt = {}
        for e in self.ENG:
            self.sems[e] = stack.enter_context(nc.semaphore("s_" + e))
        self.ninst = 0
        self.desc = {}
        self.total = 0

    def dsem(self, name):
        if name not in self.sems:
            self.sems[name] = self.stack.enter_context(self.nc.semaphore("d_" + name))
            self.dcnt[name] = 0
        return name

    def _wait(self, eng, k, v):
        if k == eng and eng == "pe":
            return
        if self.seen[eng].get(k, 0) < v:
            self.seen[eng][k] = v
            sem = self.sems[k]
            self.ninst += 1
            self.desc.setdefault(eng, []).append("wait %s>=%d" % (k, v))
            self.q[eng].append(lambda e, sem=sem, v=v: e.wait_ge(sem, v))

    def _deps(self, eng, reads, writes):
        need = {}
        for b in reads:
            if b.w is not None:
                k, v = b.w
                if need.get(k, 0) < v:
                    need[k] = v
            if b.excl:
                for k, v in b.r.items():
                    if k != eng and need.get(k, 0) < v:
                        need[k] = v
        for b in writes:
            if b.w is not None:
                k, v = b.w
                if need.get(k, 0) < v:
                    need[k] = v
            for k, v in b.r.items():
                if need.get(k, 0) < v:
                    need[k] = v
        for k, v in need.items():
            self._wait(eng, k, v)

    def op(self, eng, fn, reads=(), writes=()):
        self.total = getattr(self, "total", 0) + 1
        if self.total > DBG.get("cut", 10 ** 9):
            return
        self._deps(eng, reads, writes)
        if DBG.get("serial") and getattr(self, "last", None):
            self._wait(eng, *self.last)
        self.cnt[eng] += 1
        c = self.cnt[eng]
        self.last = (eng, c)
        sem = self.sems[eng]
        self.ninst += 1
        self.desc.setdefault(eng, []).append("op#%d (%s=%d)" % (self.total, eng, c))
        self.q[eng].append(lambda e, fn=fn, sem=sem: fn(e).then_inc(sem, 1))
        for b in writes:
            b.w = (eng, c)
            b.r = {}
        for b in reads:
            if b.w != (eng, c):
                b.r[eng] = c

    def dma(self, semname, out, in_, reads=(), writes=(), qeng="sp", **kw):
        self.total = getattr(self, "total", 0) + 1
        if self.total > DBG.get("cut", 10 ** 9):
            return
        self.dsem(semname)
        self._deps(qeng, reads, writes)
        if DBG.get("serial") and getattr(self, "last", None):
            self._wait(qeng, *self.last)
        self.dcnt[semname] += 16
        v = self.dcnt[semname]
        self.last = (semname, v)
        sem = self.sems[semname]
        self.ninst += 1
        self.desc.setdefault(qeng, []).append("dma#%d (%s=%d)" % (self.total, semname, v))
        self.q[qeng].append(
            lambda e, out=out, in_=in_, sem=sem, kw=kw: e.dma_start(out=out, in_=in_, **kw).then_inc(sem, 16))
        for b in writes:
            b.w = (semname, v)
            b.r = {}
        for b in reads:
            b.r[semname] = v

    def barrier(self):
        for e in self.ENG:
            for k in list(self.sems.keys()):
                v = self.cnt[k] if k in self.cnt else self.dcnt[k]
                if v > 0 and k != e:
                    self._wait(e, k, v)
            if e != "pe" and self.cnt[e] > 0:
                self._wait(e, e, self.cnt[e])

    def emit(self):
        nc = self.nc
        with nc.Block() as block:
            @block.tensor
            def _(e):
                for t in self.q["pe"]:
                    t(e)

            @block.scalar
            def _(e):
                for t in self.q["act"]:
                    t(e)

            @block.vector
            def _(e):
                for t in self.q["dve"]:
                    t(e)

            @block.gpsimd
            def _(e):
                for t in self.q["pool"]:
                    t(e)

            @block.sync
            def _(e):
                for t in self.q["sp"]:
                    t(e)


DBG = {}


def build_program(S, L, stages=("A", "B", "C", "O"), dbg=None):
    NT = S // 128
    nc = bass.Bass("TRN2", target_bir_lowering=False)
    dr = lambda name, shape, dt, kind="ExternalInput": nc.dram_tensor(name, shape, dt, kind=kind).ap()
    x_in = dr("x", [S, D_MODEL], F32)
    cT_in = dr("cT", [128, 8], F32)
    w_ada = dr("w_ada", [L, D_MODEL, 3 * D_MODEL], F32)
    b_ada = dr("b_ada", [L, 3 * D_MODEL], F32)
    norm_g = dr("norm_g", [L, D_MODEL], F32)
    w_in = dr("w_in", [L, D_MODEL, IN_COLS], F32)
    w_out = dr("w_out", [L, D_MODEL, D_MODEL], F32)
    smallp = dr("smallp", [L, NSMALL], F32)
    ident_b_d = dr("ident_b", [128, 128], BF16)
    ident_f_d = dr("ident_f", [128, 128], F32)
    rope_d = dr("rope", [128, NT, 72], F32)
    maskB_d = dr("maskB", [128, 384], BF16)
    retc_d = dr("retc", [128, 516], F32)
    kmask_d = dr("kmask", [128, 4], F32)
    y_out = dr("y", [S, D_MODEL], F32, kind="ExternalOutput")
    o_scr = dr("o_scr", [S, D_MODEL], BF16, kind="ExternalOutput")
    dbg_hT = dr("dbg_hT", [128, 8 * S], F32, kind="ExternalOutput") if dbg == "hT" else None
    dbg_o = dr("dbg_o", [S, D_MODEL], F32, kind="ExternalOutput") if dbg == "o" else None

    with ExitStack() as st:
        P = Prog(nc, st)

        uid = [0]

        def sbuf(stack, name, shape, dt):
            uid[0] += 1
            t = stack.enter_context(nc.sbuf_tensor("sb%d_%s" % (uid[0], name), shape, dt))
            return t, Buf(name)

        def op(eng, fn, reads=(), writes=()):
            if eng == "pool" and DBG.get("nopool"):
                eng = "dve"
            P.op(eng, fn, reads, writes)

        def cp(eng, out, in_, reads, writes):
            if eng == "act":
                op("act", lambda e: e.copy(out=out, in_=in_), reads, writes)
            else:
                op(eng, lambda e: e.tensor_copy(out=out, in_=in_), reads, writes)

        def tt(eng, out, in0, in1, alu, reads, writes):
            op(eng, lambda e: e.tensor_tensor(out=out, in0=in0, in1=in1, op=alu), reads, writes)

        def ts(eng, out, in0, s1, alu, reads, writes):
            op(eng, lambda e: e.tensor_scalar(out=out, in0=in0, scalar1=s1, scalar2=None, op0=alu), reads, writes)

        hT, b_hT = sbuf(st, "hT", [128, 8, S], BF16)
        rope, b_rope = sbuf(st, "rope", [128, NT, 72], F32)
        ident_b, b_idb = sbuf(st, "ident_b", [128, 128], BF16)
        ident_f, b_idf = sbuf(st, "ident_f", [128, 128], F32)
        maskB, b_maskB = sbuf(st, "maskB", [128, 384], BF16)
        retc, b_retc = sbuf(st, "retc", [128, 516], F32)
        kmask, b_kmask = sbuf(st, "kmask", [128, 4], F32)
        cs, b_cs = sbuf(st, "cs", [128, 8], F32)
        ones_row, b_ones = sbuf(st, "ones_row", [1, 128], F32)
        gate_bc, b_gate = sbuf(st, "gate_bc", [128, 1024], F32)
        spt, b_spt = sbuf(st, "spt", [128, NSMALL], F32)
        gA, b_gA = sbuf(st, "gA", [128, 8, 32], F32)
        gB, b_gB = sbuf(st, "gB", [128, 4, 64], F32)
        gS, b_gS = sbuf(st, "gS", [128, 2, 64], F32)
        lamt, b_lam = sbuf(st, "lamt", [128, 8], F32)
        decs, b_decs = sbuf(st, "decs", [128, 64], F32)
        DT, b_DT = sbuf(st, "DT", [128, 4, 128], F32)
        eps_t, b_eps = sbuf(st, "eps_t", [128, 1], F32)
        wstage = [sbuf(st, f"wstage{i}", [128, 8, 256], F32) for i in range(2)]
        state = {"ws": 0, "mm": 0, "st": 0, "ot": 0, "tb": 0, "scr": 0}

        psf = []
        psbv = []
        for i in range(8):
            t_ = st.enter_context(nc.psum_tensor(f"psf{i}", [128, 512], F32))
            b_ = Buf(f"psf{i}", True)
            psf.append((t_, b_))
            psbv.append((t_.bitcast(BF16), b_))
        pools = {"mm": [0, 1, 2, 3, 4, 5], "st": [1, 2, 3], "ot": [4, 5, 6, 7], "tb": [6, 7]}
        ppos = {"mm": 0, "st": 0, "ot": 0, "tb": 0}

        def set_pools(**kw):
            for k_, v_ in kw.items():
                pools[k_] = list(v_)
                ppos[k_] = 0

        def _rot(kind):
            lst = pools[kind]
            i = lst[ppos[kind] % len(lst)]
            ppos[kind] += 1
            return i

        def ps_mm():
            return psf[_rot("mm")]

        def ps_st():
            return psf[_rot("st")]

        def ps_ot():
            return psf[_rot("ot")]

        def ps_tb():
            return psbv[_rot("tb")]

        def interleave(gens, K):
            active = []
            it = iter(gens)
            more = True
            while True:
                while more and len(active) < K:
                    try:
                        active.append(next(it))
                    except StopIteration:
                        more = False
                if not active:
                    break
                for g_ in list(active):
                    try:
                        next(g_)
                    except StopIteration:
                        active.remove(g_)

        P.dma("c_rope", rope[:], rope_d[:, :, :], writes=[b_rope])
        P.dma("c_idb", ident_b[:], ident_b_d[:, :], writes=[b_idb])
        P.dma("c_idf", ident_f[:], ident_f_d[:, :], writes=[b_idf])
        P.dma("c_mb", maskB[:], maskB_d[:, :], writes=[b_maskB])
        P.dma("c_rc", retc[:], retc_d[:, :], writes=[b_retc])
        P.dma("c_km", kmask[:], kmask_d[:, :], writes=[b_kmask])
        P.dma("c_cs", cs[:], cT_in[:, :], writes=[b_cs])
        cse, b_cse = sbuf(st, "cse", [128, 8], F32)
        op("act", lambda e: e.activation(out=cse[:], in_=cs[:], func=AF.Exp, scale=-1.0), [b_cs], [b_cse])
        ts("dve", cse[:], cse[:], 1.0, ALU.add, [b_cse], [b_cse])
        op("dve", lambda e: e.reciprocal(out=cse[:], in_=cse[:]), [b_cse], [b_cse])
        tt("dve", cs[:], cs[:], cse[:], ALU.mult, [b_cs, b_cse], [b_cs])
        op("pool", lambda e: e.memset(ones_row[:], 1.0), [], [b_ones])
        op("pool", lambda e: e.memset(eps_t[:], EPS), [], [b_eps])
        one_t, b_one = sbuf(st, "one_t", [128, 1], F32)
        op("pool", lambda e: e.memset(one_t[:], 1.0), [], [b_one])

        def load_w(dst, b_dst, src_ap, ncols, col0=0):
            done = 0
            while done < ncols:
                n = min(256, ncols - done)
                i = state["ws"]
                state["ws"] ^= 1
                stg, b_stg = wstage[i]
                P.dma("ws%d" % i, stg[:, :, 0:n], src_ap[:, done:done + n].rearrange("(k p) n -> p k n", p=128), writes=[b_stg])
                cp("act" if i else "dve", dst[:, :, col0 + done:col0 + done + n], stg[:, :, 0:n], [b_stg], [b_dst])
                done += n

        def proj(t_cols, W, b_W, n, ps, b_ps, wcol0=0, pcol0=0):
            for k in range(8):
                op("pe", lambda e, k=k: e.matmul(ps[:, pcol0:pcol0 + n], lhsT=hT[:, k, t_cols], rhs=W[:, k, wcol0:wcol0 + n],
                                                start=(k == 0), stop=(k == 7)), [b_hT, b_W], [b_ps])

        def rstd_inplace(ss, b_ss, d):
            op("act", lambda e: e.activation(out=ss, in_=ss, func=AF.Ln, scale=1.0 / d, bias=eps_t[:, 0:1]), [b_ss, b_eps], [b_ss])
            op("act", lambda e: e.activation(out=ss, in_=ss, func=AF.Exp, scale=-0.5), [b_ss], [b_ss])

        def tcols(t):
            return slice(t * 128, (t + 1) * 128)

        def norm_rope(src3, b_src, G, d, gains, b_g, half, cos, sin, out3, b_out, scr):
            rot = 2 * half
            sq, b_sq = scr["sq"]
            xn, b_xn = scr["xn"]
            ssx, b_ssx = scr["ss"]
            xn3 = xn[:, 0:G * d].rearrange("p (g d) -> p g d", d=d)
            if gains is not None:
                sq3 = sq[:, 0:G * d].rearrange("p (g d) -> p g d", d=d)
                op("act", lambda e: e.activation(out=sq3, in_=src3, func=AF.Square), [b_src], [b_sq])
                yield
                op("dve", lambda e: e.reduce_sum(out=ssx[:, 0:G], in_=sq3, axis=AX.X), [b_sq], [b_ssx])
                yield
                rstd_inplace(ssx[:, 0:G], b_ssx, float(d))
                yield
                tt("dve", xn3, src3, ssx[:, 0:G].unsqueeze(2).to_broadcast([128, G, d]), ALU.mult, [b_src, b_ssx], [b_xn])
                yield
                tt("pool", xn3, xn3, gains, ALU.mult, [b_xn, b_g], [b_xn])
                yield
            else:
                cp("act", xn3, src3, [b_src], [b_xn])
                yield
            x1 = xn3[:, :, 0:half]
            x2 = xn3[:, :, half:rot]
            cb = cos.unsqueeze(1).to_broadcast([128, G, half])
            sb_ = sin.unsqueeze(1).to_broadcast([128, G, half])
            tv = []
            for i in range(4):
                tq, b_tq = scr["t%d" % i]
                tv.append((tq[:, 0:G * half].rearrange("p (g d) -> p g d", d=half), b_tq))
            tt("dve", tv[0][0], x1, cb, ALU.mult, [b_xn, b_rope], [tv[0][1]])
            tt("pool", tv[1][0], x2, sb_, ALU.mult, [b_xn, b_rope], [tv[1][1]])
            tt("pool", tv[2][0], x1, sb_, ALU.mult, [b_xn, b_rope], [tv[2][1]])
            tt("dve", tv[3][0], x2, cb, ALU.mult, [b_xn, b_rope], [tv[3][1]])
            if rot < d:
                cp("act", out3[:, :, rot:d], xn3[:, :, rot:d], [b_xn], [b_out])
            yield
            tt("dve", out3[:, :, 0:half], tv[0][0], tv[1][0], ALU.subtract, [tv[0][1], tv[1][1]], [b_out])
            tt("pool", out3[:, :, half:rot], tv[2][0], tv[3][0], ALU.add, [tv[2][1], tv[3][1]], [b_out])
            yield

        def make_scr(ph, tag, n):
            out = []
            for i in range(n):
                s = {}
                s["sq"] = sbuf(ph, f"{tag}sq{i}", [128, 256], F32)
                s["xn"] = sbuf(ph, f"{tag}xn{i}", [128, 256], F32)
                s["ss"] = sbuf(ph, f"{tag}ss{i}", [128, 8], F32)
                for j in range(4):
                    s["t%d" % j] = sbuf(ph, f"{tag}t{j}_{i}", [128, 96], F32)
                out.append(s)
            return out

        def gate_and_store(t, on2d, b_on, Wg, b_Wg, n, ocol, gsc, b_gsc, ofin, b_ofin):
            psg, b_psg = ps_mm()
            proj(tcols(t), Wg, b_Wg, n, psg, b_psg)
            yield
            op("act", lambda e: e.activation(out=gsc[:, 0:n], in_=psg[:, 0:n], func=AF.Exp, scale=-1.0), [b_psg], [b_gsc])
            yield
            op("act", lambda e: e.activation(out=gsc[:, 0:n], in_=gsc[:, 0:n], func=AF.Ln, bias=one_t[:, 0:1]), [b_gsc, b_one], [b_gsc])
            yield
            op("act", lambda e: e.activation(out=gsc[:, 0:n], in_=gsc[:, 0:n], func=AF.Exp, scale=-1.0), [b_gsc], [b_gsc])
            yield
            tt("dve", gsc[:, 0:n], psg[:, 0:n], gsc[:, 0:n], ALU.mult, [b_psg, b_gsc], [b_gsc])
            yield
            tt("dve", ofin[:, 0:n], on2d, gsc[:, 0:n], ALU.mult, [b_on, b_gsc], [b_ofin])
            yield
            P.dma("ofin_" + b_ofin.name, o_scr[t * 128:(t + 1) * 128, ocol:ocol + n], ofin[:, 0:n], reads=[b_ofin], writes=[obufs[t]])
            yield

        xbufs = [Buf(f"x{t}") for t in range(NT)]
        obufs = [Buf(f"o{t}") for t in range(NT)]

        for l in range(L):
            lam_init = 0.8 - 0.6 * math.exp(-0.3 * l)
            xsrc = x_in if l == 0 else y_out
            P.dma("spt", spt[:], smallp[l:l + 1, :].partition_broadcast(128), writes=[b_spt])
            cp("dve", gA[:, 0:4, :], spt[:, SP_QNA:SP_QNA + 32].unsqueeze(1).to_broadcast([128, 4, 32]), [b_spt], [b_gA])
            cp("dve", gA[:, 4:8, :], spt[:, SP_KNA:SP_KNA + 32].unsqueeze(1).to_broadcast([128, 4, 32]), [b_spt], [b_gA])
            cp("dve", gB[:, 0:2, :], spt[:, SP_QNB:SP_QNB + 64].unsqueeze(1).to_broadcast([128, 2, 64]), [b_spt], [b_gB])
            cp("dve", gB[:, 2:4, :], spt[:, SP_KNB:SP_KNB + 64].unsqueeze(1).to_broadcast([128, 2, 64]), [b_spt], [b_gB])
            ts("dve", gS[:], spt[:, SP_SUB:SP_SUB + 64].unsqueeze(1).to_broadcast([128, 2, 64]), 1.0 - lam_init, ALU.mult, [b_spt], [b_gS])
            tt("dve", decs[:, 0:32], spt[:, SP_LQ1:SP_LQ1 + 32], spt[:, SP_LK1:SP_LK1 + 32], ALU.mult, [b_spt], [b_decs])
            op("dve", lambda e: e.reduce_sum(out=lamt[:, 0:1], in_=decs[:, 0:32], axis=AX.X), [b_decs], [b_lam])
            tt("dve", decs[:, 32:64], spt[:, SP_LQ2:SP_LQ2 + 32], spt[:, SP_LK2:SP_LK2 + 32], ALU.mult, [b_spt], [b_decs])
            op("dve", lambda e: e.reduce_sum(out=lamt[:, 1:2], in_=decs[:, 32:64], axis=AX.X), [b_decs], [b_lam])
            op("act", lambda e: e.activation(out=lamt[:, 2:4], in_=lamt[:, 0:2], func=AF.Exp), [b_lam], [b_lam])
            tt("dve", lamt[:, 4:5], lamt[:, 3:4], lamt[:, 2:3], ALU.subtract, [b_lam], [b_lam])
            ts("dve", lamt[:, 6:7], lamt[:, 4:5], -lam_init, ALU.add, [b_lam], [b_lam])
            op("act", lambda e: e.activation(out=decs[:, 0:8], in_=spt[:, SP_DEC:SP_DEC + 8], func=AF.Exp, scale=-1.0), [b_spt, b_decs], [b_decs])
            ts("dve", decs[:, 0:8], decs[:, 0:8], 1.0, ALU.add, [b_decs], [b_decs])
            op("act", lambda e: e.activation(out=decs[:, 0:8], in_=decs[:, 0:8], func=AF.Ln), [b_decs], [b_decs])
            ts("dve", decs[:, 0:8], decs[:, 0:8], -1.0, ALU.mult, [b_decs], [b_decs])
            for (o0, i0, rc) in ((8, 0, 513), (12, 4, 515), (16, 0, 512), (20, 4, 514)):
                op("act", lambda e, o0=o0, i0=i0, rc=rc: e.activation(out=decs[:, o0:o0 + 4], in_=decs[:, i0:i0 + 4], func=AF.Exp, scale=retc[:, rc:rc + 1]),
                   [b_decs, b_retc], [b_decs])
            op("act", lambda e: e.activation(out=decs[:, 24:32], in_=decs[:, 0:8], func=AF.Exp, scale=128.0), [b_decs], [b_decs])
            ts("dve", decs[:, 8:16], decs[:, 8:16], 48 ** -0.5, ALU.mult, [b_decs], [b_decs])
            with ExitStack() as ph:
                dtmp, b_dtmp = sbuf(ph, "dtmp", [128, 128], F32)
                for h in range(4):
                    op("act", lambda e, h=h: e.activation(out=DT[:, h, :], in_=retc[:, 0:128], func=AF.Exp, scale=decs[:, h:h + 1]), [b_decs, b_retc], [b_DT])
                    tt("dve", DT[:, h, :], DT[:, h, :], retc[:, 128:256], ALU.mult, [b_DT, b_retc], [b_DT])
                    op("act", lambda e, h=h: e.activation(out=dtmp[:], in_=retc[:, 256:384], func=AF.Exp, scale=decs[:, 4 + h:5 + h]), [b_decs, b_retc], [b_dtmp])
                    tt("dve", dtmp[:], dtmp[:], retc[:, 384:512], ALU.mult, [b_dtmp, b_retc], [b_dtmp])
                    op("dve", lambda e, h=h: e.scalar_tensor_tensor(out=DT[:, h, :], in0=DT[:, h, :], scalar=1.0, in1=dtmp[:], op0=ALU.mult, op1=ALU.add), [b_dtmp, b_DT], [b_DT])
                    ts("dve", DT[:, h, :], DT[:, h, :], 48 ** -0.5, ALU.mult, [b_DT], [b_DT])
                P.barrier()

            ph_norm = ExitStack()
            gs_bc, b_gs = sbuf(ph_norm, "gs_bc", [128, 1024], F32)
            shift_bc, b_shift = sbuf(ph_norm, "shift_bc", [128, 1024], F32)
            ph_ada = ExitStack()
            modrow, b_modrow = sbuf(ph_ada, "modrow", [1, 3072], F32)
            bada, b_bada = sbuf(ph_ada, "bada", [1, 3072], F32)
            P.dma("bada", bada[:], b_ada[l:l + 1, :], writes=[b_bada])
            for nt in range(12):
                i = state["ws"]
                state["ws"] ^= 1
                stg, b_stg = wstage[i]
                P.dma("ws%d" % i, stg[:, :, :], w_ada[l, :, nt * 256:(nt + 1) * 256].rearrange("(k p) n -> p k n", p=128), writes=[b_stg])
                ps, b_ps = ps_mm()
                for k in range(8):
                    op("pe", lambda e, k=k, stg=stg, ps=ps: e.matmul(ps[0:1, 0:256], lhsT=cs[:, k:k + 1], rhs=stg[:, k, :], start=(k == 0), stop=(k == 7)),
                       [b_cs, b_stg], [b_ps])
                tt("dve", modrow[0:1, nt * 256:(nt + 1) * 256], ps[0:1, 0:256], bada[0:1, nt * 256:(nt + 1) * 256], ALU.add, [b_ps, b_bada], [b_modrow])
            with ExitStack() as ph:
                g_bc, b_gbc = sbuf(ph, "g_bc", [128, 1024], F32)
                P.dma("gbc", g_bc[:], norm_g[l:l + 1, :].partition_broadcast(128), writes=[b_gbc])
                for j in range(6):
                    ps, b_ps = ps_mm()
                    op("pe", lambda e, ps=ps, j=j: e.matmul(ps[:, :], lhsT=ones_row[0:1, :], rhs=modrow[0:1, j * 512:(j + 1) * 512], start=True, stop=True),
                       [b_ones, b_modrow], [b_ps])
                    sl = slice((j % 2) * 512, (j % 2 + 1) * 512)
                    if j < 2:
                        cp("act", shift_bc[:, sl], ps[:, :], [b_ps], [b_shift])
                    elif j < 4:
                        op("dve", lambda e, ps=ps, sl=sl: e.scalar_tensor_tensor(out=gs_bc[:, sl], in0=ps[:, :], scalar=1.0, in1=g_bc[:, sl], op0=ALU.add, op1=ALU.mult),
                           [b_ps, b_gbc], [b_gs])
                    else:
                        cp("act", gate_bc[:, sl], ps[:, :], [b_ps], [b_gate])
                P.barrier()
            ph_ada.close()

            with ExitStack() as ph:
                KN = 3
                set_pools(mm=[0, 1, 2, 3, 4, 5], tb=[5, 6, 7])
                xt = [sbuf(ph, f"xt{i}", [128, 1024], F32) for i in range(KN)]
                junk, b_junk = sbuf(ph, "junk", [128, 1024], BF16)
                h1 = [sbuf(ph, f"h1_{i}", [128, 1024], F32) for i in range(KN)]
                hb = [sbuf(ph, f"hb{i}", [128, 1024], BF16) for i in range(KN)]
                ssn = [sbuf(ph, f"ssn{i}", [128, 1], F32) for i in range(KN)]

                def norm_body(t):
                    x_t, b_x = xt[t % KN]
                    h_t, b_h = h1[t % KN]
                    hb_t, b_hb = hb[t % KN]
                    ss, b_ss = ssn[t % KN]
                    P.dma("xt%d" % (t % KN), x_t[:], xsrc[t * 128:(t + 1) * 128, :], reads=[xbufs[t]], writes=[b_x])
                    yield
                    op("act", lambda e: e.activation(out=junk[:], in_=x_t[:], func=AF.Square, accum_out=ss[:]), [b_x], [b_junk, b_ss])
                    yield
                    rstd_inplace(ss[:], b_ss, 1024.0)
                    yield
                    op("dve", lambda e: e.scalar_tensor_tensor(out=h_t[:], in0=x_t[:], scalar=ss[:, 0:1], in1=gs_bc[:], op0=ALU.mult, op1=ALU.mult),
                       [b_x, b_ss, b_gs], [b_h])
                    yield
                    tt("pool" if t % 2 else "dve", hb_t[:], h_t[:], shift_bc[:], ALU.add, [b_h, b_shift], [b_hb])
                    yield
                    pb, b_pb = ps_tb()
                    for k in range(8):
                        op("pe", lambda e, k=k: e.transpose(pb[:, k * 128:(k + 1) * 128], hb_t[:, k * 128:(k + 1) * 128], ident_b[:]),
                           [b_hb, b_idb], [b_pb])
                    yield
                    cp("act" if t % 2 else "dve", hT[:, :, tcols(t)], pb[:, :].rearrange("p (k n) -> p k n", n=128), [b_pb], [b_hT])
                    yield

                interleave((norm_body(t) for t in range(NT)), KN)
                P.barrier()
            ph_norm.close()
            if dbg == "hT":
                with ExitStack() as ph:
                    d32, b_d32 = sbuf(ph, "d32", [128, 8 * S], F32)
                    cp("dve", d32[:], hT[:].rearrange("p k s -> p (k s)"), [b_hT], [b_d32])
                    P.dma("dbg", dbg_hT[:, :], d32[:], reads=[b_d32], writes=[xbufs[0]])
                break

            for c in range(2 if "A" in stages else 0):
                with ExitStack() as ph:
                    Wqk, b_Wqk = sbuf(ph, "A_Wqk", [128, 8, 256], BF16)
                    Wv, b_Wv = sbuf(ph, "A_Wv", [128, 8, 128], BF16)
                    Wg, b_Wg = sbuf(ph, "A_Wg", [128, 8, 128], BF16)
                    wl = w_in[l]
                    load_w(Wqk, b_Wqk, wl[:, OFF_QA + 128 * c:OFF_QA + 128 * c + 128], 128, 0)
                    load_w(Wqk, b_Wqk, wl[:, OFF_KA + 128 * c:OFF_KA + 128 * c + 128], 128, 128)
                    load_w(Wv, b_Wv, wl[:, OFF_VA + 128 * c:OFF_VA + 128 * c + 128], 128)
                    load_w(Wg, b_Wg, wl[:, OFF_GA + 128 * c:OFF_GA + 128 * c + 128], 128)
                    qT, b_qT = sbuf(ph, "A_qT", [128, S], BF16)
                    kps = [sbuf(ph, f"A_kp{j_}", [128, S], BF16) for j_ in range(4)]
                    vaug, b_vaug = sbuf(ph, "A_vaug", [128, NT, 2, 66], BF16)
                    if not DBG.get("A_nomemset"):
                        op("pool", lambda e: e.memset(vaug[:, :, :, 64:65], 1.0), [], [b_vaug])
                    ph2 = ExitStack()
                    KA = 3
                    set_pools(mm=[0, 1, 2, 3, 4], tb=[5, 6, 7])
                    scr = make_scr(ph2, "A", KA)
                    qkb = [sbuf(ph2, f"A_qkb{i}", [128, 256], BF16) for i in range(KA)]

                    def a_proj_body(t):
                        ps, b_ps = ps_mm()
                        proj(tcols(t), Wqk, b_Wqk, 256, ps, b_ps)
                        yield
                        qk_t, b_qk = qkb[t % KA]
                        yield from norm_rope(ps[:, 0:256].rearrange("p (g d) -> p g d", d=32), b_ps, 8, 32, gA[:], b_gA, 4,
                                             rope[:, t, 0:4], rope[:, t, 4:8], qk_t[:].rearrange("p (g d) -> p g d", d=32), b_qk, scr[t % KA])
                        pb, b_pb = ps_tb()
                        for j in range(2):
                            op("pe", lambda e, j=j: e.transpose(pb[:, j * 128:(j + 1) * 128], qk_t[:, j * 128:(j + 1) * 128], ident_b[:]),
                               [b_qk, b_idb], [b_pb])
                        yield
                        cp("act", qT[:, tcols(t)], pb[:, 0:128], [b_pb], [b_qT])
                        for j_ in range(4):
                            ts("dve", kps[j_][0][:, tcols(t)], pb[:, 128:256], kmask[:, j_:j_ + 1], ALU.mult, [b_pb, b_kmask], [kps[j_][1]])
                        yield
                        psv, b_psv = ps_mm()
                        proj(tcols(t), Wv, b_Wv, 128, psv, b_psv)
                        yield
                        cp("act", vaug[:, t, :, 0:64], psv[:, 0:128].rearrange("p (h d) -> p h d", d=64), [b_psv], [b_vaug])
                        yield

                    interleave((a_proj_body(t) for t in range(NT)), KA)
                    P.barrier()
                    ph2.close()
                    set_pools(mm=[0], st=[1, 2, 3], ot=[4, 5, 6, 7])
                    Et = [sbuf(ph, f"A_E{i}", [128, 512], BF16) for i in range(4)]
                    OTs = [sbuf(ph, f"A_OTs{i}", [65, 2, 512], F32) for i in range(2)]
                    opre = [sbuf(ph, f"A_opre{i}", [128, 4, 2, 64], F32) for i in range(2)]
                    KE = 3
                    tA = [sbuf(ph, f"A_tA{i}", [128, 64], F32) for i in range(KE)]
                    tB = [sbuf(ph, f"A_tB{i}", [128, 64], F32) for i in range(KE)]
                    rden = [sbuf(ph, f"A_rden{i}", [128, 2], F32) for i in range(KE)]
                    osq = [sbuf(ph, f"A_osq{i}", [128, 128], F32) for i in range(KE)]
                    oss = [sbuf(ph, f"A_oss{i}", [128, 2], F32) for i in range(KE)]
                    onn = [sbuf(ph, f"A_on{i}", [128, 128], F32) for i in range(KE)]
                    gsc = [sbuf(ph, f"A_gsc{i}", [128, 128], F32) for i in range(KE)]
                    ofin = [sbuf(ph, f"A_ofin{i}", [128, 128], BF16) for i in range(KE)]
                    side = []
                    cnt_e = [0]

                    def a_epi(g, hl, ots, b_ots):
                        for t4 in range(4):
                            yield from a_epi_tile(g, hl, ots, b_ots, t4)

                    def a_epi_tile(g, hl, ots, b_ots, t4):
                        opre_, b_opre = opre[g % 2]
                        if True:
                            i_ = cnt_e[0] % KE
                            cnt_e[0] += 1
                            psT, b_psT = ps_mm()
                            for m in range(2):
                                op("pe", lambda e, m=m: e.transpose(psT[:, m * 65:(m + 1) * 65], ots[0:65, m, t4 * 128:(t4 + 1) * 128], ident_f[0:65, 0:65]),
                                   [b_ots, b_idf], [b_psT])
                            yield
                            rd, b_rd = rden[i_]
                            ta, b_ta = tA[i_]
                            tb_, b_tb = tB[i_]
                            op("dve", lambda e: e.reciprocal(out=rd[:], in_=psT[:, 0:130].rearrange("p (m e) -> p m e", e=65)[:, :, 64]), [b_psT], [b_rd])
                            yield
                            ts("dve", ta[:], psT[:, 0:64], rd[:, 0:1], ALU.mult, [b_psT, b_rd], [b_ta])
                            yield
                            op("act", lambda e: e.activation(out=tb_[:], in_=psT[:, 65:129], func=AF.Copy, scale=rd[:, 1:2]), [b_psT, b_rd], [b_tb])
                            yield
                            op("dve", lambda e: e.scalar_tensor_tensor(out=opre_[:, t4, hl, :], in0=tb_[:], scalar=lamt[:, 6:7], in1=ta[:],
                                                                       op0=ALU.mult, op1=ALU.add), [b_tb, b_ta, b_lam], [b_opre])
                            yield

                    def a_fin(g):
                        for t4 in range(4):
                            yield from a_fin_tile(g, t4)

                    def a_fin_tile(g, t4):
                        opre_, b_opre = opre[g % 2]
                        if True:
                            t = g * 4 + t4
                            i_ = cnt_e[0] % KE
                            cnt_e[0] += 1
                            sq_, b_sq = osq[i_]
                            ss_, b_ss = oss[i_]
                            on_, b_on = onn[i_]
                            o2 = opre_[:, t4, :, :]
                            tt("pool", sq_[:].rearrange("p (h d) -> p h d", d=64), o2, o2, ALU.mult, [b_opre], [b_sq])
                            yield
                            op("dve", lambda e: e.reduce_sum(out=ss_[:], in_=sq_[:].rearrange("p (h d) -> p h d", d=64), axis=AX.X), [b_sq], [b_ss])
                            yield
                            rstd_inplace(ss_[:], b_ss, 64.0)
                            yield
                            on3 = on_[:].rearrange("p (h d) -> p h d", d=64)
                            tt("dve", on3, o2, ss_[:].unsqueeze(2).to_broadcast([128, 2, 64]), ALU.mult, [b_opre, b_ss], [b_on])
                            yield
                            tt("pool", on3, on3, gS[:], ALU.mult, [b_on, b_gS], [b_on])
                            yield
                            yield from gate_and_store(t, on_[:], b_on, Wg, b_Wg, 128, OCOL_A + 128 * c, gsc[i_][0], gsc[i_][1], ofin[i_][0], ofin[i_][1])

                    def a_main():
                        LA = 2
                        items = [(g, hl, m, kt) for g in range(S // 512) for hl in range(2) for m in range(2) for kt in range(NT)]
                        pend = []
                        otmap = {}
                        for idx in range(len(items) + LA):
                            if idx < len(items):
                                g, hl, m, kt = items[idx]
                                kp, b_kp = kps[2 * hl + m]
                                stp, b_st = ps_st()
                                op("pe", lambda e, stp=stp, kp=kp, kt=kt, g=g: e.matmul(
                                    stp[:, :], lhsT=kp[:, tcols(kt)], rhs=qT[:, g * 512:(g + 1) * 512], start=True, stop=True),
                                   [b_kp, b_qT], [b_st])
                                E, b_E = Et[idx % 4]
                                op("act", lambda e, E=E, stp=stp: e.activation(out=E[:], in_=stp[:], func=AF.Exp, scale=32 ** -0.5), [b_st], [b_E])
                                pend.append((g, hl, m, kt, E, b_E))
                            if idx >= LA:
                                g, hl, m, kt, E, b_E = pend.pop(0)
                                if kt == 0:
                                    otmap[(g, hl, m)] = ps_ot()
                                ot, b_ot = otmap[(g, hl, m)]
                                op("pe", lambda e, ot=ot, kt=kt, hl=hl, E=E: e.matmul(ot[0:65, :], lhsT=vaug[:, kt, hl, 0:65], rhs=E[:], start=(kt == 0), stop=(kt == NT - 1)),
                                   [b_vaug, b_E], [b_ot])
                                if kt == NT - 1:
                                    slot = (g * 2 + hl) % 2
                                    ots, b_ots = OTs[slot]
                                    while any(tg == slot for tg, _ in side):
                                        try:
                                            next(side[0][1])
                                        except StopIteration:
                                            side.pop(0)
                                    cp("dve", ots[0:65, m, :], ot[0:65, :], [b_ot], [b_ots])
                                    if m == 1:
                                        side.append((slot, a_epi(g, hl, ots, b_ots)))
                                        if hl == 1:
                                            side.append((None, a_fin(g)))
                            yield

                    for _ in a_main():
                        if side:
                            try:
                                next(side[0][1])
                            except StopIteration:
                                side.pop(0)
                    while side:
                        try:
                            next(side[0][1])
                        except StopIteration:
                            side.pop(0)
                    P.barrier()

            for c in range(3 if "B" in stages else 0):
                with ExitStack() as ph:
                    Wqk, b_Wqk = sbuf(ph, "B_Wqk", [128, 8, 256], BF16)
                    Wv, b_Wv = sbuf(ph, "B_Wv", [128, 8, 128], BF16)
                    Wg, b_Wg = sbuf(ph, "B_Wg", [128, 8, 128], BF16)
                    wl = w_in[l]
                    load_w(Wqk, b_Wqk, wl[:, OFF_QB + 128 * c:OFF_QB + 128 * c + 128], 128, 0)
                    load_w(Wqk, b_Wqk, wl[:, OFF_KB + 128 * c:OFF_KB + 128 * c + 128], 128, 128)
                    load_w(Wv, b_Wv, wl[:, OFF_VB + 128 * c:OFF_VB + 128 * c + 128], 128)
                    load_w(Wg, b_Wg, wl[:, OFF_GB + 128 * c:OFF_GB + 128 * c + 128], 128)
                    qT, b_qT = sbuf(ph, "B_qT", [128, S], BF16)
                    kT, b_kT = sbuf(ph, "B_kT", [128, S], BF16)
                    vB, b_vB = sbuf(ph, "B_vB", [128, 3, NT, 2, 66], BF16)
                    op("pool", lambda e: e.memset(vB[:, :, :, :, 64:65].rearrange("p a b c d -> p (a b c d)"), 1.0), [], [b_vB])
                    ph2 = ExitStack()
                    KB = 3
                    set_pools(mm=[0, 1, 2, 3, 4], tb=[5, 6, 7])
                    scr = make_scr(ph2, "B", KB)
                    qkb = [sbuf(ph2, f"B_qkb{i}", [128, 256], BF16) for i in range(KB)]

                    def b_proj_body(t):
                        ps, b_ps = ps_mm()
                        proj(tcols(t), Wqk, b_Wqk, 256, ps, b_ps)
                        yield
                        qk_t, b_qk = qkb[t % KB]
                        yield from norm_rope(ps[:, 0:256].rearrange("p (g d) -> p g d", d=64), b_ps, 4, 64, gB[:], b_gB, 8,
                                             rope[:, t, 8:16], rope[:, t, 16:24], qk_t[:].rearrange("p (g d) -> p g d", d=64), b_qk, scr[t % KB])
                        pb, b_pb = ps_tb()
                        for j in range(2):
                            op("pe", lambda e, j=j: e.transpose(pb[:, j * 128:(j + 1) * 128], qk_t[:, j * 128:(j + 1) * 128], ident_b[:]),
                               [b_qk, b_idb], [b_pb])
                        yield
                        cp("act", qT[:, tcols(t)], pb[:, 0:128], [b_pb], [b_qT])
                        cp("act", kT[:, tcols(t)], pb[:, 128:256], [b_pb], [b_kT])
                        yield

                    interleave((b_proj_body(t) for t in range(NT)), KB)
                    vi = 0
                    for gi, D in enumerate((1, 4, 16)):
                        ntl = S // D // 128
                        for r in range(D):
                            for j in range(ntl):
                                psv, b_psv = ps_mm()
                                c0 = r + D * 128 * j
                                proj(slice(c0, c0 + D * 127 + 1, D), Wv, b_Wv, 128, psv, b_psv)
                                cp("act" if vi % 2 else "dve", vB[:, gi, r * ntl + j, :, 0:64], psv[:, 0:128].rearrange("p (h d) -> p h d", d=64), [b_psv], [b_vB])
                                vi += 1
                    P.barrier()
                    ph2.close()
                    set_pools(mm=[0, 7], st=[1, 2, 3], ot=[4, 5, 6])
                    Et = [sbuf(ph, f"B_E{i}", [128, 384], BF16) for i in range(4)]
                    accs = [sbuf(ph, f"B_acc{i}", [65, 2048], F32) for i in range(2)]
                    ob, b_ob = sbuf(ph, "B_ob", [128, 16, 2, 64], F32)
                    KE = 3
                    rden = [sbuf(ph, f"B_rden{i}", [128, 1], F32) for i in range(KE)]
                    gsc = [sbuf(ph, f"B_gsc{i}", [128, 128], F32) for i in range(KE)]
                    ofin = [sbuf(ph, f"B_ofin{i}", [128, 128], BF16) for i in range(KE)]
                    side = []
                    cnt_e = [0]

                    def b_epi_tile(hl, acc, b_acc, t16):
                        i_ = cnt_e[0] % KE
                        cnt_e[0] += 1
                        psT, b_psT = ps_mm()
                        op("pe", lambda e: e.transpose(psT[:, 0:65], acc[0:65, t16 * 128:(t16 + 1) * 128], ident_f[0:65, 0:65]), [b_acc, b_idf], [b_psT])
                        yield
                        rd, b_rd = rden[i_]
                        op("dve", lambda e: e.reciprocal(out=rd[:], in_=psT[:, 64:65]), [b_psT], [b_rd])
                        yield
                        ts("dve", ob[:, t16, hl, :], psT[:, 0:64], rd[:, 0:1], ALU.mult, [b_psT, b_rd], [b_ob])
                        yield

                    def b_epi(hl, acc, b_acc):
                        for t16 in range(16):
                            yield from b_epi_tile(hl, acc, b_acc, t16)

                    def b_fin(u):
                        for t16 in range(16):
                            i_ = cnt_e[0] % KE
                            cnt_e[0] += 1
                            yield from gate_and_store(u * 16 + t16, ob[:, t16, :, :].rearrange("p h d -> p (h d)"), b_ob, Wg, b_Wg, 128, OCOL_B + 128 * c,
                                                      gsc[i_][0], gsc[i_][1], ofin[i_][0], ofin[i_][1])

                    def drain(cond):
                        while any(cond(tg) for tg, _ in side):
                            try:
                                next(side[0][1])
                            except StopIteration:
                                side.pop(0)

                    def b_main():
                        LA = 2
                        items = []
                        for u in range(S // 2048):
                            for hl in range(2):
                                for gi, D in enumerate((1, 4, 16)):
                                    per = 16 // D
                                    for r in range(D):
                                        for i in range(u * per, (u + 1) * per):
                                            items.append((u, hl, gi, D, r, i))
                        pend = []
                        for idx in range(len(items) + LA):
                            if idx < len(items):
                                u, hl, gi, D, r, i = items[idx]
                                base = 64 * hl
                                ntl = S // D // 128
                                qc0 = r + D * 128 * i
                                qcols = slice(qc0, qc0 + D * 127 + 1, D)
                                js = [j for j in (i - 1, i, i + 1) if 0 <= j < ntl]
                                stp, b_st = ps_st()
                                for j in js:
                                    jj = j - (i - 1)
                                    kc0 = r + D * 128 * j
                                    op("pe", lambda e, stp=stp, jj=jj, kc0=kc0, D=D, base=base, qcols=qcols: e.matmul(
                                        stp[:, jj * 128:(jj + 1) * 128], lhsT=kT[base:base + 64, slice(kc0, kc0 + D * 127 + 1, D)], rhs=qT[base:base + 64, qcols],
                                        start=True, stop=True), [b_kT, b_qT], [b_st])
                                lo = (js[0] - (i - 1)) * 128
                                hi = (js[-1] - (i - 1) + 1) * 128
                                E, b_E = Et[idx % 4]
                                op("act", lambda e, E=E, stp=stp, lo=lo, hi=hi: e.activation(out=E[:, lo:hi], in_=stp[:, lo:hi], func=AF.Exp, scale=0.125), [b_st], [b_E])
                                tt("pool" if idx % 3 == 0 else "dve", E[:, lo:hi], E[:, lo:hi], maskB[:, lo:hi], ALU.mult, [b_E, b_maskB], [b_E])
                                pend.append((u, hl, gi, D, r, i, js, E, b_E, qc0, idx))
                            if idx >= LA:
                                u, hl, gi, D, r, i, js, E, b_E, qc0, idx0 = pend.pop(0)
                                ntl = S // D // 128
                                slot = (u * 2 + hl) % 2
                                acc, b_acc = accs[slot]
                                first_item = (gi == 0 and r == 0 and i == u * 16)
                                if first_item:
                                    drain(lambda tg: tg == slot)
                                ot, b_ot = ps_ot()
                                for j in js:
                                    jj = j - (i - 1)
                                    op("pe", lambda e, ot=ot, gi=gi, tile=r * ntl + j, hl=hl, E=E, jj=jj, first=(j == js[0]), last=(j == js[-1]): e.matmul(
                                        ot[0:65, 0:128], lhsT=vB[:, gi, tile, hl, 0:65], rhs=E[:, jj * 128:(jj + 1) * 128], start=first, stop=last),
                                       [b_vB, b_E], [b_ot])
                                a0 = qc0 - 2048 * u
                                acc_ap = acc[0:65, a0:a0 + D * 127 + 1:D]
                                if gi == 0:
                                    cp("dve", acc_ap, ot[0:65, 0:128], [b_ot], [b_acc])
                                else:
                                    tt("dve", acc_ap, ot[0:65, 0:128], acc_ap, ALU.add, [b_ot, b_acc], [b_acc])
                                last_item = (gi == 2 and r == 15 and i == u)
                                if last_item:
                                    side.append((slot, b_epi(hl, acc, b_acc)))
                                    if hl == 1:
                                        side.append((None, b_fin(u)))
                            yield

                    for _ in b_main():
                        if side:
                            try:
                                next(side[0][1])
                            except StopIteration:
                                side.pop(0)
                    drain(lambda tg: True)
                    P.barrier()

            for c in range(2 if "C" in stages else 0):
                with ExitStack() as ph:
                    Wqk, b_Wqk = sbuf(ph, "C_Wqk", [128, 8, 192], BF16)
                    Wv, b_Wv = sbuf(ph, "C_Wv", [128, 8, 192], BF16)
                    Wg, b_Wg = sbuf(ph, "C_Wg", [128, 8, 192], BF16)
                    wl = w_in[l]
                    load_w(Wqk, b_Wqk, wl[:, OFF_QC + 96 * c:OFF_QC + 96 * c + 96], 96, 0)
                    load_w(Wqk, b_Wqk, wl[:, OFF_KC + 96 * c:OFF_KC + 96 * c + 96], 96, 96)
                    load_w(Wv, b_Wv, wl[:, OFF_VC + 192 * c:OFF_VC + 192 * c + 192], 192)
                    load_w(Wg, b_Wg, wl[:, OFF_GC + 192 * c:OFF_GC + 192 * c + 192], 192)
                    qkT, b_qkT = sbuf(ph, "C_qkT", [64, 4, S], BF16)
                    vC, b_vC = sbuf(ph, "C_vC", [128, NT, 2, 96], BF16)
                    Rst, b_Rst = sbuf(ph, "C_Rst", [64, NT, 2, 2, 96], BF16)
                    ph_mid = ExitStack()
                    kfb, b_kfb = sbuf(ph_mid, "C_kfb", [128, NT, 2, 2, 48], BF16)
                    Rs = [sbuf(ph_mid, f"C_R{i}", [64, 2, 96], F32) for i in range(2)]
                    cdt = [sbuf(ph_mid, f"C_cd{i}", [64, 2, 96], F32) for i in range(2)]
                    ph2 = ExitStack()
                    KC = 3
                    set_pools(mm=[0, 1, 2, 3, 4], tb=[5, 6, 7])
                    scr = []
                    for i_ in range(KC):
                        d_ = {"sq": (None, None), "ss": (None, None)}
                        d_["xn"] = sbuf(ph2, f"Cxn{i_}", [128, 192], F32)
                        for j_ in range(4):
                            d_["t%d" % j_] = sbuf(ph2, f"Ct{j_}_{i_}", [128, 96], F32)
                        scr.append(d_)
                    qkr = [sbuf(ph2, f"C_qkr{i}", [128, 4, 48], F32) for i in range(KC)]
                    qkp = [sbuf(ph2, f"C_qkp{i}", [128, 4, 64], BF16) for i in range(KC)]
                    for i in range(KC):
                        op("pool", lambda e, i=i: e.memset(qkp[i][0][:].rearrange("p g d -> p (g d)"), 0.0), [], [qkp[i][1]])

                    def c_proj_body(t):
                        ps, b_ps = ps_mm()
                        proj(tcols(t), Wqk, b_Wqk, 192, ps, b_ps)
                        yield
                        qr, b_qr = qkr[t % KC]
                        qp, b_qp = qkp[t % KC]
                        yield from norm_rope(ps[:, 0:192].rearrange("p (g d) -> p g d", d=48), b_ps, 4, 48, None, None, 24,
                                             rope[:, t, 24:48], rope[:, t, 48:72], qr[:], b_qr, scr[t % KC])
                        cp("act", qp[:, :, 0:48], qr[:], [b_qr], [b_qp])
                        tt("dve", kfb[:, t, 0, :, :], qr[:, 2:4, :], decs[:, 8 + 2 * c:10 + 2 * c].unsqueeze(2).to_broadcast([128, 2, 48]), ALU.mult, [b_qr, b_decs], [b_kfb])
                        tt("pool", kfb[:, t, 1, :, :], qr[:, 2:4, :], decs[:, 12 + 2 * c:14 + 2 * c].unsqueeze(2).to_broadcast([128, 2, 48]), ALU.mult, [b_qr, b_decs], [b_kfb])
                        yield
                        pb, b_pb = ps_tb()
                        for s_ in range(4):
                            op("pe", lambda e, s_=s_: e.transpose(pb[0:64, s_ * 128:(s_ + 1) * 128], qp[:, s_, :], ident_b[:]), [b_qp, b_idb], [b_pb])
                        yield
                        cp("dve", qkT[:, :, tcols(t)], pb[0:64, 0:512].rearrange("p (g n) -> p g n", n=128), [b_pb], [b_qkT])
                        yield
                        psv, b_psv = ps_mm()
                        proj(tcols(t), Wv, b_Wv, 192, psv, b_psv)
                        yield
                        cp("act", vC[:, t, :, :], psv[:, 0:192].rearrange("p (h d) -> p h d", d=96), [b_psv], [b_vC])
                        yield

                    interleave((c_proj_body(t) for t in range(NT)), KC)
                    P.barrier()
                    ph2.close()
                    set_pools(mm=[0, 1, 2, 3], st=[1, 2, 3], ot=[4, 5, 6, 7])

                    def c_scan(d_):
                        R_, b_R = Rs[d_]
                        cd_, b_cd = cdt[d_]
                        op("pool", lambda e: e.memset(R_[:].rearrange("p h d -> p (h d)"), 0.0), [], [b_R])
                        cp("dve", cd_[0:48, :, :], decs[0:48, 24 + 4 * d_ + 2 * c:26 + 4 * d_ + 2 * c].unsqueeze(2).to_broadcast([48, 2, 96]), [b_decs], [b_cd])
                        yield
                        order = range(NT) if d_ == 0 else range(NT - 1, -1, -1)
                        for n in order:
                            psU, b_psU = ps_mm()
                            for hl in range(2):
                                op("pe", lambda e, psU=psU, n=n, hl=hl: e.matmul(psU[0:48, hl * 96:(hl + 1) * 96], lhsT=kfb[:, n, d_, hl, :], rhs=vC[:, n, hl, :], start=True, stop=True),
                                   [b_kfb, b_vC], [b_psU])
                            cp("dve", Rst[0:48, n, d_, :, :], R_[0:48, :, :], [b_R], [b_Rst])
                            yield
                            tt("dve", R_[0:48, :, :], R_[0:48, :, :], cd_[0:48, :, :], ALU.mult, [b_R, b_cd], [b_R])
                            yield
                            tt("dve", R_[0:48, :, :], psU[0:48, 0:192].rearrange("p (h d) -> p h d", d=96), R_[0:48, :, :], ALU.add, [b_psU, b_R], [b_R])
                            yield

                    interleave((c_scan(d_) for d_ in range(2)), 2)
                    P.barrier()
                    ph_mid.close()
                    set_pools(ot=[0, 1, 2, 3, 4, 5, 6, 7])
                    KO = 3
                    att = [sbuf(ph, f"C_att{i}", [128, 128], BF16) for i in range(2 * KO)]
                    oc = [sbuf(ph, f"C_oc{i}", [128, 2, 96], F32) for i in range(KO)]
                    osq = [sbuf(ph, f"C_osq{i}", [128, 2, 96], F32) for i in range(KO)]
                    oss = [sbuf(ph, f"C_oss{i}", [128, 2], F32) for i in range(KO)]
                    gsc = [sbuf(ph, f"C_gsc{i}", [128, 192], F32) for i in range(KO)]
                    ofin = [sbuf(ph, f"C_ofin{i}", [128, 192], BF16) for i in range(KO)]

                    def c_out_head(n, hl, oc_, b_oc, gs_, b_gs):
                        head = 2 * c + hl
                        bk, b_bk = ps_ot()
                        gsl = gs_[:, hl * 96:(hl + 1) * 96]
                        for k in range(8):
                            op("pe", lambda e, k=k: e.matmul(bk[:, 416:512], lhsT=hT[:, k, tcols(n)], rhs=Wg[:, k, hl * 96:(hl + 1) * 96], start=(k == 0), stop=(k == 7)),
                               [b_hT, b_Wg], [b_bk])
                        yield
                        op("act", lambda e: e.activation(out=gsl, in_=bk[:, 416:512], func=AF.Exp, scale=-1.0), [b_bk], [b_gs])
                        yield
                        op("act", lambda e: e.activation(out=gsl, in_=gsl, func=AF.Ln, bias=one_t[:, 0:1]), [b_gs, b_one], [b_gs])
                        yield
                        op("act", lambda e: e.activation(out=gsl, in_=gsl, func=AF.Exp, scale=-1.0), [b_gs], [b_gs])
                        yield
                        tt("dve", gsl, bk[:, 416:512], gsl, ALU.mult, [b_bk, b_gs], [b_gs])
                        yield
                        op("pe", lambda e: e.matmul(bk[:, 288:416], lhsT=qkT[0:48, 2 + hl, tcols(n)], rhs=qkT[0:48, hl, tcols(n)], start=True, stop=True),
                           [b_qkT], [b_bk])
                        yield
                        at_, b_at = att[(n % KO) * 2 + hl]
                        tt("dve", at_[:], bk[:, 288:416], DT[:, head, :], ALU.mult, [b_bk, b_DT], [b_at])
                        yield
                        op("pe", lambda e: e.matmul(bk[:, 0:96], lhsT=at_[:], rhs=vC[:, n, hl, :], start=True, stop=True), [b_at, b_vC], [b_bk])
                        for d_ in range(2):
                            op("pe", lambda e, d_=d_: e.matmul(bk[:, 96 * (d_ + 1):96 * (d_ + 2)], lhsT=qkT[0:48, hl, tcols(n)], rhs=Rst[0:48, n, d_, hl, :],
                                                              start=True, stop=True), [b_qkT, b_Rst], [b_bk])
                        yield
                        cp("act", oc_[:, hl, :], bk[:, 0:96], [b_bk], [b_oc])
                        yield
                        for d_ in range(2):
                            op("dve", lambda e, d_=d_: e.scalar_tensor_tensor(
                                out=oc_[:, hl, :], in0=bk[:, 96 * (d_ + 1):96 * (d_ + 2)], scalar=decs[:, 16 + 4 * d_ + head:17 + 4 * d_ + head], in1=oc_[:, hl, :],
                                op0=ALU.mult, op1=ALU.add), [b_bk, b_decs, b_oc], [b_oc])
                            yield

                    def c_out_body(n):
                        oc_, b_oc = oc[n % KO]
                        gs_, b_gs = gsc[n % KO]
                        h0 = c_out_head(n, 0, oc_, b_oc, gs_, b_gs)
                        h1 = c_out_head(n, 1, oc_, b_oc, gs_, b_gs)
                        live = [h0, h1]
                        while live:
                            for g_ in list(live):
                                try:
                                    next(g_)
                                except StopIteration:
                                    live.remove(g_)
                            yield
                        sq_, b_sq = osq[n % KO]
                        ss_, b_ss = oss[n % KO]
                        of_, b_of = ofin[n % KO]
                        tt("pool", sq_[:], oc_[:], oc_[:], ALU.mult, [b_oc], [b_sq])
                        yield
                        op("dve", lambda e: e.reduce_sum(out=ss_[:], in_=sq_[:], axis=AX.X), [b_sq], [b_ss])
                        yield
                        rstd_inplace(ss_[:], b_ss, 96.0)
                        yield
                        tt("dve", sq_[:], oc_[:], ss_[:].unsqueeze(2).to_broadcast([128, 2, 96]), ALU.mult, [b_oc, b_ss], [b_sq])
                        yield
                        tt("pool", sq_[:], sq_[:], spt[:, SP_GNC:SP_GNC + 96].unsqueeze(1).to_broadcast([128, 2, 96]), ALU.mult, [b_sq, b_spt], [b_sq])
                        yield
                        tt("dve", of_[:], sq_[:].rearrange("p h d -> p (h d)"), gs_[:], ALU.mult, [b_sq, b_gs], [b_of])
                        yield
                        P.dma("ofin_" + b_of.name, o_scr[n * 128:(n + 1) * 128, OCOL_C + 192 * c:OCOL_C + 192 * c + 192], of_[:], reads=[b_of], writes=[obufs[n]])
                        yield

                    interleave((c_out_body(n) for n in range(NT)), KO)
                    P.barrier()

            if dbg == "o":
                with ExitStack() as ph:
                    ob16, b_ob16 = sbuf(ph, "dbg_ob", [128, 1024], BF16)
                    o32, b_o32 = sbuf(ph, "dbg_o32", [128, 1024], F32)
                    for t in range(NT):
                        P.dma("dbg_l", ob16[:], o_scr[t * 128:(t + 1) * 128, :], reads=[obufs[t]], writes=[b_ob16])
                        cp("dve", o32[:], ob16[:], [b_ob16], [b_o32])
                        P.dma("dbg_s", dbg_o[t * 128:(t + 1) * 128, :], o32[:], reads=[b_o32], writes=[xbufs[t]])
                break

            if "O" in stages:
                with ExitStack() as ph:
                    Wout, b_Wout = sbuf(ph, "Wout", [128, 8, 1024], BF16)
                    load_w(Wout, b_Wout, w_out[l], 1024)
                    KQ = 3
                    set_pools(mm=[0, 1, 2, 3, 4], tb=[5, 6, 7])
                    ott = [sbuf(ph, f"O_ot{i}", [128, 1024], BF16) for i in range(KQ)]
                    oTt = [sbuf(ph, f"O_oT{i}", [128, 8, 128], BF16) for i in range(KQ)]
                    xt = [sbuf(ph, f"O_xt{i}", [128, 1024], F32) for i in range(KQ)]
                    tm = [sbuf(ph, f"O_tm{i}", [128, 1024], F32) for i in range(KQ)]

                    def o_half(oT_t, b_oT, tm_t, b_tm, nh):
                        ps, b_ps = ps_mm()
                        for k in range(8):
                            op("pe", lambda e, k=k: e.matmul(ps[:, :], lhsT=oT_t[:, k, :], rhs=Wout[:, k, nh * 512:(nh + 1) * 512], start=(k == 0), stop=(k == 7)),
                               [b_oT, b_Wout], [b_ps])
                        yield
                        sl = slice(nh * 512, (nh + 1) * 512)
                        tt("dve", tm_t[:, sl], ps[:, :], gate_bc[:, sl], ALU.mult, [b_ps, b_gate], [b_tm])
                        yield

                    def o_body(t):
                        o_t, b_o = ott[t % KQ]
                        oT_t, b_oT = oTt[t % KQ]
                        x_t, b_x = xt[t % KQ]
                        tm_t, b_tm = tm[t % KQ]
                        P.dma("O_ot%d" % (t % KQ), o_t[:], o_scr[t * 128:(t + 1) * 128, :], reads=[obufs[t]], writes=[b_o])
                        P.dma("O_xt%d" % (t % KQ), x_t[:], xsrc[t * 128:(t + 1) * 128, :], reads=[xbufs[t]], writes=[b_x])
                        yield
                        pb, b_pb = ps_tb()
                        for k in range(8):
                            op("pe", lambda e, k=k: e.transpose(pb[:, k * 128:(k + 1) * 128], o_t[:, k * 128:(k + 1) * 128], ident_b[:]), [b_o, b_idb], [b_pb])
                        yield
                        cp("act", oT_t[:], pb[:, :].rearrange("p (k n) -> p k n", n=128), [b_pb], [b_oT])
                        yield
                        for nh in range(2):
                            yield from o_half(oT_t, b_oT, tm_t, b_tm, nh)
                        tt("pool" if t % 2 else "dve", tm_t[:], tm_t[:], x_t[:], ALU.add, [b_tm, b_x], [b_tm])
                        yield
                        P.dma("O_st%d" % (t % KQ), y_out[t * 128:(t + 1) * 128, :], tm_t[:], reads=[b_tm], writes=[xbufs[t]])
                        yield

                    interleave((o_body(t) for t in range(NT)), KQ)
                    P.barrier()

        P.barrier()
        P.emit()
    return nc, P


def host_constants(S):
    NT = S // 128
    pos = np.arange(S, dtype=np.float32)

    def tab(theta, rot):
        inv = (np.float32(theta) ** (-np.arange(0, rot, 2, dtype=np.float32) / np.float32(rot))).astype(np.float32)
        ang = (pos[:, None] * inv[None, :]).astype(np.float32)
        return np.cos(ang).astype(np.float32), np.sin(ang).astype(np.float32)

    cA, sA = tab(ROT_THETA, 8)
    cB, sB = tab(ROT_THETA, 16)
    cC, sC = tab(RET_THETA, 48)
    rope = np.concatenate([cA, sA, cB, sB, cC, sC], axis=1)
    rope = np.ascontiguousarray(rope.reshape(NT, 128, 72).transpose(1, 0, 2)).astype(np.float32)
    a = np.arange(128)[:, None]
    b = np.arange(128)[None, :]
    maskB = np.concatenate([(a - b >= 64), (np.abs(a - b) <= 64), (b - a >= 64)], axis=1).astype(np.float32).astype(ml_dtypes.bfloat16)
    j = a
    i = b
    retc = np.zeros((128, 516), np.float32)
    retc[:, 0:128] = np.maximum(i - j, 0)
    retc[:, 128:256] = (i >= j)
    retc[:, 256:384] = np.maximum(j - i, 0)
    retc[:, 384:512] = (j > i)
    p = np.arange(128, dtype=np.float32)
    retc[:, 512] = p + 1
    retc[:, 513] = 127 - p
    retc[:, 514] = 128 - p
    retc[:, 515] = p
    kmask = np.zeros((128, 4), np.float32)
    for j_ in range(4):
        kmask[32 * j_:32 * j_ + 32, j_] = 1.0
    return {
        "rope": rope, "maskB": maskB, "retc": retc, "kmask": kmask,
        "ident_b": np.eye(128, dtype=np.float32).astype(ml_dtypes.bfloat16),
        "ident_f": np.eye(128, dtype=np.float32),
    }


def pack_small(inp, L):
    parts = [inp["qn_a"], inp["kn_a"], inp["lambda_q1"], inp["lambda_k1"], inp["lambda_q2"], inp["lambda_k2"],
             inp["subln_a"], inp["qn_b"], inp["kn_b"], np.asarray(inp["ret_decay"]).reshape(L, 8), inp["gn_c"]]
    return np.ascontiguousarray(np.concatenate([np.asarray(p_, np.float32).reshape(L, -1) for p_ in parts], axis=1))


def make_in_maps(inp, S, L, B):
    consts = host_constants(S)
    f = lambda k: np.ascontiguousarray(np.asarray(inp[k], np.float32))
    shared = {
        "w_ada": f("w_ada")[:L], "b_ada": f("b_ada")[:L], "norm_g": f("norm_g")[:L], "w_in": f("w_in")[:L], "w_out": f("w_out")[:L],
        "smallp": pack_small({k: np.asarray(v)[:L] for k, v in inp.items() if k not in ("x", "c")}, L),
    }
    shared.update(consts)
    maps = []
    x = f("x")
    c = f("c")
    for b in range(B):
        m = dict(shared)
        m["x"] = np.ascontiguousarray(x[b])
        m["cT"] = np.ascontiguousarray(c[b].reshape(8, 128).T)
        maps.append(m)
    return maps


_CACHE = {}


def kernel(**inputs):
    x = np.asarray(inputs["x"])
    B, S, _ = x.shape
    L = np.asarray(inputs["w_in"]).shape[0]
    key = (S, L)
    if key not in _CACHE:
        _CACHE[key] = build_program(S, L)[0]
    nc = _CACHE[key]
    maps = make_in_maps(inputs, S, L, B)
    in_maps = [maps[i % B] for i in range(8)]
    res = run_bass_kernel_spmd(nc, in_maps, core_ids=list(range(8)))
    out = np.stack([np.asarray(res.results[b]["y"], np.float32) for b in range(B)], axis=0)
    return out
```
